# Optimizing a Trainium2 kernel written in Bass

```python
import math
import jax, jax.numpy as jnp
from jax import lax
import numpy as np

D_MODEL = 1024
BATCH = 16
SEQ = 4096
DEPTH = 4
DEC_BATCH = 8
DEC_SEQ = 2048
PAST_LEN = 128

GRID_W = 64
CHUNK = 128
SSD_HEADS = 8
SSD_HEADDIM = 64
SSD_WIDTH = SSD_HEADS * SSD_HEADDIM
SSD_GROUPS = 2
SSD_HPG = SSD_HEADS // SSD_GROUPS
SSD_STATE = 128
SSD_BC = SSD_GROUPS * SSD_STATE
SSD_XBC = SSD_WIDTH + 2 * SSD_BC
SSD_CONV = 5
RET_HEADS = 4
RET_QK = 64
RET_V = 128
RET_WIDTH = RET_HEADS * RET_V
ROPE_BASE = 10000.0
NA_HEADS = 8
NA_HEADDIM = 64
NA_WIDTH = NA_HEADS * NA_HEADDIM
NA_ROWS = 8
NA_COLS = 16
N_BRANCH = 3
BRANCH_WIDTH = 512
D_FF = 2816
FFN_CONV = 3
EPS = 1e-6
PROJ_SIZES = (SSD_WIDTH,
              SSD_XBC,
              2 * SSD_HEADS,
              RET_HEADS * RET_QK,
              RET_HEADS * RET_QK,
              RET_WIDTH,
              RET_WIDTH,
              NA_WIDTH,
              NA_WIDTH,
              NA_WIDTH,
              N_BRANCH * D_MODEL)
PROJ_TOTAL = sum(PROJ_SIZES)

kernel_name = "hybrid_ssd_retention_natten_encoder"


def rmsnorm(x, g=None):
    xf = x.astype(jnp.float32)
    y = xf * lax.rsqrt(jnp.mean(xf * xf, axis=-1, keepdims=True) + EPS)
    if g is not None:
        y = y * g.astype(jnp.float32)
    return y.astype(x.dtype)


def dwconv(x, w, bias):
    k = w.shape[0]
    out = lax.conv_general_dilated(
        x, w[:, None, :].astype(x.dtype), window_strides=(1,),
        padding=[(k // 2, k // 2)], dimension_numbers=('NWC', 'WIO', 'NWC'),
        feature_group_count=x.shape[-1])
    return out + bias.astype(x.dtype)


def rotary(x, pos):
    half = x.shape[-1] // 2
    inv = 1.0 / (ROPE_BASE ** (jnp.arange(half, dtype=jnp.float32) / half))
    ang = pos[:, None] * inv[None, :]
    cos = jnp.cos(ang)[None, :, None, :]
    sin = jnp.sin(ang)[None, :, None, :]
    xf = x.astype(jnp.float32)
    x1, x2 = xf[..., :half], xf[..., half:]
    return jnp.concatenate([x1 * cos - x2 * sin, x2 * cos + x1 * sin], axis=-1).astype(x.dtype)


def decay_scan(q, k, v, log_a, strict):
    b, l, g, n = q.shape
    r, p = v.shape[3], v.shape[4]
    c = l // CHUNK
    qc = q.reshape(b, c, CHUNK, g, n)
    kc = k.reshape(b, c, CHUNK, g, n)
    vc = v.reshape(b, c, CHUNK, g, r, p)
    acum = jnp.cumsum(log_a.astype(jnp.float32).reshape(b, c, CHUNK, g, r), axis=2)
    at = jnp.moveaxis(acum, 2, -1)
    seg = at[..., :, None] - at[..., None, :]
    mask = jnp.tril(jnp.ones((CHUNK, CHUNK), dtype=bool), -1 if strict else 0)
    decay = jnp.exp(jnp.where(mask, seg, -jnp.inf))
    scores = jnp.einsum('bcign,bcjgn->bcgij', qc, kc).astype(jnp.float32)
    y_intra = jnp.einsum('bcgrij,bcjgrp->bcigrp', scores[:, :, :, None] * decay, vc)
    last = at[..., -1:]
    states = jnp.einsum('bcjgn,bcgrj,bcjgrp->bcgrnp', kc, jnp.exp(last - at), vc)
    chunk_decay = jnp.exp(last[..., 0])

    def step(s, inp):
        st, dec = inp
        return dec[..., None, None] * s + st, s

    s0 = jnp.zeros((b, g, r, n, p), states.dtype)
    _, prev = lax.scan(step, s0, (jnp.moveaxis(states, 1, 0), jnp.moveaxis(chunk_decay, 1, 0)))
    prev = jnp.moveaxis(prev, 0, 1)
    y_inter = jnp.einsum('bcign,bcgrnp,bcgri->bcigrp', qc, prev, jnp.exp(at))
    return (y_intra + y_inter).reshape(b, l, g, r, p).astype(v.dtype)


def bidir_scan(q, k, v_f, la_f, v_b, la_b):
    flip = lambda t: jnp.flip(t, axis=1)
    y_f = decay_scan(q, k, v_f, la_f, strict=False)
    y_b = flip(decay_scan(flip(q), flip(k), flip(v_b), flip(la_b), strict=True))
    return y_f + y_b


def neighborhood_attention(q, k, v, rpb):
    b, l, h, d = q.shape
    rows = l // GRID_W
    kh = min(NA_ROWS, rows)
    qg = q.reshape(b, rows, GRID_W, h, d)
    kg = k.reshape(b, rows, GRID_W, h, d)
    vg = v.reshape(b, rows, GRID_W, h, d)
    cols = jnp.arange(GRID_W)
    col_start = jnp.clip(cols - NA_COLS // 2, 0, GRID_W - NA_COLS)
    col_idx = col_start[:, None] + jnp.arange(NA_COLS)[None, :]
    col_off = col_idx - cols[:, None]
    bias_col = rpb.astype(jnp.float32)[:, :, col_off + NA_COLS - 1]
    row_start = jnp.clip(jnp.arange(rows) - kh // 2, 0, rows - kh)
    scale = d ** -0.5

    def one_row(r):
        rs = row_start[r]
        kb = lax.dynamic_slice_in_dim(kg, rs, kh, axis=1)[:, :, col_idx]
        vb = lax.dynamic_slice_in_dim(vg, rs, kh, axis=1)[:, :, col_idx]
        qr = lax.dynamic_index_in_dim(qg, r, axis=1, keepdims=False)
        s = jnp.einsum('bqhd,bxqyhd->bhqxy', qr, kb).astype(jnp.float32) * scale
        row_off = rs + jnp.arange(kh) - r
        bias = jnp.take(bias_col, row_off + NA_ROWS - 1, axis=1)
        s = s + jnp.transpose(bias, (0, 2, 1, 3))[None]
        pr = jax.nn.softmax(s.reshape(b, h, GRID_W, kh * NA_COLS), axis=-1)
        pr = pr.reshape(b, h, GRID_W, kh, NA_COLS).astype(v.dtype)
        return jnp.einsum('bhqxy,bxqyhd->bqhd', pr, vb)

    out = lax.map(one_row, jnp.arange(rows))
    return jnp.moveaxis(out, 0, 1).reshape(b, l, h, d)


def encoder_layer(x, norm_mix, w_in, gate_bias, ssd_conv_w, ssd_conv_b, ssd_dt_bias,
                  ssd_a_log, ssd_d, ssd_norm, ret_theta, na_rpb, w_branch, w_out,
                  norm_ffn, ffn_w_up, ffn_conv_w, ffn_conv_b, ffn_w_down):
    b, l, _ = x.shape
    h = rmsnorm(x, norm_mix)
    proj = h @ w_in
    splits, acc = [], 0
    for s in PROJ_SIZES[:-1]:
        acc += s
        splits.append(acc)
    z, xbc, dt_raw, rq, rk, rv, rg, nq, nk, nv, gates = jnp.split(proj, splits, axis=-1)

    xbc = jax.nn.silu(dwconv(xbc, ssd_conv_w, ssd_conv_b))
    xs, bm, cm = jnp.split(xbc, [SSD_WIDTH, SSD_WIDTH + SSD_BC], axis=-1)
    xh = xs.reshape(b, l, SSD_GROUPS, SSD_HPG, SSD_HEADDIM)
    bm = bm.reshape(b, l, SSD_GROUPS, SSD_STATE)
    cm = cm.reshape(b, l, SSD_GROUPS, SSD_STATE)
    dt = jax.nn.softplus(dt_raw.astype(jnp.float32).reshape(b, l, 2, SSD_GROUPS, SSD_HPG)
                         + ssd_dt_bias.astype(jnp.float32).reshape(2, SSD_GROUPS, SSD_HPG))
    a = -jnp.exp(ssd_a_log.astype(jnp.float32)).reshape(2, SSD_GROUPS, SSD_HPG)
    dt_f, dt_b = dt[:, :, 0], dt[:, :, 1]
    y = bidir_scan(cm, bm, xh * dt_f[..., None], dt_f * a[0], xh * dt_b[..., None], dt_b * a[1])
    y = y + ssd_d.reshape(SSD_GROUPS, SSD_HPG)[..., None] * xh
    o_ssd = rmsnorm(y.astype(x.dtype).reshape(b, l, SSD_WIDTH) * jax.nn.silu(z), ssd_norm)

    pos = jnp.arange(l, dtype=jnp.float32)
    q = rotary(rq.reshape(b, l, RET_HEADS, RET_QK), pos)
    k = rotary(rk.reshape(b, l, RET_HEADS, RET_QK), pos) * (RET_QK ** -0.5)
    v = rv.reshape(b, l, RET_HEADS, 1, RET_V)
    log_gamma = -jnp.exp(ret_theta.astype(jnp.float32))
    la_f = jnp.broadcast_to(log_gamma[0][:, None], (b, l, RET_HEADS, 1))
    la_b = jnp.broadcast_to(log_gamma[1][:, None], (b, l, RET_HEADS, 1))
    yr = bidir_scan(q, k, v, la_f, v, la_b).reshape(b, l, RET_HEADS, RET_V)
    o_ret = rmsnorm(yr).reshape(b, l, RET_WIDTH).astype(x.dtype) * jax.nn.silu(rg)

    o_na = neighborhood_attention(nq.reshape(b, l, NA_HEADS, NA_HEADDIM),
                                  nk.reshape(b, l, NA_HEADS, NA_HEADDIM),
                                  nv.reshape(b, l, NA_HEADS, NA_HEADDIM),
                                  na_rpb).reshape(b, l, NA_WIDTH)

    gates = jax.nn.sigmoid((gates + gate_bias).astype(jnp.float32)).astype(x.dtype)
    gates = gates.reshape(b, l, N_BRANCH, D_MODEL)
    branches = (o_ssd, o_ret, o_na)
    merged = gates[:, :, 0] * (branches[0] @ w_branch[0])
    for i in range(1, N_BRANCH):
        merged = merged + gates[:, :, i] * (branches[i] @ w_branch[i])
    x = x + merged @ w_out

    h2 = rmsnorm(x, norm_ffn)
    u = dwconv(h2 @ ffn_w_up, ffn_conv_w, ffn_conv_b)
    val, gt = jnp.split(u, 2, axis=-1)
    x = x + (jax.nn.silu(gt) * val) @ ffn_w_down
    return x


def trunk(x, norm_mix, w_in, gate_bias, ssd_conv_w, ssd_conv_b, ssd_dt_bias, ssd_a_log,
          ssd_d, ssd_norm, ret_theta, na_rpb, w_branch, w_out, norm_ffn, ffn_w_up,
          ffn_conv_w, ffn_conv_b, ffn_w_down, norm_final):
    for i in range(DEPTH):
        x = encoder_layer(x, norm_mix[i], w_in[i], gate_bias[i], ssd_conv_w[i], ssd_conv_b[i],
                          ssd_dt_bias[i], ssd_a_log[i], ssd_d[i], ssd_norm[i], ret_theta[i],
                          na_rpb[i], w_branch[i], w_out[i], norm_ffn[i], ffn_w_up[i],
                          ffn_conv_w[i], ffn_conv_b[i], ffn_w_down[i])
    return rmsnorm(x, norm_final)


def setup_inputs(seed: int = 0) -> dict:
    key = jax.random.key(seed)
    ks = jax.random.split(key, 24)
    f32 = jnp.float32
    nrm = lambda k, shape, s: jax.random.normal(k, shape, f32) * s
    dt0 = jnp.exp(jax.random.uniform(ks[8], (DEPTH, 2, SSD_HEADS), f32,
                                     math.log(1e-3), math.log(1e-1)))
    ret_base = jnp.asarray(np.log(-np.log(1.0 - 2.0 ** (-5.0 - np.arange(RET_HEADS)))), dtype=f32)
    return {
        "x_prompt": jax.random.normal(ks[0], (BATCH, SEQ, D_MODEL), f32),
        "x_sample": jax.random.normal(ks[1], (DEC_BATCH, DEC_SEQ, D_MODEL), f32),
        "norm_mix": 1.0 + nrm(ks[2], (DEPTH, D_MODEL), 0.02),
        "w_in": nrm(ks[3], (DEPTH, D_MODEL, PROJ_TOTAL), D_MODEL ** -0.5),
        "gate_bias": nrm(ks[4], (DEPTH, N_BRANCH * D_MODEL), 0.01),
        "ssd_conv_w": nrm(ks[5], (DEPTH, SSD_CONV, SSD_XBC), SSD_CONV ** -0.5),
        "ssd_conv_b": nrm(ks[6], (DEPTH, SSD_XBC), 0.01),
        "ssd_dt_bias": dt0 + jnp.log(-jnp.expm1(-dt0)),
        "ssd_a_log": jnp.log(jax.random.uniform(ks[9], (DEPTH, 2, SSD_HEADS), f32, 1.0, 16.0)),
        "ssd_d": 1.0 + nrm(ks[10], (DEPTH, SSD_HEADS), 0.1),
        "ssd_norm": 1.0 + nrm(ks[11], (DEPTH, SSD_WIDTH), 0.02),
        "ret_theta": ret_base[None, None, :] + nrm(ks[12], (DEPTH, 2, RET_HEADS), 0.05),
        "na_rpb": nrm(ks[13], (DEPTH, NA_HEADS, 2 * NA_ROWS - 1, 2 * NA_COLS - 1), 0.02),
        "w_branch": nrm(ks[14], (DEPTH, N_BRANCH, BRANCH_WIDTH, D_MODEL), BRANCH_WIDTH ** -0.5),
        "w_out": nrm(ks[15], (DEPTH, D_MODEL, D_MODEL), D_MODEL ** -0.5),
        "norm_ffn": 1.0 + nrm(ks[16], (DEPTH, D_MODEL), 0.02),
        "ffn_w_up": nrm(ks[17], (DEPTH, D_MODEL, 2 * D_FF), D_MODEL ** -0.5),
        "ffn_conv_w": nrm(ks[18], (DEPTH, FFN_CONV, 2 * D_FF), FFN_CONV ** -0.5),
        "ffn_conv_b": nrm(ks[19], (DEPTH, 2 * D_FF), 0.01),
        "ffn_w_down": nrm(ks[20], (DEPTH, D_FF, D_MODEL), D_FF ** -0.5),
        "norm_final": 1.0 + nrm(ks[21], (D_MODEL,), 0.02),
    }


def reference(x_prompt, x_sample, norm_mix, w_in, gate_bias, ssd_conv_w, ssd_conv_b,
              ssd_dt_bias, ssd_a_log, ssd_d, ssd_norm, ret_theta, na_rpb, w_branch, w_out,
              norm_ffn, ffn_w_up, ffn_conv_w, ffn_conv_b, ffn_w_down, norm_final):
    y_prompt = trunk(x_prompt, norm_mix, w_in, gate_bias, ssd_conv_w, ssd_conv_b, ssd_dt_bias,
                     ssd_a_log, ssd_d, ssd_norm, ret_theta, na_rpb, w_branch, w_out, norm_ffn,
                     ffn_w_up, ffn_conv_w, ffn_conv_b, ffn_w_down, norm_final)
    y_sample = trunk(x_sample, norm_mix, w_in, gate_bias, ssd_conv_w, ssd_conv_b, ssd_dt_bias,
                     ssd_a_log, ssd_d, ssd_norm, ret_theta, na_rpb, w_branch, w_out, norm_ffn,
                     ffn_w_up, ffn_conv_w, ffn_conv_b, ffn_w_down, norm_final)
    return (y_prompt, y_sample)
```

```python
import math
from contextlib import ExitStack
import numpy as np
import concourse.bass as bass
import concourse.mybir as mybir
from concourse.bass_utils import run_bass_kernel_spmd

F32 = mybir.dt.float32
BF16 = mybir.dt.bfloat16
U8 = mybir.dt.uint8
AF = mybir.ActivationFunctionType
ALU = mybir.AluOpType
AX = mybir.AxisListType

DM = 1024
KC = 8
PROJ = 7696
DFF = 2816
EPS = 1e-6
ENG = ["pe", "act", "dve", "pool", "sp"]
NDS = 72
ARENA = 200 * 1024

C_Z, C_XBC, C_DT, C_RQ, C_RK, C_RV, C_RG, C_NQ, C_NK, C_NV, C_GATE = 0, 512, 1536, 1552, 1808, 2064, 2576, 3088, 3600, 4112, 4624

S_GMIX, S_GFFN, S_GB, S_CW, S_CB, S_FW, S_FB, S_DSK, S_NW, S_TH, S_DTB, S_ALOG = 0, 8, 16, 40, 80, 88, 220, 264, 272, 784, 792, 793
NSM = 800
K_ID = 0
K_MNF = 128
K_MNB = 640
K_RM = 1152
K_SEL = 1664
K_MF = 3712
K_PROT = 3716
K_EJ = 3844
K_EI = 3848
K_M1 = 4104
K_M2 = 4232
K_J = 4360
K_CV = 4424
NCON = 4488


class Op:
    __slots__ = ("eng", "fn", "deps", "needed", "value", "is_dma", "sem", "dval")


class Buf:
    def __init__(self, name, ap):
        self.name = name
        self.ap = ap
        self.w = {}
        self.r = {}
        self.dsem = None
        self.const = False

    def __getitem__(self, k):
        return self.ap[k]


class Prog:
    def __init__(self, nc, es):
        self.nc = nc
        self.ops = {e: [] for e in ENG}
        self.arena = es.enter_context(nc.sbuf_tensor("arena", [128, ARENA], U8))
        self.off = 0
        self.esem = {e: es.enter_context(nc.semaphore("s_" + e)) for e in ENG}
        self.dsems = [es.enter_context(nc.semaphore("d%d" % i)) for i in range(NDS)]
        self.dcount = [0] * NDS
        self.dnext = 0
        self.dlast = {}
        self.last_real = {}
        self.ps = []
        for i in range(8):
            t = es.enter_context(nc.psum_tensor("ps%d" % i, [128, 512], F32))
            self.ps.append(Buf("ps%d" % i, t[:]))
        self.psi = 0
        self.psk = {}
        self.evi = 0
        self.dummy = Buf("dummy", None)
        self.nops = 0

    def alloc(self, name, shape, dt):
        esz = 4 if dt == F32 else 2
        n = 1
        for s in shape:
            n *= s
        off = (self.off + 63) // 64 * 64
        assert off + n * esz <= ARENA, ("SBUF arena overflow", name, off, n * esz)
        ap = self.arena[:, off:off + n * esz].bitcast(dt)
        if len(shape) == 2:
            ap = ap.rearrange("p (a b) -> p a b", a=shape[0])
        elif len(shape) == 3:
            ap = ap.rearrange("p (a b c) -> p a b c", a=shape[0], b=shape[1])
        self.off = off + n * esz
        return Buf(name, ap)

    def psum(self, lo=0, hi=8):
        n = hi - lo
        k = self.psk.get((lo, hi), 0)
        self.psk[(lo, hi)] = k + 1
        return self.ps[lo + k % n]

    def op(self, eng, fn, r=(), w=(), join=False, is_dma=False, sem=None):
        o = Op()
        o.eng, o.fn, o.needed, o.value, o.is_dma, o.sem, o.dval = eng, fn, False, 0, is_dma, sem, 0
        deps = {}

        def add(d, raw):
            if d.is_dma or d.eng != eng or (raw and eng in ("act", "dve", "pool")):
                deps[id(d)] = d

        for b in r:
            for d in b.w.values():
                add(d, True)
        for b in w:
            if eng == "pe" and not join and b.w and not b.r and not is_dma:
                assert all(d.eng != "pe" for d in b.w.values()), ("PSUM bank overwritten before being read", b.name)
            for d in b.r.values():
                add(d, False)
            if not join or b.r:
                for d in b.w.values():
                    add(d, False)
        o.deps = list(deps.values())
        for d in o.deps:
            d.needed = True
        key = ("d", sem) if is_dma else eng
        for b in r:
            if not b.const:
                b.r[key] = o
        for b in w:
            if join and not b.r:
                b.w[key] = o
            else:
                b.w = {key: o}
            b.r = {}
        self.ops[eng].append(o)
        if not is_dma:
            self.last_real[eng] = o
        self.nops += 1
        return o

    def pe(self, fn, r=(), w=(), join=False):
        return self.op("pe", fn, r, w, join)

    def act(self, fn, r=(), w=(), join=False):
        return self.op("act", fn, r, w, join)

    def dve(self, fn, r=(), w=(), join=False):
        return self.op("dve", fn, r, w, join)

    def pool(self, fn, r=(), w=(), join=False):
        return self.op("pool", fn, r, w, join)

    def dma(self, out, in_, r=(), w=(), join=False, q="sp"):
        bl = list(w) + list(r)
        b = bl[0] if bl else self.dummy
        if b.dsem is None:
            b.dsem = self.dnext % NDS
            self.dnext += 1
        s = b.dsem
        o = self.op(q, lambda e: e.dma_start(out=out, in_=in_), r, w, join, is_dma=True, sem=s)
        self.dcount[s] += 16
        o.dval = self.dcount[s]
        self.dlast[s] = o
        return o

    def barrier(self):
        lasts = list(self.last_real.values()) + list(self.dlast.values())
        for e in ENG:
            o = Op()
            o.eng, o.fn, o.needed, o.value, o.is_dma, o.sem, o.dval = e, (lambda h: None), False, 0, False, None, 0
            o.deps = [d for d in lasts if d.is_dma or d.eng != e]
            for d in o.deps:
                d.needed = True
            self.ops[e].append(o)
        self.dlast = {}
        self.dummy = Buf("dummy", None)

    def evac(self, out, in_, r, w, join=False):
        self.evi += 1
        if self.evi % 2:
            return self.act(lambda e: e.activation(out=out, in_=in_, func=AF.Copy), r, w, join)
        return self.dve(lambda e: e.tensor_copy(out=out, in_=in_), r, w, join)

    def emit(self, block):
        for e in ENG:
            c = 0
            for o in self.ops[e]:
                if o.needed and not o.is_dma:
                    c += 1
                    o.value = c

        def run(e, h):
            known = {}
            for o in self.ops[e]:
                need = {}
                for d in o.deps:
                    if d.is_dma:
                        key, sem, val = ("d", d.sem), self.dsems[d.sem], d.dval
                    else:
                        key, sem, val = d.eng, self.esem[d.eng], d.value
                    if need.get(key, (None, 0))[1] < val:
                        need[key] = (sem, val)
                for key, (sem, val) in need.items():
                    if known.get(key, 0) < val:
                        h.wait_ge(sem, val)
                        known[key] = val
                inst = o.fn(h)
                if inst is None:
                    continue
                if o.is_dma:
                    inst.then_inc(self.dsems[o.sem], 16)
                elif o.needed:
                    inst.then_inc(self.esem[e], 1)

        @block.tensor
        def _(h):
            run("pe", h)

        @block.scalar
        def _(h):
            run("act", h)

        @block.vector
        def _(h):
            run("dve", h)

        @block.gpsimd
        def _(h):
            run("pool", h)

        @block.sync
        def _(h):
            run("sp", h)


def bc(ap, shape):
    return ap.broadcast_to(shape)


class Builder:
    def __init__(self, seq_lens, depth, debug=()):
        self.seq_lens = list(seq_lens)
        self.depth = depth
        self.debug = set(debug)
        self.run_layers = depth
        self._dumps = {}
        self.only = None
        self.Lmax = max(seq_lens)
        self.nc = bass.Bass("TRN2", target_bir_lowering=False)

    def dram(self, name, shape, dt, kind="Internal"):
        if name in self.debug:
            kind = "ExternalOutput"
        return self.nc.dram_tensor(name, shape, dt, kind=kind).ap()

    def build(self, phases=None):
        nc = self.nc
        D = self.depth
        Lm = self.Lmax
        I = "ExternalInput"
        self.x_in = [self.dram("x%d" % i, [L, DM], F32, I) for i, L in enumerate(self.seq_lens)]
        self.y_out = [self.dram("y%d" % i, [L, DM], F32, "ExternalOutput") for i, L in enumerate(self.seq_lens)]
        self.w_in = self.dram("w_in", [D, DM, PROJ], F32, I)
        self.w_br = self.dram("w_branch", [D, 3, 512, DM], F32, I)
        self.w_out = self.dram("w_out", [D, DM, DM], F32, I)
        self.w_up = self.dram("ffn_w_up", [D, DM, 2 * DFF], F32, I)
        self.w_dn = self.dram("ffn_w_down", [D, DFF, DM], F32, I)
        self.small = self.dram("small", [D + 1, 128, NSM], F32, I)
        self.consts = self.dram("consts", [128, NCON], F32, I)
        self.rot = self.dram("rot", [2, 128, Lm], F32, I)
        self.rpbp = self.dram("rpbp", [D, 8, 15, 127], F32, I)
        self.rvc = {L: self.dram("rvc%d" % L, [128, (L // 512) * 64], F32, I) for L in sorted(set(self.seq_lens))}
        self.wb_in = self.dram("wb_in", [D, DM, PROJ], BF16)
        self.wb_br = self.dram("wb_br", [D, 3, 512, DM], BF16)
        self.wb_out = self.dram("wb_out", [D, DM, DM], BF16)
        self.wb_up = self.dram("wb_up", [D, DM, 2 * DFF], BF16)
        self.wb_dn = self.dram("wb_dn", [D, DFF, DM], BF16)
        self.X = self.dram("X", [DM, Lm], F32)
        self.XM = self.dram("XM", [DM, Lm], F32)
        self.XBC = self.dram("XBC", [1024, Lm], BF16)
        self.DT = self.dram("DT", [16, Lm], F32)
        self.RQ = self.dram("RQ", [256, Lm], BF16)
        self.RK = self.dram("RK", [256, Lm], BF16)
        self.NQ = self.dram("NQ", [512, Lm], BF16)
        self.NK = self.dram("NK", [512, Lm], BF16)
        self.GT = self.dram("GT", [3072, Lm], BF16)
        self.ZT = self.dram("ZT", [Lm, 512], BF16)
        self.RVT = self.dram("RVT", [Lm, 512], BF16)
        self.RGT = self.dram("RGT", [Lm, 512], BF16)
        self.NVT = self.dram("NVT", [Lm, 512], BF16)
        self.BCF = self.dram("BCF", [512, Lm], BF16)
        self.XTOK = self.dram("XTOK", [Lm, 512], BF16)
        self.FT = self.dram("FT", [Lm, 128], F32)
        self.UF = self.dram("UF", [16, Lm], F32)
        self.RKR = self.dram("RKR", [256, Lm], BF16)
        self.OS = self.dram("OS", [512, Lm], BF16)
        self.OR = self.dram("OR", [512, Lm], BF16)
        self.ON = self.dram("ON", [512, Lm], BF16)

        with ExitStack() as es:
            P = Prog(nc, es)
            self.P = P
            self.setup_consts()
            self.convert_weights()
            P.barrier()
            self.base_off = P.off
            for s, L in enumerate(self.seq_lens):
                self.s, self.L = s, L
                self.NT = L // 512
                self.phase(self.ph_in)
                for l in range(self.run_layers):
                    self.l = l
                    for nm, f in (("a", self.ph_a), ("scan", self.ph_scan), ("na", self.ph_na), ("merge", self.ph_merge), ("ffn", self.ph_ffn)):
                        if self.only is None or nm in self.only:
                            self.phase(f)
                self.phase(self.ph_out)
            blk = es.enter_context(nc.Block())
            P.emit(blk)
        return nc

    def dump(self, name, buf, ap, shape, dt):
        if name not in self.debug:
            return
        if name not in self._dumps:
            self._dumps[name] = self.nc.dram_tensor(name, [128] + list(shape), dt, kind="ExternalOutput").ap()
        self.P.dma(self._dumps[name], ap, r=[buf])

    def phase(self, fn):
        self.P.off = self.base_off
        self.P.dnext = 1
        fn()
        assert self.P.dnext <= NDS, self.P.dnext
        self.P.barrier()

    def setup_consts(self):
        P = self.P
        self.cf = P.alloc("cf", [NCON], F32)
        P.dma(self.cf[:], self.consts[:, :], w=[self.cf])
        cf = self.cf
        self.identb = P.alloc("identb", [128], BF16)
        P.dve(lambda e: e.tensor_copy(out=self.identb[:], in_=cf[:, K_ID:K_ID + 128]), r=[cf], w=[self.identb])
        self.protb = P.alloc("protb", [128], BF16)
        P.dve(lambda e: e.tensor_copy(out=self.protb[:], in_=cf[:, K_PROT:K_PROT + 128]), r=[cf], w=[self.protb])
        self.jb = P.alloc("jb", [64], BF16)
        P.dve(lambda e: e.tensor_copy(out=self.jb[:], in_=cf[:, K_J:K_J + 64]), r=[cf], w=[self.jb])
        self.onesm = P.alloc("onesm", [128], BF16)
        P.dve(lambda e: e.memset(self.onesm[:], 1.0 / 1024.0), w=[self.onesm])
        self.onesf = P.alloc("onesf", [128], F32)
        P.dve(lambda e: e.memset(self.onesf[:], 1.0), w=[self.onesf])
        self.epsb = P.alloc("epsb", [1], F32)
        P.dve(lambda e: e.memset(self.epsb[:], EPS), w=[self.epsb])
        for b in (cf, self.identb, self.protb, self.jb, self.onesm, self.onesf, self.epsb):
            b.const = True

    def convert_weights(self):
        P = self.P
        D = self.depth
        for l in range(D):
            for (src, dst, rows) in ((self.w_in[l], self.wb_in[l], DM), (self.w_out[l], self.wb_out[l], DM),
                                     (self.w_up[l], self.wb_up[l], DM), (self.w_dn[l], self.wb_dn[l], DFF)):
                for r0 in range(0, rows, 128):
                    P.dma(dst[r0:r0 + 128, :], src[r0:r0 + 128, :], q="pool")
            for i in range(3):
                for r0 in range(0, 512, 128):
                    P.dma(self.wb_br[l, i, r0:r0 + 128, :], self.w_br[l, i, r0:r0 + 128, :], q="pool")

    def load_small(self, l):
        P = self.P
        sm = P.alloc("sm", [NSM], F32)
        P.dma(sm[:], self.small[l], w=[sm])
        return sm

    def norm(self, xt, n, g, h, sq, rs, out_f32=False):
        P = self.P
        P.act(lambda e: e.activation(out=sq[:, :, 0:n], in_=xt[:, :, 0:n], func=AF.Square), r=[xt], w=[sq])
        ps = P.psum()
        for c in range(KC):
            P.pe(lambda e, c=c: e.matmul(ps[:, 0:n], lhsT=self.onesm[:], rhs=sq[:, c, 0:n], start=(c == 0), stop=(c == KC - 1)),
                 r=[sq], w=[ps], join=(c > 0))
        P.act(lambda e: e.activation(out=rs[:, 0:n], in_=ps[:, 0:n], func=AF.Sqrt, bias=self.epsb[:, 0:1]), r=[ps], w=[rs])
        P.dve(lambda e: e.reciprocal(out=rs[:, 0:n], in_=rs[:, 0:n]), r=[rs], w=[rs])
        for c in range(KC):
            P.dve(lambda e, c=c: e.scalar_tensor_tensor(out=h[:, c, 0:n], in0=xt[:, c, 0:n], scalar=g[:, c:c + 1], in1=rs[:, 0:n],
                                                        op0=ALU.mult, op1=ALU.mult), r=[xt, rs], w=[h], join=(c > 0))

    def ph_in(self):
        P = self.P
        x = self.x_in[self.s]
        cf = self.cf
        xin = [P.alloc("xin%d" % i, [4, DM], F32) for i in range(2)]
        xt = [P.alloc("xt%d" % i, [KC, 512], F32) for i in range(2)]
        for t in range(self.NT):
            t0 = t * 512
            a, b = xin[t % 2], xt[t % 2]
            P.dma(a[:], x[t0:t0 + 512, :].rearrange("(tc p) f -> p tc f", p=128), w=[a])
            for fc in range(KC):
                ps = P.psum()
                for tc in range(4):
                    P.pe(lambda e, tc=tc, fc=fc, ps=ps, a=a: e.transpose(ps[:, tc * 128:(tc + 1) * 128], a[:, tc, fc * 128:(fc + 1) * 128],
                                                                         cf[:, K_ID:K_ID + 128]), r=[a], w=[ps], join=(tc > 0))
                P.evac(b[:, fc, :], ps[:], r=[ps], w=[b], join=(fc > 0))
            P.dma(self.X[:, t0:t0 + 512].rearrange("(c p) t -> p c t", p=128), b[:], r=[b])

    def ph_out(self):
        P = self.P
        y = self.y_out[self.s]
        cf = self.cf
        sm = self.load_small(self.depth)
        xt = [P.alloc("xt%d" % i, [KC, 512], F32) for i in range(2)]
        sq = P.alloc("sq", [KC, 512], BF16)
        rs = P.alloc("rs", [512], F32)
        yn = P.alloc("yn", [KC, 512], F32)
        yt = [P.alloc("yt%d" % i, [4, DM], F32) for i in range(2)]
        for t in range(self.NT):
            t0 = t * 512
            a, o = xt[t % 2], yt[t % 2]
            P.dma(a[:], self.X[:, t0:t0 + 512].rearrange("(c p) t -> p c t", p=128), w=[a])
            self.norm(a, 512, sm[:, S_GMIX:S_GMIX + 8], yn, sq, rs)
            for tc in range(4):
                for hf in range(2):
                    ps = P.psum()
                    for k in range(4):
                        P.pe(lambda e, k=k, hf=hf, tc=tc, ps=ps: e.transpose(ps[:, k * 128:(k + 1) * 128], yn[:, hf * 4 + k, tc * 128:(tc + 1) * 128],
                                                                            cf[:, K_ID:K_ID + 128]), r=[yn], w=[ps], join=(k > 0))
                    P.evac(o[:, tc, hf * 512:(hf + 1) * 512], ps[:], r=[ps], w=[o], join=(tc > 0 or hf > 0))
            P.dma(y[t0:t0 + 512, :].rearrange("(tc p) f -> p tc f", p=128), o[:], r=[o])

    def ph_a(self):
        P = self.P
        l = self.l
        sm = self.load_small(l)
        W = self.wb_in[l]
        xt = [P.alloc("xt%d" % i, [KC, 512], F32) for i in range(2)]
        sq = P.alloc("sq", [KC, 512], BF16)
        rs = P.alloc("rs", [512], F32)
        hb = [P.alloc("h%d" % i, [KC, 512], BF16) for i in range(2)]
        wt = [P.alloc("w%d" % i, [KC, 1024], BF16) for i in range(2)]
        of = [P.alloc("of%d" % i, [4, 512], BF16) for i in range(2)]
        ot = [P.alloc("ot%d" % i, [4, 512], BF16) for i in range(2)]
        dtf = P.alloc("dtf", [512], F32)
        wi = [0]
        oi = [0]

        def loadw(c0, n):
            w = wt[wi[0] % 2]
            wi[0] += 1
            P.dma(w[:, :, 0:n], W[:, c0:c0 + n].rearrange("(c p) n -> p c n", p=128), w=[w])
            return w

        fjobs = [(self.XBC, C_XBC, 1024), (self.RQ, C_RQ, 256), (self.RK, C_RK, 256), (self.NQ, C_NQ, 512),
                 (self.NK, C_NK, 512), (self.GT, C_GATE, 1024), (self.GT, C_GATE + 1024, 1024), (self.GT, C_GATE + 2048, 1024)]
        tjobs = [(self.ZT, C_Z), (self.RVT, C_RV), (self.RGT, C_RG), (self.NVT, C_NV)]
        for t in range(self.NT):
            t0 = t * 512
            a, h = xt[t % 2], hb[t % 2]
            P.dma(a[:], self.X[:, t0:t0 + 512].rearrange("(c p) t -> p c t", p=128), w=[a])
            self.norm(a, 512, sm[:, S_GMIX:S_GMIX + 8], h, sq, rs)
            w = loadw(C_DT, 16)
            ps = P.psum()
            for c in range(KC):
                P.pe(lambda e, c=c, w=w, ps=ps, h=h: e.matmul(ps[0:16, :], lhsT=w[:, c, 0:16], rhs=h[:, c, :], start=(c == 0), stop=(c == KC - 1)),
                     r=[w, h], w=[ps], join=(c > 0))
            P.act(lambda e, ps=ps: e.activation(out=dtf[0:16, :], in_=ps[0:16, :], func=AF.Copy), r=[ps], w=[dtf])
            P.dma(self.DT[:, t0:t0 + 512], dtf[0:16, :], r=[dtf])
            for (dst, c0, n) in fjobs:
                r0 = c0 - (C_GATE if dst is self.GT else c0)
                w = loadw(c0, n)
                for m in range(n // 128):
                    if m % 4 == 0:
                        o = of[oi[0] % 2]
                        oi[0] += 1
                    ps = P.psum()
                    for c in range(KC):
                        P.pe(lambda e, c=c, m=m, w=w, ps=ps, h=h: e.matmul(ps[:], lhsT=w[:, c, m * 128:(m + 1) * 128], rhs=h[:, c, :],
                                                                          start=(c == 0), stop=(c == KC - 1)), r=[w, h], w=[ps], join=(c > 0))
                    P.evac(o[:, m % 4, :], ps[:], r=[ps], w=[o], join=(m % 4 > 0))
                    if m % 4 == 3 or m == n // 128 - 1:
                        k = m % 4 + 1
                        rr = r0 + (m - (k - 1)) * 128
                        P.dma(dst[rr:rr + k * 128, t0:t0 + 512].rearrange("(c p) t -> p c t", p=128), o[:, 0:k, :], r=[o])
            for (dst, c0) in tjobs:
                w = loadw(c0, 512)
                o = ot[oi[0] % 2]
                oi[0] += 1
                for tc in range(4):
                    ps = P.psum()
                    for c in range(KC):
                        P.pe(lambda e, c=c, tc=tc, w=w, ps=ps, h=h: e.matmul(ps[:], lhsT=h[:, c, tc * 128:(tc + 1) * 128], rhs=w[:, c, 0:512],
                                                                            start=(c == 0), stop=(c == KC - 1)), r=[w, h], w=[ps], join=(c > 0))
                    P.evac(o[:, tc, :], ps[:], r=[ps], w=[o], join=(tc > 0))
                P.dma(dst[t0:t0 + 512, :].rearrange("(tc p) c -> p tc c", p=128), o[:], r=[o])

    def rotary(self, src, dst, cs, sn, scale, tmp1, tmp2, split=False):
        P = self.P
        for hp in range(2):
            ps = P.psum()
            P.pe(lambda e, hp=hp, ps=ps: e.matmul(ps[:], lhsT=self.protb[:], rhs=src[:, hp, :], start=True, stop=True), r=[src], w=[ps])
            P.dve(lambda e, hp=hp: e.scalar_tensor_tensor(out=tmp1[:], in0=src[:, hp, :], scalar=scale, in1=cs[:], op0=ALU.mult, op1=ALU.mult),
                  r=[src, cs], w=[tmp1])
            P.dve(lambda e, ps=ps: e.scalar_tensor_tensor(out=tmp2[:], in0=ps[:], scalar=scale, in1=sn[:], op0=ALU.mult, op1=ALU.mult),
                  r=[ps, sn], w=[tmp2])
            if split:
                for hf in range(2):
                    o = hf * 64
                    P.pool(lambda e, hp=hp, hf=hf, o=o: e.tensor_tensor(out=dst[o:o + 64, 2 * hp + hf, :], in0=tmp1[o:o + 64, :], in1=tmp2[o:o + 64, :], op=ALU.add),
                           r=[tmp1, tmp2], w=[dst], join=True)
            else:
                P.pool(lambda e, hp=hp: e.tensor_tensor(out=dst[:, hp, :], in0=tmp1[:], in1=tmp2[:], op=ALU.add), r=[tmp1, tmp2], w=[dst], join=(hp > 0))

    def transpose_to(self, src_aps, dst_ap, rbufs, wbuf, join):
        P = self.P
        ps = P.psum()
        psb = ps.ap.bitcast(BF16)
        for k, s in enumerate(src_aps):
            P.pe(lambda e, k=k, s=s, psb=psb: e.transpose(psb[:, k * 128:(k + 1) * 128], s, self.identb[:]), r=rbufs, w=[ps], join=(k > 0))
        n = len(src_aps)
        src = psb[:, 0:n * 128]
        if len(dst_ap.shape) == 3:
            src = src.rearrange("p (a b) -> p a b", a=n)
        P.evac(dst_ap, src, r=[ps], w=[wbuf], join=join)

    def ph_scan(self):
        P = self.P
        l, L, NT = self.l, self.L, self.NT
        C = L // 128
        cf = self.cf
        sm = self.load_small(l)
        RSm = [P.alloc("RS%d" % c, [512], BF16) for c in range(C)]
        RSf = [Buf("RSf%d" % c, b.ap[0:64]) for c, b in enumerate(RSm)]
        RSb = [Buf("RSb%d" % c, b.ap[64:128]) for c, b in enumerate(RSm)]
        lgv = P.alloc("lgv", [8], F32)
        P.act(lambda e: e.activation(out=lgv[:], in_=sm[:, S_TH:S_TH + 8], func=AF.Exp), r=[sm], w=[lgv])
        P.dve(lambda e: e.tensor_scalar(out=lgv[:], in0=lgv[:], scalar1=-1.0, scalar2=None, op0=ALU.mult), r=[lgv], w=[lgv])
        lg_hd = lgv[:].rearrange("p (d h) -> p h d", d=2)
        A16 = P.alloc("A16", [1], F32)
        P.act(lambda e: e.activation(out=A16[0:16, :], in_=sm[0:16, S_ALOG:S_ALOG + 1], func=AF.Exp), r=[sm], w=[A16])
        P.dve(lambda e: e.tensor_scalar(out=A16[0:16, :], in0=A16[0:16, :], scalar1=-1.0, scalar2=None, op0=ALU.mult), r=[A16], w=[A16])
        WFB = P.alloc("WFB", [4, 2], F32)
        P.dve(lambda e: e.tensor_tensor(out=WFB[:], in0=lg_hd, in1=bc(cf[:, K_EJ:K_EJ + 2].unsqueeze(1), [128, 4, 2]), op=ALU.mult), r=[lgv], w=[WFB])
        P.act(lambda e: e.activation(out=WFB[:], in_=WFB[:], func=AF.Exp), r=[WFB], w=[WFB])
        DECR = P.alloc("DECR", [4], F32)
        P.act(lambda e: e.activation(out=DECR[0:64, :], in_=lgv[0:64, 0:4], func=AF.Exp, scale=128.0), r=[lgv], w=[DECR])
        P.act(lambda e: e.activation(out=DECR[64:128, :], in_=lgv[64:128, 4:8], func=AF.Exp, scale=128.0), r=[lgv], w=[DECR], join=True)
        mark_rs = P.off
        SF = [P.alloc("SF%d" % c, [512], BF16) for c in range(C)]
        SB = [P.alloc("SB%d" % c, [512], BF16) for c in range(C)]
        DEC = P.alloc("DEC", [C, 16], F32)
        mark = P.off

        xbch = [P.alloc("xbch%d" % i, [8, 516], BF16) for i in range(2)]
        acc = [P.alloc("acc%d" % i, [512], F32) for i in range(3)]
        xc = P.alloc("xc", [8, 512], BF16)
        xtok = [P.alloc("xtok%d" % i, [4, 512], BF16) for i in range(2)]
        btok = P.alloc("btok", [4, 256], BF16)
        Fb = [P.alloc("F%d" % i, [512], F32) for i in range(2)]
        ftb = [P.alloc("ft%d" % i, [4, 128], F32) for i in range(2)]
        dtr = P.alloc("dtr", [512], F32)
        s16 = [P.alloc("s16_%d" % i, [512], F32) for i in range(6)]
        dt16 = P.alloc("dt16", [512], F32)
        et = P.alloc("et", [4], F32)
        DD = P.alloc("DD", [4, 16], F32)
        xw = [P.alloc("xw%d" % i, [512], BF16) for i in range(2)]
        for b in Fb:
            P.pool(lambda e, b=b: e.memset(b[:], 0.0), w=[b])
        cw = sm[:, S_CW:S_CW + 40].rearrange("p (c j) -> p c j", j=5)
        cb = sm[:, S_CB:S_CB + 8]
        mf, mb, nmb = cf[0:16, K_MF:K_MF + 1], cf[0:16, K_MF + 1:K_MF + 2], cf[0:16, K_MF + 2:K_MF + 3]
        for t in range(NT):
            t0 = t * 512
            xb = xbch[t % 2]
            lo, hi = max(t0 - 2, 0), min(t0 + 514, L)
            if t == 0:
                P.pool(lambda e, xb=xb: e.memset(xb[:, :, 0:2], 0.0), w=[xb])
            if t == NT - 1:
                P.pool(lambda e, xb=xb: e.memset(xb[:, :, 514:516], 0.0), w=[xb])
            P.dma(xb[:, :, lo - (t0 - 2):hi - (t0 - 2)], self.XBC[:, lo:hi].rearrange("(c p) t -> p c t", p=128), w=[xb],
                  join=(t == 0 or t == NT - 1))
            P.dma(dtr[0:16, :], self.DT[:, t0:t0 + 512], w=[dtr])
            for c in range(8):
                a = acc[c % 3]
                P.pool(lambda e, c=c, a=a, xb=xb: e.tensor_scalar(out=a[:], in0=xb[:, c, 0:512], scalar1=cw[:, c, 0:1], scalar2=cb[:, c:c + 1],
                                                                  op0=ALU.mult, op1=ALU.add), r=[xb, sm], w=[a])
                for j in range(1, 5):
                    P.dve(lambda e, c=c, j=j, a=a, xb=xb: e.scalar_tensor_tensor(out=a[:], in0=xb[:, c, j:j + 512], scalar=cw[:, c, j:j + 1], in1=a[:],
                                                                                 op0=ALU.mult, op1=ALU.add), r=[xb, a, sm], w=[a])
                P.act(lambda e, c=c, a=a: e.activation(out=xc[:, c, :], in_=a[:], func=AF.Silu), r=[a], w=[xc], join=(c > 0))
            P.dma(self.BCF[:, t0:t0 + 512].rearrange("(c p) t -> p c t", p=128), xc[:, 4:8, :], r=[xc])
            F = Fb[t % 2]
            ft = ftb[t % 2]
            e1, la, cs, tme, tq, ew = s16
            P.act(lambda e: e.activation(out=e1[0:16, :], in_=dtr[0:16, :], func=AF.Exp, bias=sm[0:16, S_DTB:S_DTB + 1]), r=[dtr, sm], w=[e1])
            P.act(lambda e: e.activation(out=dt16[0:16, :], in_=e1[0:16, :], func=AF.Ln, bias=1.0), r=[e1], w=[dt16])
            P.act(lambda e, F=F: e.activation(out=F[32:48, :], in_=dt16[0:16, :], func=AF.Copy), r=[dt16], w=[F])
            P.dve(lambda e: e.tensor_scalar(out=la[0:16, :], in0=dt16[0:16, :], scalar1=A16[0:16, 0:1], scalar2=None, op0=ALU.mult), r=[dt16, A16], w=[la])
            P.dve(lambda e: e.tensor_tensor_scan(out=cs[0:16, :], data0=cf[0:16, K_RM:K_RM + 512], data1=la[0:16, :], initial=0.0,
                                                 op0=ALU.mult, op1=ALU.add), r=[la], w=[cs])
            P.dve(lambda e, F=F: e.scalar_tensor_tensor(out=F[0:16, :], in0=la[0:16, :], scalar=nmb, in1=cs[0:16, :], op0=ALU.mult, op1=ALU.add),
                  r=[la, cs], w=[F], join=True)
            cs3 = cs[0:16, :].rearrange("p (c j) -> p c j", j=128)
            P.dve(lambda e, F=F: e.tensor_tensor(out=tme[0:16, :].rearrange("p (c j) -> p c j", j=128), in0=bc(cs3[:, :, 127:128], [16, 4, 128]),
                                                 in1=F[0:16, :].rearrange("p (c j) -> p c j", j=128), op=ALU.subtract), r=[cs, F], w=[tme])
            P.dve(lambda e, F=F: e.tensor_scalar(out=tq[0:16, :], in0=F[0:16, :], scalar1=mb, scalar2=None, op0=ALU.mult), r=[F], w=[tq])
            P.dve(lambda e: e.scalar_tensor_tensor(out=tq[0:16, :], in0=tme[0:16, :], scalar=mf, in1=tq[0:16, :], op0=ALU.mult, op1=ALU.add),
                  r=[tme, tq], w=[tq])
            P.act(lambda e: e.activation(out=ew[0:16, :], in_=tq[0:16, :], func=AF.Exp), r=[tq], w=[ew])
            P.dve(lambda e, F=F: e.tensor_tensor(out=F[64:80, :], in0=ew[0:16, :], in1=dt16[0:16, :], op=ALU.mult), r=[ew, dt16], w=[F], join=True)
            P.dve(lambda e: e.tensor_scalar(out=e1[0:16, :], in0=tme[0:16, :], scalar1=mb, scalar2=None, op0=ALU.mult), r=[tme], w=[e1])
            P.dve(lambda e, F=F: e.scalar_tensor_tensor(out=e1[0:16, :], in0=F[0:16, :], scalar=mf, in1=e1[0:16, :], op0=ALU.mult, op1=ALU.add),
                  r=[F, e1], w=[e1])
            P.act(lambda e, F=F: e.activation(out=F[96:112, :], in_=e1[0:16, :], func=AF.Exp), r=[e1], w=[F], join=True)
            P.act(lambda e: e.activation(out=et[0:16, :], in_=cs3[:, :, 127], func=AF.Exp), r=[cs], w=[et])
            P.dve(lambda e: e.tensor_tensor(out=DD[0:16, :, :], in0=bc(et[0:16, :].unsqueeze(2), [16, 4, 16]),
                                            in1=bc(cf[0:16, K_ID:K_ID + 16].unsqueeze(1), [16, 4, 16]), op=ALU.mult), r=[et], w=[DD])
            ps = P.psum()
            P.pe(lambda e, ps=ps: e.matmul(ps[:, 0:64], lhsT=self.onesf[0:16, :], rhs=DD[0:16, :, :].rearrange("p a b -> p (a b)"), start=True, stop=True),
                 r=[DD], w=[ps])
            P.act(lambda e, ps=ps, t=t: e.activation(out=DEC[:, 4 * t:4 * t + 4, :].rearrange("p a b -> p (a b)"), in_=ps[:, 0:64], func=AF.Copy),
                  r=[ps], w=[DEC], join=True)
            P.dma(self.UF[:, t0:t0 + 512], F[0:16, :], r=[F])
            ps = P.psum()
            for tc in range(4):
                P.pe(lambda e, tc=tc, ps=ps, F=F: e.transpose(ps[:, tc * 128:(tc + 1) * 128], F[:, tc * 128:(tc + 1) * 128], cf[:, K_ID:K_ID + 128]),
                     r=[F], w=[ps], join=(tc > 0))
            P.evac(ft[:].rearrange("p a b -> p (a b)"), ps[:], r=[ps], w=[ft])
            P.dma(self.FT[t0:t0 + 512, :].rearrange("(tc p) c -> p tc c", p=128), ft[:], r=[ft])
            xk = xtok[t % 2]
            for tc in range(4):
                self.transpose_to([xc[:, c4, tc * 128:(tc + 1) * 128] for c4 in range(4)], xk[:, tc, :], [xc], xk, join=(tc > 0))
                self.transpose_to([xc[:, 4 + g, tc * 128:(tc + 1) * 128] for g in range(2)], btok[:, tc, :], [xc], btok, join=(tc > 0))
            P.dma(self.XTOK[t0:t0 + 512, :].rearrange("(tc p) c -> p tc c", p=128), xk[:], r=[xk])
            for tc in range(4):
                c = 4 * t + tc
                for d, S in ((0, SF), (1, SB)):
                    x_w = xw[d]
                    P.dve(lambda e, tc=tc, d=d, x_w=x_w, xk=xk, ft=ft: e.tensor_tensor(
                        out=x_w[:].rearrange("p (h q) -> p h q", q=64), in0=xk[:, tc, :].rearrange("p (h q) -> p h q", q=64),
                        in1=bc(ft[:, tc, 64 + 8 * d:72 + 8 * d].unsqueeze(2), [128, 8, 64]), op=ALU.mult), r=[xk, ft], w=[x_w])
                    ps = P.psum()
                    for g in range(2):
                        P.pe(lambda e, g=g, tc=tc, ps=ps, x_w=x_w: e.matmul(ps[:, g * 256:(g + 1) * 256], lhsT=btok[:, tc, g * 128:(g + 1) * 128],
                                                                          rhs=x_w[:, g * 256:(g + 1) * 256], start=True, stop=True),
                             r=[btok, x_w], w=[ps], join=(g > 0))
                    P.evac(S[c][:], ps[:], r=[ps], w=[S[c]])
        P.barrier()
        P.off = mark
        kt_ = [P.alloc("kt%d" % i, [2, 512], BF16) for i in range(2)]
        kr = P.alloc("kr", [2, 512], BF16)
        cs_ = [P.alloc("cs%d" % i, [512], F32) for i in range(2)]
        sn_ = [P.alloc("sn%d" % i, [512], F32) for i in range(2)]
        t1 = P.alloc("t1", [512], F32)
        t2 = P.alloc("t2", [512], F32)
        vtok = [P.alloc("vtok%d" % i, [4, 512], BF16) for i in range(2)]
        ktok = P.alloc("ktok", [4, 256], BF16)
        kw = P.alloc("kw", [4, 128], BF16)
        for t in range(NT):
            t0 = t * 512
            ktb, csb, snb, vt = kt_[t % 2], cs_[t % 2], sn_[t % 2], vtok[t % 2]
            P.dma(ktb[:], self.RK[:, t0:t0 + 512].rearrange("(c p) t -> p c t", p=128), w=[ktb])
            P.dma(csb[:], self.rot[0, :, t0:t0 + 512], w=[csb])
            P.dma(snb[:], self.rot[1, :, t0:t0 + 512], w=[snb])
            P.dma(vt[:], self.RVT[t0:t0 + 512, :].rearrange("(tc p) c -> p tc c", p=128), w=[vt])
            self.rotary(ktb, kr, csb, snb, 0.125, t1, t2)
            P.dma(self.RKR[:, t0:t0 + 512].rearrange("(c p) t -> p c t", p=128), kr[:], r=[kr])
            for tc in range(4):
                self.transpose_to([kr[:, hp, tc * 128:(tc + 1) * 128] for hp in range(2)], ktok[:, tc, :], [kr], ktok, join=(tc > 0))
            for tc in range(4):
                c = 4 * t + tc
                P.dve(lambda e, tc=tc: e.tensor_tensor(out=kw[:].rearrange("p h (d n) -> p h d n", d=2),
                                                       in0=bc(ktok[:, tc, :].rearrange("p (h n) -> p h n", n=64).unsqueeze(2), [128, 4, 2, 64]),
                                                       in1=bc(WFB[:].unsqueeze(3), [128, 4, 2, 64]), op=ALU.mult), r=[ktok, WFB], w=[kw])
                ps = P.psum()
                for h in range(4):
                    P.pe(lambda e, h=h, tc=tc, ps=ps, vt=vt: e.matmul(ps[:, h * 128:(h + 1) * 128], lhsT=kw[:, h, :], rhs=vt[:, tc, h * 128:(h + 1) * 128],
                                                                     start=True, stop=True), r=[kw, vt], w=[ps], join=(h > 0))
                P.evac(RSm[c][:], ps[:], r=[ps], w=[RSf[c], RSb[c]])

        P.barrier()
        P.off = mark
        if getattr(self, "scan_stop", 9) < 1:
            return
        Rf = [P.alloc("Rf%d" % i, [512], F32) for i in range(2)]
        Rb = [P.alloc("Rb%d" % i, [512], F32) for i in range(2)]
        RR = [P.alloc("RR%d" % i, [512], F32) for i in range(2)]
        RRf = [Buf("RRf%d" % i, b.ap[0:64]) for i, b in enumerate(RR)]
        RRb = [Buf("RRb%d" % i, b.ap[64:128]) for i, b in enumerate(RR)]
        for b in (Rf[0], Rb[0], RR[0]):
            P.pool(lambda e, b=b: e.memset(b[:], 0.0), w=[b])
        RRf[0].w = dict(RR[0].w)
        RRb[0].w = dict(RR[0].w)
        for k in range(C):
            cur, nxt = k % 2, (k + 1) % 2
            for (R, S, c, d0) in ((Rf, SF, k, 0), (Rb, SB, C - 1 - k, 8)):
                P.dve(lambda e, R=R, c=c, d0=d0, cur=cur, nxt=nxt: e.tensor_tensor(
                    out=R[nxt][:].rearrange("p (h q) -> p h q", q=64), in0=R[cur][:].rearrange("p (h q) -> p h q", q=64),
                    in1=bc(DEC[:, c, d0:d0 + 8].unsqueeze(2), [128, 8, 64]), op=ALU.mult), r=[R[cur], DEC], w=[R[nxt]])
                P.pool(lambda e, R=R, S=S, c=c, nxt=nxt: e.tensor_tensor(out=R[nxt][:], in0=R[nxt][:], in1=S[c][:], op=ALU.add),
                       r=[R[nxt], S[c]], w=[R[nxt]])
                P.act(lambda e, R=R, S=S, c=c, cur=cur: e.activation(out=S[c][:], in_=R[cur][:], func=AF.Copy), r=[R[cur]], w=[S[c]])
            for (RRh, RSh, c, p0) in ((RRf, RSf, k, 0), (RRb, RSb, C - 1 - k, 64)):
                P.dve(lambda e, c=c, p0=p0, cur=cur, nxt=nxt: e.tensor_tensor(
                    out=RR[nxt][p0:p0 + 64, :].rearrange("p (h q) -> p h q", q=128), in0=RR[cur][p0:p0 + 64, :].rearrange("p (h q) -> p h q", q=128),
                    in1=bc(DECR[p0:p0 + 64, :].unsqueeze(2), [64, 4, 128]), op=ALU.mult), r=[RRh[cur], DECR], w=[RRh[nxt]])
                P.pool(lambda e, c=c, p0=p0, nxt=nxt: e.tensor_tensor(out=RR[nxt][p0:p0 + 64, :], in0=RR[nxt][p0:p0 + 64, :], in1=RSm[c][p0:p0 + 64, :],
                                                                      op=ALU.add), r=[RRh[nxt], RSh[c]], w=[RRh[nxt]])
                P.act(lambda e, c=c, p0=p0, cur=cur: e.activation(out=RSm[c][p0:p0 + 64, :], in_=RR[cur][p0:p0 + 64, :], func=AF.Copy),
                      r=[RRh[cur]], w=[RSh[c]])
        for c in range(C):
            self.dump("D_SF%d" % c, SF[c], SF[c][:], [512], BF16)
            self.dump("D_SB%d" % c, SB[c], SB[c][:], [512], BF16)
        self.dump("D_DEC", DEC, DEC[:].rearrange("p a b -> p (a b)"), [C * 16], F32)
        P.barrier()
        P.off = mark
        if getattr(self, "scan_stop", 9) < 2:
            return
        bcf = [P.alloc("bcf%d" % i, [4, 512], BF16) for i in range(2)]
        xtok = [P.alloc("xtok%d" % i, [4, 512], BF16) for i in range(2)]
        ftb = [P.alloc("ft%d" % i, [4, 128], F32) for i in range(2)]
        ufb = [P.alloc("uf%d" % i, [512], F32) for i in range(2)]
        ztok = [P.alloc("ztok%d" % i, [4, 512], BF16) for i in range(2)]
        vfb = [P.alloc("vf%d" % i, [512], BF16) for i in range(2)]
        seg = [P.alloc("seg%d" % i, [512], F32) for i in range(2)]
        dec = [P.alloc("dec%d" % i, [512], F32) for i in range(2)]
        Gb = [P.alloc("G%d" % i, [512], BF16) for i in range(4)]
        m_ = [P.alloc("m%d" % i, [512], F32) for i in range(4)]
        szb = P.alloc("sz", [512], F32)
        yzb = P.alloc("yz", [512], F32)
        junk = P.alloc("junk", [512], BF16)
        ssb = P.alloc("ss", [8], F32)
        otk = P.alloc("otk", [512], BF16)
        oT = [P.alloc("oT%d" % i, [4, 512], BF16) for i in range(2)]
        DSK = sm[:, S_DSK:S_DSK + 8]
        NW = sm[:, S_NW:S_NW + 512]
        gi = [0]
        for t in range(NT):
            t0 = t * 512
            bf_, xk, ft, uf, zt = bcf[t % 2], xtok[t % 2], ftb[t % 2], ufb[t % 2], ztok[t % 2]
            P.dma(bf_[:], self.BCF[:, t0:t0 + 512].rearrange("(c p) t -> p c t", p=128), w=[bf_])
            P.dma(xk[:], self.XTOK[t0:t0 + 512, :].rearrange("(tc p) c -> p tc c", p=128), w=[xk])
            P.dma(ft[:], self.FT[t0:t0 + 512, :].rearrange("(tc p) c -> p tc c", p=128), w=[ft])
            P.dma(uf[0:16, :], self.UF[:, t0:t0 + 512], w=[uf])
            P.dma(zt[:], self.ZT[t0:t0 + 512, :].rearrange("(tc p) c -> p tc c", p=128), w=[zt])
            oTs = oT[t % 2]
            for tc in (range(4) if getattr(self, "s2_part", "both") in ("both", "ssd") else []):
                c = 4 * t + tc
                cols = slice(tc * 128, (tc + 1) * 128)
                for d in range(2):
                    P.pool(lambda e, d=d, tc=tc, xk=xk, ft=ft: e.tensor_tensor(
                        out=vfb[d][:].rearrange("p (h q) -> p h q", q=64), in0=xk[:, tc, :].rearrange("p (h q) -> p h q", q=64),
                        in1=bc(ft[:, tc, 32 + 8 * d:40 + 8 * d].unsqueeze(2), [128, 8, 64]), op=ALU.mult), r=[xk, ft], w=[vfb[d]])
                pss = P.psum()
                for g in range(2):
                    P.pe(lambda e, g=g, pss=pss, bf_=bf_, cols=cols: e.matmul(pss[:, g * 128:(g + 1) * 128], lhsT=bf_[:, g, cols], rhs=bf_[:, 2 + g, cols],
                                                                            start=True, stop=True), r=[bf_], w=[pss], join=(g > 0))
                psY = P.psum()
                Gs = {}
                for g in range(2):
                    for d in range(2):
                        psa = P.psum()
                        q0 = d * 8 + g * 4
                        mk = K_MNF if d == 0 else K_MNB
                        P.pe(lambda e, psa=psa, mk=mk: e.matmul(psa[:], lhsT=cf[:, K_ID:K_ID + 128], rhs=cf[:, mk:mk + 512], start=True, stop=False),
                             r=[], w=[psa])
                        for r_ in range(4):
                            P.pe(lambda e, r_=r_, q0=q0, psa=psa, uf=uf, cols=cols: e.matmul(
                                psa[:, r_ * 128:(r_ + 1) * 128], lhsT=cf[0:16, K_SEL + (q0 + r_) * 128:K_SEL + (q0 + r_ + 1) * 128], rhs=uf[0:16, cols],
                                start=False, stop=(r_ == 3)), r=[uf], w=[psa], join=True)
                        sg, dc = seg[gi[0] % 2], dec[gi[0] % 2]
                        G = Gb[gi[0] % 4]
                        gi[0] += 1
                        P.dve(lambda e, psa=psa, sg=sg, ft=ft, tc=tc, q0=q0, d=d: e.tensor_tensor(
                            out=sg[:].rearrange("p (h q) -> p h q", q=128), in0=psa[:].rearrange("p (h q) -> p h q", q=128),
                            in1=bc(ft[:, tc, q0:q0 + 4].unsqueeze(2), [128, 4, 128]), op=(ALU.subtract if d == 0 else ALU.add)), r=[psa, ft], w=[sg])
                        P.act(lambda e, sg=sg, dc=dc: e.activation(out=dc[:], in_=sg[:], func=AF.Exp), r=[sg], w=[dc])
                        P.dve(lambda e, dc=dc, G=G, pss=pss, g=g: e.tensor_tensor(
                            out=G[:].rearrange("p (h q) -> p h q", q=128), in0=dc[:].rearrange("p (h q) -> p h q", q=128),
                            in1=bc(pss[:, g * 128:(g + 1) * 128].unsqueeze(1), [128, 4, 128]), op=ALU.mult), r=[dc, pss], w=[G])
                        Gs[(g, d)] = G
                for g in range(2):
                    for r_ in range(4):
                        h = g * 4 + r_
                        for d in range(2):
                            G = Gs[(g, d)]
                            P.pe(lambda e, G=G, r_=r_, h=h, d=d, psY=psY: e.matmul(psY[:, h * 64:(h + 1) * 64], lhsT=G[:, r_ * 128:(r_ + 1) * 128],
                                                                                 rhs=vfb[d][:, h * 64:(h + 1) * 64], start=(d == 0), stop=(d == 1)),
                                 r=[G, vfb[d]], w=[psY], join=not (g == 0 and r_ == 0 and d == 0))
                psI = []
                for d, S in ((0, SF), (1, SB)):
                    pi = P.psum()
                    for g in range(2):
                        P.pe(lambda e, g=g, pi=pi, S=S, c=c, bf_=bf_, cols=cols: e.matmul(pi[:, g * 256:(g + 1) * 256], lhsT=bf_[:, 2 + g, cols],
                                                                                        rhs=S[c][:, g * 256:(g + 1) * 256], start=True, stop=True),
                             r=[bf_, S[c]], w=[pi], join=(g > 0))
                    psI.append(pi)
                m1, m2, m3, yb = m_
                for d, mm in ((0, m1), (1, m2)):
                    P.dve(lambda e, d=d, mm=mm, pi=psI[d], ft=ft, tc=tc: e.tensor_tensor(
                        out=mm[:].rearrange("p (h q) -> p h q", q=64), in0=pi[:].rearrange("p (h q) -> p h q", q=64),
                        in1=bc(ft[:, tc, 96 + 8 * d:104 + 8 * d].unsqueeze(2), [128, 8, 64]), op=ALU.mult), r=[psI[d], ft], w=[mm])
                P.pool(lambda e, xk=xk, tc=tc: e.tensor_tensor(out=m3[:].rearrange("p (h q) -> p h q", q=64), in0=xk[:, tc, :].rearrange("p (h q) -> p h q", q=64),
                                                              in1=bc(DSK.unsqueeze(2), [128, 8, 64]), op=ALU.mult), r=[xk, sm], w=[m3])
                P.pool(lambda e: e.tensor_tensor(out=m1[:], in0=m1[:], in1=m2[:], op=ALU.add), r=[m1, m2], w=[m1])
                P.pool(lambda e: e.tensor_tensor(out=m1[:], in0=m1[:], in1=m3[:], op=ALU.add), r=[m1, m3], w=[m1])
                P.dve(lambda e, psY=psY: e.tensor_tensor(out=yb[:], in0=psY[:], in1=m1[:], op=ALU.add), r=[psY, m1], w=[yb])
                if c == getattr(self, "dbgc", 0):
                    self.dump("D_SFX", SF[c], SF[c][:], [512], BF16)
                    self.dump("D_SBX", SB[c], SB[c][:], [512], BF16)
                    self.dump("D_FT", ft, ft[:, tc, :], [128], F32)
                    self.dump("D_Y", yb, yb[:], [512], F32)
                    self.dump("D_G", Gs[(0, 0)], Gs[(0, 0)][:], [512], BF16)
                    self.dump("D_M1", m1, m1[:], [512], F32)
                P.act(lambda e, zt=zt, tc=tc: e.activation(out=szb[:], in_=zt[:, tc, :], func=AF.Silu), r=[zt], w=[szb])
                P.pool(lambda e: e.tensor_tensor(out=yzb[:], in0=yb[:], in1=szb[:], op=ALU.mult), r=[yb, szb], w=[yzb])
                P.act(lambda e: e.activation(out=junk[:], in_=yzb[:], func=AF.Square, accum_out=ssb[:, 0:1]), r=[yzb], w=[junk, ssb])
                P.act(lambda e: e.activation(out=ssb[:, 1:2], in_=ssb[:, 0:1], func=AF.Sqrt, bias=self.epsb[:, 0:1], scale=1.0 / 512.0), r=[ssb], w=[ssb])
                P.dve(lambda e: e.reciprocal(out=ssb[:, 2:3], in_=ssb[:, 1:2]), r=[ssb], w=[ssb])
                P.dve(lambda e: e.scalar_tensor_tensor(out=otk[:], in0=yzb[:], scalar=ssb[:, 2:3], in1=NW, op0=ALU.mult, op1=ALU.mult),
                      r=[yzb, ssb, sm], w=[otk])
                self.transpose_to([otk[:, c4 * 128:(c4 + 1) * 128] for c4 in range(4)], oTs[:, :, cols], [otk], oTs, join=(tc > 0))
            P.dma(self.OS[:, t0:t0 + 512].rearrange("(c p) t -> p c t", p=128), oTs[:], r=[oTs])
        P.barrier()
        P.off = mark_rs
        qt_ = [P.alloc("qt%d" % i, [2, 512], BF16) for i in range(2)]
        qr = P.alloc("qrm", [4, 512], BF16)
        P.pool(lambda e: e.memset(qr[:], 0.0), w=[qr])
        krt = [P.alloc("krt%d" % i, [2, 512], BF16) for i in range(2)]
        cs_ = [P.alloc("cs%d" % i, [512], F32) for i in range(2)]
        sn_ = [P.alloc("sn%d" % i, [512], F32) for i in range(2)]
        t1 = P.alloc("t1", [512], F32)
        t2 = P.alloc("t2", [512], F32)
        vtok = [P.alloc("vtok%d" % i, [4, 512], BF16) for i in range(2)]
        rgt = [P.alloc("rgt%d" % i, [4, 512], BF16) for i in range(2)]
        qfb = P.alloc("qfb", [4, 512], BF16)
        Gr = [P.alloc("Gr%d" % i, [512], BF16) for i in range(2)]
        sqr = P.alloc("sqr", [512], F32)
        onr = P.alloc("onr", [512], F32)
        sgr = P.alloc("sgr", [512], F32)
        otr = P.alloc("otr", [512], BF16)
        oTr = [P.alloc("oTr%d" % i, [4, 512], BF16) for i in range(2)]
        ssr = P.alloc("ssr2", [8], F32)
        SCT = P.alloc("SCT", [4, 2, 128], F32)
        P.dve(lambda e: e.tensor_tensor(out=SCT[:], in0=bc(lg_hd.unsqueeze(3), [128, 4, 2, 128]),
                                        in1=bc(cf[:, K_EI:K_EI + 256].rearrange("p (d i) -> p d i", d=2).unsqueeze(1), [128, 4, 2, 128]), op=ALU.mult),
              r=[lgv], w=[SCT])
        P.act(lambda e: e.activation(out=SCT[:], in_=SCT[:], func=AF.Exp), r=[SCT], w=[SCT])
        DCOMB = P.alloc("DCOMB", [4, 128], F32)
        dtmp = P.alloc("dtmp", [4, 128], F32)
        P.dve(lambda e: e.tensor_tensor(out=DCOMB[:], in0=bc(cf[:, K_M1:K_M1 + 128].unsqueeze(1), [128, 4, 128]),
                                        in1=bc(lgv[:, 0:4].unsqueeze(2), [128, 4, 128]), op=ALU.mult), r=[lgv], w=[DCOMB])
        P.dve(lambda e: e.tensor_tensor(out=dtmp[:], in0=bc(cf[:, K_M2:K_M2 + 128].unsqueeze(1), [128, 4, 128]),
                                        in1=bc(lgv[:, 4:8].unsqueeze(2), [128, 4, 128]), op=ALU.mult), r=[lgv], w=[dtmp])
        P.pool(lambda e: e.tensor_tensor(out=DCOMB[:], in0=DCOMB[:], in1=dtmp[:], op=ALU.add), r=[DCOMB, dtmp], w=[DCOMB])
        P.act(lambda e: e.activation(out=DCOMB[:], in_=DCOMB[:], func=AF.Exp), r=[DCOMB], w=[DCOMB])
        for t in range(NT):
            t0 = t * 512
            qtb, krb, csb, snb, vt, rg = qt_[t % 2], krt[t % 2], cs_[t % 2], sn_[t % 2], vtok[t % 2], rgt[t % 2]
            P.dma(qtb[:], self.RQ[:, t0:t0 + 512].rearrange("(c p) t -> p c t", p=128), w=[qtb])
            P.dma(krb[:], self.RKR[:, t0:t0 + 512].rearrange("(c p) t -> p c t", p=128), w=[krb])
            P.dma(csb[:], self.rot[0, :, t0:t0 + 512], w=[csb])
            P.dma(snb[:], self.rot[1, :, t0:t0 + 512], w=[snb])
            P.dma(vt[:], self.RVT[t0:t0 + 512, :].rearrange("(tc p) c -> p tc c", p=128), w=[vt])
            P.dma(rg[:], self.RGT[t0:t0 + 512, :].rearrange("(tc p) c -> p tc c", p=128), w=[rg])
            oTrr = oTr[t % 2]
            self.rotary(qtb, qr, csb, snb, 1.0, t1, t2, split=True)
            for h in (range(4) if not getattr(self, "dbg_noqfb", False) else []):
                hp, o = h // 2, (h % 2) * 64
                for d in range(2):
                    P.dve(lambda e, h=h, hp=hp, o=o, d=d: e.tensor_tensor(
                        out=qfb[d * 64:(d + 1) * 64, h, :].rearrange("p (c i) -> p c i", i=128), in0=qr[o:o + 64, h, :].rearrange("p (c i) -> p c i", i=128),
                        in1=bc(SCT[o:o + 64, h, d, :].unsqueeze(1), [64, 4, 128]), op=ALU.mult), r=[qr, SCT], w=[qfb], join=not (h == 0 and d == 0))
            RST = getattr(self, "ret_stop", 9)
            for tc in (range(4) if RST >= 1 else []):
                c = 4 * t + tc
                cols = slice(tc * 128, (tc + 1) * 128)
                pss = P.psum()
                for h in range(4):
                    hp, o = h // 2, (h % 2) * 64
                    P.pe(lambda e, h=h, hp=hp, o=o, pss=pss, krb=krb, cols=cols: e.matmul(pss[:, h * 128:(h + 1) * 128], lhsT=krb[:, hp, cols],
                                                                                        rhs=qr[:, h, cols], start=True, stop=True),
                         r=[krb, qr], w=[pss], join=(h > 0))
                if RST < 2:
                    continue
                G = Gr[tc % 2]
                P.dve(lambda e, G=G, pss=pss: e.tensor_tensor(out=G[:], in0=pss[:], in1=DCOMB[:].rearrange("p a b -> p (a b)"), op=ALU.mult),
                      r=[pss, DCOMB], w=[G])
                if RST < 3:
                    continue
                psY = P.psum()
                for h in range(4):
                    P.pe(lambda e, h=h, G=G, psY=psY, vt=vt, tc=tc: e.matmul(psY[:, h * 128:(h + 1) * 128], lhsT=G[:, h * 128:(h + 1) * 128],
                                                                           rhs=vt[:, tc, h * 128:(h + 1) * 128], start=True, stop=False),
                         r=[G, vt], w=[psY], join=(h > 0))
                    P.pe(lambda e, h=h, psY=psY, c=c, cols=cols: e.matmul(psY[:, h * 128:(h + 1) * 128], lhsT=qfb[:, h, cols],
                                                                        rhs=RSm[c][:, h * 128:(h + 1) * 128], start=False, stop=True),
                         r=[qfb, RSf[c], RSb[c]], w=[psY], join=True)
                if RST < 4:
                    continue
                for h in range(4):
                    P.act(lambda e, psY=psY, h=h: e.activation(out=sqr[:, h * 128:(h + 1) * 128], in_=psY[:, h * 128:(h + 1) * 128], func=AF.Square,
                                                              accum_out=ssr[:, 4 + h:5 + h]), r=[psY], w=[sqr, ssr], join=(h > 0))
                P.act(lambda e: e.activation(out=ssr[:, 4:8], in_=ssr[:, 4:8], func=AF.Sqrt, bias=self.epsb[:, 0:1], scale=1.0 / 128.0), r=[ssr], w=[ssr])
                P.dve(lambda e: e.reciprocal(out=ssr[:, 4:8], in_=ssr[:, 4:8]), r=[ssr], w=[ssr])
                P.dve(lambda e, psY=psY: e.tensor_tensor(out=onr[:].rearrange("p (h q) -> p h q", q=128), in0=psY[:].rearrange("p (h q) -> p h q", q=128),
                                                        in1=bc(ssr[:, 4:8].unsqueeze(2), [128, 4, 128]), op=ALU.mult), r=[psY, ssr], w=[onr])
                if RST < 5:
                    continue
                P.act(lambda e, rg=rg, tc=tc: e.activation(out=sgr[:], in_=rg[:, tc, :], func=AF.Silu), r=[rg], w=[sgr])
                P.pool(lambda e: e.tensor_tensor(out=otr[:], in0=onr[:], in1=sgr[:], op=ALU.mult), r=[onr, sgr], w=[otr])
                self.transpose_to([otr[:, c4 * 128:(c4 + 1) * 128] for c4 in range(4)], oTrr[:, :, cols], [otr], oTrr, join=(tc > 0))
            if RST >= 5:
                P.dma(self.OR[:, t0:t0 + 512].rearrange("(c p) t -> p c t", p=128), oTrr[:], r=[oTrr])


    def ph_na(self):
        P = self.P
        l, L, NT = self.l, self.L, self.NT
        rows = L // 64
        cf = self.cf
        rvc = P.alloc("rvc", [NT, 8, 8], F32)
        P.dma(rvc[:].rearrange("p a b c -> p (a b c)"), self.rvc[L][:, :], w=[rvc])
        ZR = P.alloc("ZR", [8, 22, 64], BF16)
        mark = P.off
        TT = P.alloc("TT", [8, 15, 64], F32)
        E = P.alloc("E", [8, 17, 64], BF16)
        P.dma(TT[0:64, :, :, :], bass.AP(self.rpbp.tensor, self.rpbp[l].offset, [[1, 64], [15 * 127, 8], [127, 15], [1, 64]]), w=[TT])
        P.pool(lambda e: e.memset(E[0:64, :, :, :], 0.0), w=[E])
        P.pool(lambda e: e.memset(ZR[:], 0.0), w=[ZR])
        P.act(lambda e: e.activation(out=E[0:64, :, 1:16, :], in_=TT[0:64, :, :, :], func=AF.Exp), r=[TT], w=[E])
        for h in range(8):
            for ub in range(2):
                ps = P.psum()
                for k in range(8):
                    u = 3 + ub * 8 + k
                    s = 10 - u + 8
                    P.pe(lambda e, h=h, k=k, s=s, ps=ps: e.matmul(ps[:, k * 64:(k + 1) * 64], lhsT=E[0:64, h, s:s + 2, :].rearrange("p a b -> p (a b)"),
                                                                 rhs=self.jb[0:64, :], start=True, stop=True), r=[E], w=[ps], join=(k > 0))
                P.dve(lambda e, h=h, ub=ub, ps=ps: e.tensor_tensor(out=ZR[:, h, 3 + ub * 8:11 + ub * 8, :], in0=ps[:].rearrange("p (u q) -> p u q", q=64),
                                                                   in1=bc(cf[:, K_CV:K_CV + 64].unsqueeze(1), [128, 8, 64]), op=ALU.mult),
                      r=[ps], w=[ZR], join=True)
        P.barrier()
        P.off = mark
        nq = [P.alloc("nq%d" % i, [4, 512], BF16) for i in range(2)]
        nk = [P.alloc("nk%d" % i, [4, 1024], BF16) for i in range(2)]
        nv = [P.alloc("nv%d" % i, [8, 512], BF16) for i in range(2)]
        va = [P.alloc("va%d" % i, [8, 8, 128], BF16) for i in range(2)]
        eb = [P.alloc("eb%d" % i, [512], BF16) for i in range(3)]
        p1 = [P.alloc("p1_%d" % i, [512], BF16) for i in range(3)]
        p2 = [P.alloc("p2_%d" % i, [512], BF16) for i in range(3)]
        rc = [P.alloc("rc%d" % i, [512], F32) for i in range(2)]
        ona = [P.alloc("ona%d" % i, [4, 512], BF16) for i in range(2)]
        for b in va:
            P.pool(lambda e, b=b: e.memset(b[:], 1.0), w=[b])
        ei = [0]
        for t in range(NT):
            t0 = t * 512
            R0 = 8 * t
            rs0 = min(max(R0 - 4, 0), rows - 8)
            rs7 = min(max(R0 + 7 - 4, 0), rows - 8)
            kt0, kt1 = rs0 // 2, (rs7 + 7) // 2
            nkt = kt1 - kt0 + 1
            q, k_, v, vaug, on = nq[t % 2], nk[t % 2], nv[t % 2], va[t % 2], ona[t % 2]
            P.dma(q[:], self.NQ[:, t0:t0 + 512].rearrange("(c p) t -> p c t", p=128), w=[q])
            P.dma(k_[:, :, 0:nkt * 128], self.NK[:, kt0 * 128:(kt1 + 1) * 128].rearrange("(c p) t -> p c t", p=128), w=[k_])
            P.dma(v[:, 0:nkt, :], self.NVT[kt0 * 128:(kt1 + 1) * 128, :].rearrange("(j p) c -> p j c", p=128), w=[v])
            P.pool(lambda e, v=v, vaug=vaug, nkt=nkt: e.tensor_copy(out=vaug[:, 0:nkt, :, 0:64], in_=v[:, 0:nkt, :].rearrange("p j (h d) -> p j h d", d=64)),
                   r=[v], w=[vaug])
            for h in range(8):
                hp, o = h // 2, (h % 2) * 64
                psO = P.psum(0, 2)
                for j in range(nkt):
                    kt = kt0 + j
                    u0 = 10 - (2 * kt - R0)
                    pS = P.psum(2, 8)
                    P.pe(lambda e, pS=pS, k_=k_, q=q, o=o, hp=hp, j=j: e.matmul(pS[:], lhsT=k_[o:o + 64, hp, j * 128:(j + 1) * 128], rhs=q[o:o + 64, hp, :],
                                                                              start=True, stop=True), r=[k_, q], w=[pS])
                    e_, p1_, p2_ = eb[ei[0] % 3], p1[ei[0] % 3], p2[ei[0] % 3]
                    ei[0] += 1
                    P.act(lambda e, pS=pS, e_=e_: e.activation(out=e_[:], in_=pS[:], func=AF.Exp, scale=0.125), r=[pS], w=[e_])
                    P.dve(lambda e, e_=e_, p1_=p1_, h=h, u0=u0: e.tensor_tensor(out=p1_[:], in0=e_[:], in1=ZR[:, h, u0:u0 + 8, :].rearrange("p a b -> p (a b)"),
                                                                              op=ALU.mult), r=[e_, ZR], w=[p1_])
                    P.pool(lambda e, p1_=p1_, p2_=p2_, t=t, j=j: e.tensor_tensor(out=p2_[:].rearrange("p (b q) -> p b q", q=64), in0=p1_[:].rearrange("p (b q) -> p b q", q=64),
                                                                               in1=bc(rvc[:, t, j, :].unsqueeze(2), [128, 8, 64]), op=ALU.mult), r=[p1_, rvc], w=[p2_])
                    P.pe(lambda e, psO=psO, vaug=vaug, p2_=p2_, j=j, h=h, nkt=nkt: e.matmul(psO[:], lhsT=vaug[:, j, h, :], rhs=p2_[:], start=(j == 0), stop=(j == nkt - 1)),
                         r=[vaug, p2_], w=[psO], join=(j > 0))
                r_ = rc[h % 2]
                P.dve(lambda e, psO=psO, r_=r_: e.reciprocal(out=r_[0:64, :], in_=psO[64:128, :]), r=[psO], w=[r_])
                P.dve(lambda e, psO=psO, r_=r_, on=on, o=o, hp=hp: e.tensor_tensor(out=on[o:o + 64, hp, :], in0=psO[0:64, :], in1=r_[0:64, :], op=ALU.mult),
                      r=[psO, r_], w=[on], join=(h > 0))
            P.dma(self.ON[:, t0:t0 + 512].rearrange("(c p) t -> p c t", p=128), on[:], r=[on])

    def ph_merge(self):
        P = self.P
        l, NT = self.l, self.NT
        sm = self.load_small(l)
        xt = [P.alloc("xt%d" % i, [KC, 512], F32) for i in range(2)]
        ob = [[P.alloc("o%d_%d" % (i, k), [4, 512], BF16) for k in range(2)] for i in range(3)]
        gt = [P.alloc("gt%d" % i, [8, 512], BF16) for i in range(2)]
        wb = [P.alloc("wb%d" % i, [4, 1024], BF16) for i in range(2)]
        wo = [P.alloc("wo%d" % i, [KC, 512], BF16) for i in range(2)]
        gs = [P.alloc("gs%d" % i, [512], F32) for i in range(2)]
        tmp = [P.alloc("tmp%d" % i, [512], F32) for i in range(2)]
        MG = P.alloc("MG", [8, 512], F32)
        MGb = P.alloc("MGb", [8, 512], BF16)
        xo = [P.alloc("xo%d" % i, [KC, 512], F32) for i in range(2)]
        srcs = [self.OS, self.OR, self.ON]
        gi, wi = [0], [0]
        for t in range(NT):
            t0 = t * 512
            a, xn = xt[t % 2], xo[t % 2]
            P.dma(a[:], self.X[:, t0:t0 + 512].rearrange("(c p) t -> p c t", p=128), w=[a])
            for i in range(3):
                o = ob[i][t % 2]
                P.dma(o[:], srcs[i][:, t0:t0 + 512].rearrange("(c p) t -> p c t", p=128), w=[o])
            for i in range(3):
                o = ob[i][t % 2]
                g = gt[gi[0] % 2]
                w = wb[gi[0] % 2]
                gi[0] += 1
                P.dma(g[:], self.GT[i * 1024:(i + 1) * 1024, t0:t0 + 512].rearrange("(c p) t -> p c t", p=128), w=[g])
                P.dma(w[:], self.wb_br[l, i].rearrange("(c p) n -> p c n", p=128), w=[w])
                for m in range(8):
                    ps = P.psum()
                    for c in range(4):
                        P.pe(lambda e, c=c, m=m, ps=ps, w=w, o=o: e.matmul(ps[:], lhsT=w[:, c, m * 128:(m + 1) * 128], rhs=o[:, c, :], start=(c == 0), stop=(c == 3)),
                             r=[w, o], w=[ps], join=(c > 0))
                    s_ = gs[m % 2]
                    P.act(lambda e, g=g, m=m, i=i, s_=s_: e.activation(out=s_[:], in_=g[:, m, :], func=AF.Sigmoid, bias=sm[:, S_GB + i * 8 + m:S_GB + i * 8 + m + 1]),
                          r=[g, sm], w=[s_])
                    if i == 0:
                        P.dve(lambda e, ps=ps, s_=s_, m=m: e.tensor_tensor(out=MG[:, m, :], in0=ps[:], in1=s_[:], op=ALU.mult), r=[ps, s_], w=[MG], join=(m > 0))
                    else:
                        tm = tmp[m % 2]
                        P.dve(lambda e, ps=ps, s_=s_, tm=tm: e.tensor_tensor(out=tm[:], in0=ps[:], in1=s_[:], op=ALU.mult), r=[ps, s_], w=[tm])
                        if i == 1:
                            P.pool(lambda e, tm=tm, m=m: e.tensor_tensor(out=MG[:, m, :], in0=MG[:, m, :], in1=tm[:], op=ALU.add), r=[tm, MG], w=[MG], join=(m > 0))
                        else:
                            P.pool(lambda e, tm=tm, m=m: e.tensor_tensor(out=MGb[:, m, :], in0=MG[:, m, :], in1=tm[:], op=ALU.add), r=[tm, MG], w=[MGb], join=(m > 0))
            for hf in range(2):
                w = wo[wi[0] % 2]
                wi[0] += 1
                P.dma(w[:], self.wb_out[l][:, hf * 512:(hf + 1) * 512].rearrange("(c p) n -> p c n", p=128), w=[w])
                for mm in range(4):
                    m = hf * 4 + mm
                    ps = P.psum()
                    for c in range(KC):
                        P.pe(lambda e, c=c, mm=mm, ps=ps, w=w: e.matmul(ps[:], lhsT=w[:, c, mm * 128:(mm + 1) * 128], rhs=MGb[:, c, :], start=(c == 0), stop=(c == KC - 1)),
                             r=[w, MGb], w=[ps], join=(c > 0))
                    P.dve(lambda e, ps=ps, m=m, a=a, xn=xn: e.tensor_tensor(out=xn[:, m, :], in0=ps[:], in1=a[:, m, :], op=ALU.add), r=[ps, a], w=[xn], join=(m > 0))
            P.dma(self.XM[:, t0:t0 + 512].rearrange("(c p) t -> p c t", p=128), xn[:], r=[xn])

    def ph_ffn(self):
        P = self.P
        l, L = self.l, self.L
        sm = self.load_small(l)
        ntile = (L + 509) // 510
        TF = (L + ntile - 1) // ntile
        xt = [P.alloc("xt%d" % i, [KC, 512], F32) for i in range(2)]
        sq = P.alloc("sq", [KC, 512], BF16)
        rs = P.alloc("rs", [512], F32)
        hb = P.alloc("h", [KC, 512], BF16)
        wu = [P.alloc("wu%d" % i, [KC, 2, 512], BF16) for i in range(2)]
        wd = [P.alloc("wd%d" % i, [22, 256], BF16) for i in range(2)]
        av = [P.alloc("av%d" % i, [512], F32) for i in range(2)]
        ag = [P.alloc("ag%d" % i, [512], F32) for i in range(2)]
        sg = [P.alloc("sg%d" % i, [512], F32) for i in range(2)]
        actb = P.alloc("actb", [22, 512], BF16)
        xo = [P.alloc("xo%d" % i, [KC, 512], F32) for i in range(2)]
        fw = sm[:, S_FW:S_FW + 132].rearrange("p (c j) -> p c j", j=3)
        fb = sm[:, S_FB:S_FB + 44]
        wi = [0]
        for t in range(ntile):
            t0 = t * TF
            n = min(TF, L - t0)
            a, xn = xt[t % 2], xo[t % 2]
            lo, hi = max(t0 - 1, 0), min(t0 + n + 1, L)
            if lo > t0 - 1:
                P.pool(lambda e, a=a: e.memset(a[:, :, 0:1], 0.0), w=[a])
            if hi < t0 + n + 1:
                P.pool(lambda e, a=a, n=n: e.memset(a[:, :, n + 1:n + 2], 0.0), w=[a])
            P.dma(a[:, :, lo - (t0 - 1):hi - (t0 - 1)], self.XM[:, lo:hi].rearrange("(c p) t -> p c t", p=128), w=[a], join=True)
            self.norm(a, n + 2, sm[:, S_GFFN:S_GFFN + 8], hb, sq, rs)
            for mb_ in range(0, 22, 4):
                nm = min(4, 22 - mb_)
                w = wu[wi[0] % 2]
                wi[0] += 1
                for part in range(2):
                    c0 = part * DFF + mb_ * 128
                    P.dma(w[:, :, part, 0:nm * 128], self.wb_up[l][:, c0:c0 + nm * 128].rearrange("(c p) n -> p c n", p=128), w=[w], join=(part > 0))
                for mi in range(nm):
                    m = mb_ + mi
                    res = []
                    for part, accb in ((0, av), (1, ag)):
                        ch = part * 22 + m
                        ps = P.psum()
                        for c in range(KC):
                            P.pe(lambda e, c=c, part=part, mi=mi, ps=ps, w=w, n=n: e.matmul(ps[:, 0:n + 2], lhsT=w[:, c, part, mi * 128:(mi + 1) * 128], rhs=hb[:, c, 0:n + 2],
                                                                                         start=(c == 0), stop=(c == KC - 1)), r=[w, hb], w=[ps], join=(c > 0))
                        ac = accb[m % 2]
                        P.act(lambda e, ps=ps, ac=ac, ch=ch, n=n: e.activation(out=ac[:, 0:n], in_=ps[:, 0:n], func=AF.Identity, bias=fb[:, ch:ch + 1], scale=fw[:, ch, 0:1]),
                              r=[ps, sm], w=[ac])
                        for j in (1, 2):
                            P.dve(lambda e, ps=ps, ac=ac, ch=ch, j=j, n=n: e.scalar_tensor_tensor(out=ac[:, 0:n], in0=ps[:, j:j + n], scalar=fw[:, ch, j:j + 1], in1=ac[:, 0:n],
                                                                                               op0=ALU.mult, op1=ALU.add), r=[ps, ac, sm], w=[ac])
                        res.append(ac)
                    s_ = sg[m % 2]
                    P.act(lambda e, s_=s_, g_=res[1], n=n: e.activation(out=s_[:, 0:n], in_=g_[:, 0:n], func=AF.Silu), r=[res[1]], w=[s_])
                    P.pool(lambda e, s_=s_, v_=res[0], m=m, n=n: e.tensor_tensor(out=actb[:, m, 0:n], in0=s_[:, 0:n], in1=v_[:, 0:n], op=ALU.mult),
                           r=[s_, res[0]], w=[actb], join=(m > 0))
            for mp in range(4):
                w = wd[mp % 2]
                P.dma(w[:], self.wb_dn[l][:, mp * 256:(mp + 1) * 256].rearrange("(c p) n -> p c n", p=128), w=[w])
                for mi in range(2):
                    m = mp * 2 + mi
                    ps = P.psum()
                    for c in range(22):
                        P.pe(lambda e, c=c, mi=mi, ps=ps, w=w, n=n: e.matmul(ps[:, 0:n], lhsT=w[:, c, mi * 128:(mi + 1) * 128], rhs=actb[:, c, 0:n], start=(c == 0), stop=(c == 21)),
                             r=[w, actb], w=[ps], join=(c > 0))
                    P.dve(lambda e, ps=ps, m=m, a=a, xn=xn, n=n: e.tensor_tensor(out=xn[:, m, 0:n], in0=ps[:, 0:n], in1=a[:, m, 1:n + 1], op=ALU.add), r=[ps, a], w=[xn], join=(m > 0))
            P.dma(self.X[:, t0:t0 + n].rearrange("(c p) t -> p c t", p=128), xn[:, :, 0:n], r=[xn])


def make_consts(Lmax):
    c = np.zeros((128, NCON), np.float32)
    c[:, K_ID:K_ID + 128] = np.eye(128, dtype=np.float32)
    j = np.arange(128)[:, None]
    i = np.arange(128)[None, :]
    mnf = np.where(i >= j, 0.0, -30000.0).astype(np.float32)
    mnb = np.where(j > i, 0.0, -30000.0).astype(np.float32)
    c[:, K_MNF:K_MNF + 512] = np.tile(mnf, (1, 4))
    c[:, K_MNB:K_MNB + 512] = np.tile(mnb, (1, 4))
    rm = np.ones(512, np.float32)
    rm[::128] = 0.0
    c[:, K_RM:K_RM + 512] = rm[None, :]
    sel = np.zeros((128, 16, 128), np.float32)
    for q in range(16):
        sel[q, q, :] = 1.0 if q < 8 else -1.0
    c[:, K_SEL:K_SEL + 2048] = sel.reshape(128, 2048)
    c[0:8, K_MF] = 1.0
    c[8:16, K_MF + 1] = 1.0
    c[8:16, K_MF + 2] = -1.0
    prot = np.zeros((128, 128), np.float32)
    for m in range(128):
        if m % 64 < 32:
            prot[m + 32, m] = -1.0
        else:
            prot[m - 32, m] = 1.0
    c[:, K_PROT:K_PROT + 128] = prot
    c[:, K_EJ] = 127 - np.arange(128)
    c[:, K_EJ + 1] = np.arange(128)
    c[:, K_EI:K_EI + 128] = (np.arange(128) + 1)[None, :]
    c[:, K_EI + 128:K_EI + 256] = (128 - np.arange(128))[None, :]
    c[:, K_M1:K_M1 + 128] = np.maximum(i - j, 0)
    c[:, K_M2:K_M2 + 128] = np.maximum(j - i, 0)
    c[0:64, K_J:K_J + 64] = np.eye(64, dtype=np.float32)[::-1]
    qc = np.arange(64)[None, :]
    kc = np.arange(64)[:, None]
    cst = np.clip(qc - 8, 0, 48)
    cv = ((kc >= cst) & (kc < cst + 16)).astype(np.float32)
    c[:, K_CV:K_CV + 64] = np.tile(cv, (2, 1))
    half = 32
    inv = (1.0 / (10000.0 ** (np.arange(half, dtype=np.float32) / half))).astype(np.float32)
    pos = np.arange(Lmax, dtype=np.float32)
    ang = pos[None, :] * inv[:, None]
    f = (np.arange(128) % 64) % 32
    rot = np.stack([np.cos(ang)[f], np.sin(ang)[f]]).astype(np.float32)
    return c, rot


def make_rvc(L):
    rows = L // 64
    NT = L // 512
    r = np.zeros((128, NT, 8, 8), np.float32)
    for t in range(NT):
        R0 = 8 * t
        rs0 = min(max(R0 - 4, 0), rows - 8)
        kt0 = rs0 // 2
        for j in range(8):
            for a in range(2):
                kr = 2 * (kt0 + j) + a
                for b in range(8):
                    qr = R0 + b
                    rs = min(max(qr - 4, 0), rows - 8)
                    if rs <= kr < rs + 8:
                        r[a * 64:(a + 1) * 64, t, j, b] = 1.0
    return r.reshape(128, NT * 64)


def make_small(inp, depth):
    sm = np.zeros((depth + 1, 128, NSM), np.float32)

    def pc(v):
        return np.asarray(v, np.float32).reshape(-1, 128).T

    for l in range(depth):
        sm[l, :, S_GMIX:S_GMIX + 8] = pc(inp["norm_mix"][l])
        sm[l, :, S_GFFN:S_GFFN + 8] = pc(inp["norm_ffn"][l])
        sm[l, :, S_GB:S_GB + 24] = pc(inp["gate_bias"][l])
        cw = np.asarray(inp["ssd_conv_w"][l], np.float32)
        sm[l, :, S_CW:S_CW + 40] = cw.reshape(5, 8, 128).transpose(2, 1, 0).reshape(128, 40)
        sm[l, :, S_CB:S_CB + 8] = pc(inp["ssd_conv_b"][l])
        fw = np.asarray(inp["ffn_conv_w"][l], np.float32)
        sm[l, :, S_FW:S_FW + 132] = fw.reshape(3, 44, 128).transpose(2, 1, 0).reshape(128, 132)
        sm[l, :, S_FB:S_FB + 44] = pc(inp["ffn_conv_b"][l])
        sm[l, :, S_DSK:S_DSK + 8] = np.asarray(inp["ssd_d"][l], np.float32)[None, :]
        sm[l, :, S_NW:S_NW + 512] = np.asarray(inp["ssd_norm"][l], np.float32)[None, :]
        sm[l, :, S_TH:S_TH + 8] = np.asarray(inp["ret_theta"][l], np.float32).reshape(8)[None, :]
        sm[l, 0:16, S_DTB] = np.asarray(inp["ssd_dt_bias"][l], np.float32).reshape(16)
        sm[l, 0:16, S_ALOG] = np.asarray(inp["ssd_a_log"][l], np.float32).reshape(16)
    sm[depth, :, S_GMIX:S_GMIX + 8] = pc(inp["norm_final"])
    return sm


def host_inputs(inp, depth, seq_lens):
    Lmax = max(seq_lens)
    c, rot = make_consts(Lmax)
    rp = np.zeros((depth, 8, 15, 127), np.float32)
    rp[:, :, :, 48:79] = np.asarray(inp["na_rpb"], np.float32)[:depth]
    common = {
        "w_in": np.ascontiguousarray(np.asarray(inp["w_in"], np.float32)[:depth]),
        "w_branch": np.ascontiguousarray(np.asarray(inp["w_branch"], np.float32)[:depth]),
        "w_out": np.ascontiguousarray(np.asarray(inp["w_out"], np.float32)[:depth]),
        "ffn_w_up": np.ascontiguousarray(np.asarray(inp["ffn_w_up"], np.float32)[:depth]),
        "ffn_w_down": np.ascontiguousarray(np.asarray(inp["ffn_w_down"], np.float32)[:depth]),
        "small": make_small(inp, depth),
        "consts": c,
        "rot": rot,
        "rpbp": rp,
    }
    for L in sorted(set(seq_lens)):
        common["rvc%d" % L] = make_rvc(L)
    return common


_CACHE = {}


def kernel(**inputs):
    xp = np.asarray(inputs["x_prompt"], np.float32)
    xs = np.asarray(inputs["x_sample"], np.float32)
    depth = np.asarray(inputs["w_in"]).shape[0]
    n = 8
    seq_lens = [xp.shape[1], xp.shape[1], xs.shape[1]]
    key = (tuple(seq_lens), depth)
    if key not in _CACHE:
        _CACHE[key] = Builder(seq_lens, depth).build()
    nc = _CACHE[key]
    common = host_inputs(inputs, depth, seq_lens)
    in_maps = []
    for c in range(n):
        m = dict(common)
        m["x0"] = np.ascontiguousarray(xp[2 * c])
        m["x1"] = np.ascontiguousarray(xp[2 * c + 1])
        m["x2"] = np.ascontiguousarray(xs[c])
        in_maps.append(m)
    res = run_bass_kernel_spmd(nc, in_maps, core_ids=list(range(n)))
    yp = np.empty_like(xp)
    ys = np.empty_like(xs)
    for c in range(n):
        r = res.results[c]
        yp[2 * c] = r["y0"]
        yp[2 * c + 1] = r["y1"]
        ys[c] = r["y2"]
    return (yp, ys)
```

```python
import math
from contextlib import ExitStack
import numpy as np
import concourse.bass as bass
import concourse.mybir as mybir
from concourse.bass_utils import run_bass_kernel_spmd

F32 = mybir.dt.float32
BF16 = mybir.dt.bfloat16
U8 = mybir.dt.uint8
AF = mybir.ActivationFunctionType
ALU = mybir.AluOpType
AX = mybir.AxisListType

DM = 1024
KC = 8
PROJ = 7696
DFF = 2816
EPS = 1e-6
ENG = ["pe", "act", "dve", "pool", "sp"]
NDS = 72
ARENA = 200 * 1024

C_Z, C_XBC, C_DT, C_RQ, C_RK, C_RV, C_RG, C_NQ, C_NK, C_NV, C_GATE = 0, 512, 1536, 1552, 1808, 2064, 2576, 3088, 3600, 4112, 4624

S_GMIX, S_GFFN, S_GB, S_CW, S_CB, S_FW, S_FB, S_DSK, S_NW, S_TH, S_DTB, S_ALOG = 0, 8, 16, 40, 80, 88, 220, 264, 272, 784, 792, 793
NSM = 800
K_ID = 0
K_MNF = 128
K_MNB = 640
K_RM = 1152
K_SEL = 1664
K_MF = 3712
K_PROT = 3716
K_EJ = 3844
K_EI = 3848
K_M1 = 4104
K_M2 = 4232
K_J = 4360
K_CV = 4424
K_RMK = 4488
NCON = 4512


class Op:
    __slots__ = ("eng", "fn", "deps", "needed", "value", "is_dma", "sem", "dval")


class Buf:
    def __init__(self, name, ap):
        self.name = name
        self.ap = ap
        self.w = {}
        self.r = {}
        self.dsem = None
        self.const = False

    def __getitem__(self, k):
        return self.ap[k]


class Prog:
    def __init__(self, nc, es):
        self.nc = nc
        self.ops = {e: [] for e in ENG}
        self.arena = es.enter_context(nc.sbuf_tensor("arena", [128, ARENA], U8))
        self.off = 0
        self.esem = {e: es.enter_context(nc.semaphore("s_" + e)) for e in ENG}
        self.dsems = [es.enter_context(nc.semaphore("d%d" % i)) for i in range(NDS)]
        self.dcount = [0] * NDS
        self.dnext = 0
        self.dlast = {}
        self.last_real = {}
        self.ps = []
        for i in range(8):
            t = es.enter_context(nc.psum_tensor("ps%d" % i, [128, 512], F32))
            self.ps.append(Buf("ps%d" % i, t[:]))
        self.psi = 0
        self.psk = {}
        self.evi = 0
        self.dummy = Buf("dummy", None)
        self.nops = 0

    def alloc(self, name, shape, dt):
        esz = 4 if dt == F32 else 2
        n = 1
        for s in shape:
            n *= s
        off = (self.off + 63) // 64 * 64
        assert off + n * esz <= ARENA, ("SBUF arena overflow", name, off, n * esz)
        ap = self.arena[:, off:off + n * esz].bitcast(dt)
        if len(shape) == 2:
            ap = ap.rearrange("p (a b) -> p a b", a=shape[0])
        elif len(shape) == 3:
            ap = ap.rearrange("p (a b c) -> p a b c", a=shape[0], b=shape[1])
        self.off = off + n * esz
        return Buf(name, ap)

    def psum(self, lo=0, hi=8):
        n = hi - lo
        k = self.psk.get((lo, hi), 0)
        self.psk[(lo, hi)] = k + 1
        return self.ps[lo + k % n]

    def op(self, eng, fn, r=(), w=(), join=False, is_dma=False, sem=None):
        o = Op()
        o.eng, o.fn, o.needed, o.value, o.is_dma, o.sem, o.dval = eng, fn, False, 0, is_dma, sem, 0
        deps = {}

        def add(d, raw):
            if d.is_dma or d.eng != eng or (raw and eng in ("act", "dve", "pool")):
                deps[id(d)] = d

        for b in r:
            for d in b.w.values():
                add(d, True)
        for b in w:
            if eng == "pe" and not join and b.w and not b.r and not is_dma:
                assert all(d.eng != "pe" for d in b.w.values()), ("PSUM bank overwritten before being read", b.name)
            for d in b.r.values():
                add(d, False)
            if not join or b.r:
                for d in b.w.values():
                    add(d, False)
        o.deps = list(deps.values())
        for d in o.deps:
            d.needed = True
        key = ("d", sem) if is_dma else eng
        for b in r:
            if not b.const:
                b.r[key] = o
        for b in w:
            if join and not b.r:
                b.w[key] = o
            else:
                b.w = {key: o}
            b.r = {}
        self.ops[eng].append(o)
        if not is_dma:
            self.last_real[eng] = o
        self.nops += 1
        return o

    def pe(self, fn, r=(), w=(), join=False):
        return self.op("pe", fn, r, w, join)

    def act(self, fn, r=(), w=(), join=False):
        return self.op("act", fn, r, w, join)

    def dve(self, fn, r=(), w=(), join=False):
        return self.op("dve", fn, r, w, join)

    def pool(self, fn, r=(), w=(), join=False):
        return self.op("pool", fn, r, w, join)

    def dma(self, out, in_, r=(), w=(), join=False, q="sp"):
        bl = list(w) + list(r)
        b = bl[0] if bl else self.dummy
        if b.dsem is None:
            b.dsem = self.dnext % NDS
            self.dnext += 1
        s = b.dsem
        o = self.op(q, lambda e: e.dma_start(out=out, in_=in_), r, w, join, is_dma=True, sem=s)
        self.dcount[s] += 16
        o.dval = self.dcount[s]
        self.dlast[s] = o
        return o

    def barrier(self):
        lasts = list(self.last_real.values()) + list(self.dlast.values())
        for e in ENG:
            o = Op()
            o.eng, o.fn, o.needed, o.value, o.is_dma, o.sem, o.dval = e, (lambda h: None), False, 0, False, None, 0
            o.deps = [d for d in lasts if d.is_dma or d.eng != e]
            for d in o.deps:
                d.needed = True
            self.ops[e].append(o)
        self.dlast = {}
        self.dummy = Buf("dummy", None)

    def evac(self, out, in_, r, w, join=False):
        self.evi += 1
        if self.evi % 2:
            return self.act(lambda e: e.activation(out=out, in_=in_, func=AF.Copy), r, w, join)
        return self.dve(lambda e: e.tensor_copy(out=out, in_=in_), r, w, join)

    def emit(self, block):
        for e in ENG:
            c = 0
            for o in self.ops[e]:
                if o.needed and not o.is_dma:
                    c += 1
                    o.value = c

        def run(e, h):
            known = {}
            for o in self.ops[e]:
                need = {}
                for d in o.deps:
                    if d.is_dma:
                        key, sem, val = ("d", d.sem), self.dsems[d.sem], d.dval
                    else:
                        key, sem, val = d.eng, self.esem[d.eng], d.value
                    if need.get(key, (None, 0))[1] < val:
                        need[key] = (sem, val)
                for key, (sem, val) in need.items():
                    if known.get(key, 0) < val:
                        h.wait_ge(sem, val)
                        known[key] = val
                inst = o.fn(h)
                if inst is None:
                    continue
                if o.is_dma:
                    inst.then_inc(self.dsems[o.sem], 16)
                elif o.needed:
                    inst.then_inc(self.esem[e], 1)

        @block.tensor
        def _(h):
            run("pe", h)

        @block.scalar
        def _(h):
            run("act", h)

        @block.vector
        def _(h):
            run("dve", h)

        @block.gpsimd
        def _(h):
            run("pool", h)

        @block.sync
        def _(h):
            run("sp", h)


def bc(ap, shape):
    return ap.broadcast_to(shape)


class Builder:
    def __init__(self, seq_lens, depth, debug=()):
        self.seq_lens = list(seq_lens)
        self.depth = depth
        self.debug = set(debug)
        self.run_layers = depth
        self._dumps = {}
        self.only = None
        self.Lmax = max(seq_lens)
        self.nc = bass.Bass("TRN2", target_bir_lowering=False)

    def dram(self, name, shape, dt, kind="Internal"):
        if name in self.debug:
            kind = "ExternalOutput"
        return self.nc.dram_tensor(name, shape, dt, kind=kind).ap()

    def build(self, phases=None):
        nc = self.nc
        D = self.depth
        Lm = self.Lmax
        I = "ExternalInput"
        self.x_in = [self.dram("x%d" % i, [L, DM], F32, I) for i, L in enumerate(self.seq_lens)]
        self.y_out = [self.dram("y%d" % i, [L, DM], F32, "ExternalOutput") for i, L in enumerate(self.seq_lens)]
        self.w_in = self.dram("w_in", [D, DM, PROJ], F32, I)
        self.w_br = self.dram("w_branch", [D, 3, 512, DM], F32, I)
        self.w_out = self.dram("w_out", [D, DM, DM], F32, I)
        self.w_up = self.dram("ffn_w_up", [D, DM, 2 * DFF], F32, I)
        self.w_dn = self.dram("ffn_w_down", [D, DFF, DM], F32, I)
        self.small = self.dram("small", [D + 1, 128, NSM], F32, I)
        self.consts = self.dram("consts", [128, NCON], F32, I)
        self.rot = self.dram("rot", [2, 128, Lm], F32, I)
        self.rpbp = self.dram("rpbp", [D, 8, 15, 127], F32, I)
        self.rvc = {L: self.dram("rvc%d" % L, [128, (L // 512) * 64], F32, I) for L in sorted(set(self.seq_lens))}
        self.wb_in = self.dram("wb_in", [D, DM, PROJ], BF16)
        self.wb_br = self.dram("wb_br", [D, 3, 512, DM], BF16)
        self.wb_out = self.dram("wb_out", [D, DM, DM], BF16)
        self.wb_up = self.dram("wb_up", [D, DM, 2 * DFF], BF16)
        self.wb_dn = self.dram("wb_dn", [D, DFF, DM], BF16)
        self.X = self.dram("X", [DM, Lm], F32)
        self.XM = self.dram("XM", [DM, Lm], F32)
        self.XBC = self.dram("XBC", [1024, Lm], BF16)
        self.DT = self.dram("DT", [16, Lm], F32)
        self.RQ = self.dram("RQ", [256, Lm], BF16)
        self.RK = self.dram("RK", [256, Lm], BF16)
        self.NQ = self.dram("NQ", [512, Lm], BF16)
        self.NK = self.dram("NK", [512, Lm], BF16)
        self.GT = self.dram("GT", [3072, Lm], BF16)
        self.ZT = self.dram("ZT", [Lm, 512], BF16)
        self.RVT = self.dram("RVT", [Lm, 512], BF16)
        self.RGT = self.dram("RGT", [Lm, 512], BF16)
        self.NVT = self.dram("NVT", [Lm, 512], BF16)
        self.BCF = self.dram("BCF", [512, Lm], BF16)
        self.XTOK = self.dram("XTOK", [Lm, 512], BF16)
        self.FT = self.dram("FT", [Lm, 128], F32)
        self.UF = self.dram("UF", [16, Lm], F32)
        self.RKR = self.dram("RKR", [256, Lm], BF16)
        self.OS = self.dram("OS", [512, Lm], BF16)
        self.OR = self.dram("OR", [512, Lm], BF16)
        self.ON = self.dram("ON", [512, Lm], BF16)

        with ExitStack() as es:
            P = Prog(nc, es)
            self.P = P
            self.setup_consts()
            self.convert_weights()
            P.barrier()
            self.base_off = P.off
            for s, L in enumerate(self.seq_lens):
                self.s, self.L = s, L
                self.NT = L // 512
                self.phase(self.ph_in)
                for l in range(self.run_layers):
                    self.l = l
                    for nm, f in (("a", self.ph_a), ("scan", self.ph_scan), ("na", self.ph_na), ("merge", self.ph_merge), ("ffn", self.ph_ffn)):
                        if self.only is None or nm in self.only:
                            self.phase(f)
                self.phase(self.ph_out)
            blk = es.enter_context(nc.Block())
            P.emit(blk)
        return nc

    def dump(self, name, buf, ap, shape, dt):
        if name not in self.debug:
            return
        if name not in self._dumps:
            self._dumps[name] = self.nc.dram_tensor(name, [128] + list(shape), dt, kind="ExternalOutput").ap()
        self.P.dma(self._dumps[name], ap, r=[buf])

    def phase(self, fn):
        self.P.off = self.base_off
        self.P.dnext = 1
        fn()
        assert self.P.dnext <= NDS, self.P.dnext
        self.P.barrier()

    def setup_consts(self):
        P = self.P
        self.cf = P.alloc("cf", [NCON], F32)
        P.dma(self.cf[:], self.consts[:, :], w=[self.cf])
        cf = self.cf
        self.identb = P.alloc("identb", [128], BF16)
        P.dve(lambda e: e.tensor_copy(out=self.identb[:], in_=cf[:, K_ID:K_ID + 128]), r=[cf], w=[self.identb])
        self.protb = P.alloc("protb", [128], BF16)
        P.dve(lambda e: e.tensor_copy(out=self.protb[:], in_=cf[:, K_PROT:K_PROT + 128]), r=[cf], w=[self.protb])
        self.jb = P.alloc("jb", [64], BF16)
        P.dve(lambda e: e.tensor_copy(out=self.jb[:], in_=cf[:, K_J:K_J + 64]), r=[cf], w=[self.jb])
        self.onesm = P.alloc("onesm", [128], BF16)
        P.dve(lambda e: e.memset(self.onesm[:], 1.0 / 1024.0), w=[self.onesm])
        self.onesf = P.alloc("onesf", [128], F32)
        P.dve(lambda e: e.memset(self.onesf[:], 1.0), w=[self.onesf])
        self.epsb = P.alloc("epsb", [1], F32)
        P.dve(lambda e: e.memset(self.epsb[:], EPS), w=[self.epsb])
        for b in (cf, self.identb, self.protb, self.jb, self.onesm, self.onesf, self.epsb):
            b.const = True

    def convert_weights(self):
        P = self.P
        D = self.depth
        for l in range(D):
            for (src, dst, rows) in ((self.w_in[l], self.wb_in[l], DM), (self.w_out[l], self.wb_out[l], DM),
                                     (self.w_up[l], self.wb_up[l], DM), (self.w_dn[l], self.wb_dn[l], DFF)):
                for r0 in range(0, rows, 128):
                    P.dma(dst[r0:r0 + 128, :], src[r0:r0 + 128, :], q="pool")
            for i in range(3):
                for r0 in range(0, 512, 128):
                    P.dma(self.wb_br[l, i, r0:r0 + 128, :], self.w_br[l, i, r0:r0 + 128, :], q="pool")

    def load_small(self, l):
        P = self.P
        sm = P.alloc("sm", [NSM], F32)
        P.dma(sm[:], self.small[l], w=[sm])
        return sm

    def norm(self, xt, n, g, h, sq, rs, out_f32=False):
        P = self.P
        P.act(lambda e: e.activation(out=sq[:, :, 0:n], in_=xt[:, :, 0:n], func=AF.Square), r=[xt], w=[sq])
        ps = P.psum()
        for c in range(KC):
            P.pe(lambda e, c=c: e.matmul(ps[:, 0:n], lhsT=self.onesm[:], rhs=sq[:, c, 0:n], start=(c == 0), stop=(c == KC - 1)),
                 r=[sq], w=[ps], join=(c > 0))
        P.act(lambda e: e.activation(out=rs[:, 0:n], in_=ps[:, 0:n], func=AF.Sqrt, bias=self.epsb[:, 0:1]), r=[ps], w=[rs])
        P.dve(lambda e: e.reciprocal(out=rs[:, 0:n], in_=rs[:, 0:n]), r=[rs], w=[rs])
        for c in range(KC):
            P.dve(lambda e, c=c: e.scalar_tensor_tensor(out=h[:, c, 0:n], in0=xt[:, c, 0:n], scalar=g[:, c:c + 1], in1=rs[:, 0:n],
                                                        op0=ALU.mult, op1=ALU.mult), r=[xt, rs], w=[h], join=(c > 0))

    def ph_in(self):
        P = self.P
        x = self.x_in[self.s]
        cf = self.cf
        xin = [P.alloc("xin%d" % i, [4, DM], F32) for i in range(2)]
        xt = [P.alloc("xt%d" % i, [KC, 512], F32) for i in range(2)]
        for t in range(self.NT):
            t0 = t * 512
            a, b = xin[t % 2], xt[t % 2]
            P.dma(a[:], x[t0:t0 + 512, :].rearrange("(tc p) f -> p tc f", p=128), w=[a])
            for fc in range(KC):
                ps = P.psum()
                for tc in range(4):
                    P.pe(lambda e, tc=tc, fc=fc, ps=ps, a=a: e.transpose(ps[:, tc * 128:(tc + 1) * 128], a[:, tc, fc * 128:(fc + 1) * 128],
                                                                         cf[:, K_ID:K_ID + 128]), r=[a], w=[ps], join=(tc > 0))
                P.evac(b[:, fc, :], ps[:], r=[ps], w=[b], join=(fc > 0))
            P.dma(self.X[:, t0:t0 + 512].rearrange("(c p) t -> p c t", p=128), b[:], r=[b])

    def ph_out(self):
        P = self.P
        y = self.y_out[self.s]
        cf = self.cf
        sm = self.load_small(self.depth)
        xt = [P.alloc("xt%d" % i, [KC, 512], F32) for i in range(2)]
        sq = P.alloc("sq", [KC, 512], BF16)
        rs = P.alloc("rs", [512], F32)
        yn = P.alloc("yn", [KC, 512], F32)
        yt = [P.alloc("yt%d" % i, [4, DM], F32) for i in range(2)]
        for t in range(self.NT):
            t0 = t * 512
            a, o = xt[t % 2], yt[t % 2]
            P.dma(a[:], self.X[:, t0:t0 + 512].rearrange("(c p) t -> p c t", p=128), w=[a])
            self.norm(a, 512, sm[:, S_GMIX:S_GMIX + 8], yn, sq, rs)
            for tc in range(4):
                for hf in range(2):
                    ps = P.psum()
                    for k in range(4):
                        P.pe(lambda e, k=k, hf=hf, tc=tc, ps=ps: e.transpose(ps[:, k * 128:(k + 1) * 128], yn[:, hf * 4 + k, tc * 128:(tc + 1) * 128],
                                                                            cf[:, K_ID:K_ID + 128]), r=[yn], w=[ps], join=(k > 0))
                    P.evac(o[:, tc, hf * 512:(hf + 1) * 512], ps[:], r=[ps], w=[o], join=(tc > 0 or hf > 0))
            P.dma(y[t0:t0 + 512, :].rearrange("(tc p) f -> p tc f", p=128), o[:], r=[o])

    def ph_a(self):
        P = self.P
        l = self.l
        sm = self.load_small(l)
        W = self.wb_in[l]
        xt = [P.alloc("xt%d" % i, [KC, 512], F32) for i in range(2)]
        sq = P.alloc("sq", [KC, 512], BF16)
        rs = P.alloc("rs", [512], F32)
        hb = [P.alloc("h%d" % i, [KC, 512], BF16) for i in range(2)]
        wt = [P.alloc("w%d" % i, [KC, 1024], BF16) for i in range(2)]
        of = [P.alloc("of%d" % i, [4, 512], BF16) for i in range(2)]
        ot = [P.alloc("ot%d" % i, [4, 512], BF16) for i in range(2)]
        dtf = P.alloc("dtf", [512], F32)
        wi = [0]
        oi = [0]

        def loadw(c0, n):
            w = wt[wi[0] % 2]
            wi[0] += 1
            P.dma(w[:, :, 0:n], W[:, c0:c0 + n].rearrange("(c p) n -> p c n", p=128), w=[w])
            return w

        def loadx(t):
            P.dma(xt[t % 2][:], self.X[:, t * 512:(t + 1) * 512].rearrange("(c p) t -> p c t", p=128), w=[xt[t % 2]])

        jobs = [("d", None, C_DT, 16)]
        jobs += [("f", d_, c_, n_) for (d_, c_, n_) in
                 [(self.XBC, C_XBC, 1024), (self.RQ, C_RQ, 256), (self.RK, C_RK, 256), (self.NQ, C_NQ, 512),
                  (self.NK, C_NK, 512), (self.GT, C_GATE, 1024), (self.GT, C_GATE + 1024, 1024), (self.GT, C_GATE + 2048, 1024)]]
        jobs += [("t", d_, c_, 512) for (d_, c_) in [(self.ZT, C_Z), (self.RVT, C_RV), (self.RGT, C_RG), (self.NVT, C_NV)]]
        seq = [(t, j) for t in range(self.NT) for j in range(len(jobs))]
        wq = {}

        def ensure(i):
            if i < len(seq) and i not in wq:
                wq[i] = loadw(jobs[seq[i][1]][2], jobs[seq[i][1]][3])

        loadx(0)
        ensure(0)
        for i, (t, j) in enumerate(seq):
            kind, dst, c0, n = jobs[j]
            t0 = t * 512
            a, h = xt[t % 2], hb[t % 2]
            if j == 0:
                if t + 1 < self.NT:
                    loadx(t + 1)
                self.norm(a, 512, sm[:, S_GMIX:S_GMIX + 8], h, sq, rs)
            ensure(i + 1)
            w = wq.pop(i)
            if kind == "d":
                ps = P.psum()
                for c in range(KC):
                    P.pe(lambda e, c=c, w=w, ps=ps, h=h: e.matmul(ps[0:16, :], lhsT=w[:, c, 0:16], rhs=h[:, c, :], start=(c == 0), stop=(c == KC - 1)),
                         r=[w, h], w=[ps], join=(c > 0))
                P.act(lambda e, ps=ps: e.activation(out=dtf[0:16, :], in_=ps[0:16, :], func=AF.Copy), r=[ps], w=[dtf])
                P.dma(self.DT[:, t0:t0 + 512], dtf[0:16, :], r=[dtf])
            elif kind == "f":
                r0 = c0 - (C_GATE if dst is self.GT else c0)
                for m in range(n // 128):
                    if m % 4 == 0:
                        o = of[oi[0] % 2]
                        oi[0] += 1
                    ps = P.psum()
                    for c in range(KC):
                        P.pe(lambda e, c=c, m=m, w=w, ps=ps, h=h: e.matmul(ps[:], lhsT=w[:, c, m * 128:(m + 1) * 128], rhs=h[:, c, :],
                                                                          start=(c == 0), stop=(c == KC - 1)), r=[w, h], w=[ps], join=(c > 0))
                    P.evac(o[:, m % 4, :], ps[:], r=[ps], w=[o], join=(m % 4 > 0))
                    if m % 4 == 3 or m == n // 128 - 1:
                        k = m % 4 + 1
                        rr = r0 + (m - (k - 1)) * 128
                        P.dma(dst[rr:rr + k * 128, t0:t0 + 512].rearrange("(c p) t -> p c t", p=128), o[:, 0:k, :], r=[o])
            else:
                o = ot[oi[0] % 2]
                oi[0] += 1
                for tc in range(4):
                    ps = P.psum()
                    for c in range(KC):
                        P.pe(lambda e, c=c, tc=tc, w=w, ps=ps, h=h: e.matmul(ps[:], lhsT=h[:, c, tc * 128:(tc + 1) * 128], rhs=w[:, c, 0:512],
                                                                            start=(c == 0), stop=(c == KC - 1)), r=[w, h], w=[ps], join=(c > 0))
                    P.evac(o[:, tc, :], ps[:], r=[ps], w=[o], join=(tc > 0))
                P.dma(dst[t0:t0 + 512, :].rearrange("(tc p) c -> p tc c", p=128), o[:], r=[o])

    def rotary(self, src, dst, cs, sn, scale, tmp1, tmp2, split=False):
        P = self.P
        for hp in range(2):
            ps = P.psum()
            P.pe(lambda e, hp=hp, ps=ps: e.matmul(ps[:], lhsT=self.protb[:], rhs=src[:, hp, :], start=True, stop=True), r=[src], w=[ps])
            P.dve(lambda e, hp=hp: e.scalar_tensor_tensor(out=tmp1[:], in0=src[:, hp, :], scalar=scale, in1=cs[:], op0=ALU.mult, op1=ALU.mult),
                  r=[src, cs], w=[tmp1])
            P.dve(lambda e, ps=ps: e.scalar_tensor_tensor(out=tmp2[:], in0=ps[:], scalar=scale, in1=sn[:], op0=ALU.mult, op1=ALU.mult),
                  r=[ps, sn], w=[tmp2])
            if split:
                for hf in range(2):
                    o = hf * 64
                    P.pool(lambda e, hp=hp, hf=hf, o=o: e.tensor_tensor(out=dst[o:o + 64, 2 * hp + hf, :], in0=tmp1[o:o + 64, :], in1=tmp2[o:o + 64, :], op=ALU.add),
                           r=[tmp1, tmp2], w=[dst], join=True)
            else:
                P.pool(lambda e, hp=hp: e.tensor_tensor(out=dst[:, hp, :], in0=tmp1[:], in1=tmp2[:], op=ALU.add), r=[tmp1, tmp2], w=[dst], join=(hp > 0))

    def transpose_to(self, src_aps, dst_ap, rbufs, wbuf, join):
        P = self.P
        ps = P.psum()
        psb = ps.ap.bitcast(BF16)
        for k, s in enumerate(src_aps):
            P.pe(lambda e, k=k, s=s, psb=psb: e.transpose(psb[:, k * 128:(k + 1) * 128], s, self.identb[:]), r=rbufs, w=[ps], join=(k > 0))
        n = len(src_aps)
        src = psb[:, 0:n * 128]
        if len(dst_ap.shape) == 3:
            src = src.rearrange("p (a b) -> p a b", a=n)
        P.evac(dst_ap, src, r=[ps], w=[wbuf], join=join)

    def ph_scan(self):
        P = self.P
        l, L, NT = self.l, self.L, self.NT
        C = L // 128
        cf = self.cf
        sm = self.load_small(l)
        RSm = [P.alloc("RS%d" % c, [512], BF16) for c in range(C)]
        RSf = [Buf("RSf%d" % c, b.ap[0:64]) for c, b in enumerate(RSm)]
        RSb = [Buf("RSb%d" % c, b.ap[64:128]) for c, b in enumerate(RSm)]
        lgv = P.alloc("lgv", [8], F32)
        P.act(lambda e: e.activation(out=lgv[:], in_=sm[:, S_TH:S_TH + 8], func=AF.Exp), r=[sm], w=[lgv])
        P.dve(lambda e: e.tensor_scalar(out=lgv[:], in0=lgv[:], scalar1=-1.0, scalar2=None, op0=ALU.mult), r=[lgv], w=[lgv])
        lg_hd = lgv[:].rearrange("p (d h) -> p h d", d=2)
        A16 = P.alloc("A16", [1], F32)
        P.act(lambda e: e.activation(out=A16[0:16, :], in_=sm[0:16, S_ALOG:S_ALOG + 1], func=AF.Exp), r=[sm], w=[A16])
        P.dve(lambda e: e.tensor_scalar(out=A16[0:16, :], in0=A16[0:16, :], scalar1=-1.0, scalar2=None, op0=ALU.mult), r=[A16], w=[A16])
        WFB = P.alloc("WFB", [4, 2], F32)
        P.dve(lambda e: e.tensor_tensor(out=WFB[:], in0=lg_hd, in1=bc(cf[:, K_EJ:K_EJ + 2].unsqueeze(1), [128, 4, 2]), op=ALU.mult), r=[lgv], w=[WFB])
        P.act(lambda e: e.activation(out=WFB[:], in_=WFB[:], func=AF.Exp), r=[WFB], w=[WFB])
        DECR = P.alloc("DECR", [4], F32)
        P.act(lambda e: e.activation(out=DECR[0:64, :], in_=lgv[0:64, 0:4], func=AF.Exp, scale=128.0), r=[lgv], w=[DECR])
        P.act(lambda e: e.activation(out=DECR[64:128, :], in_=lgv[64:128, 4:8], func=AF.Exp, scale=128.0), r=[lgv], w=[DECR], join=True)
        mark_rs = P.off
        SF = [P.alloc("SF%d" % c, [512], BF16) for c in range(C)]
        SB = [P.alloc("SB%d" % c, [512], BF16) for c in range(C)]
        DEC = P.alloc("DEC", [C, 16], F32)
        mark = P.off

        xbch = [P.alloc("xbch%d" % i, [8, 516], BF16) for i in range(2)]
        acc = [P.alloc("acc%d" % i, [512], F32) for i in range(3)]
        xc = P.alloc("xc", [8, 512], BF16)
        xtok = [P.alloc("xtok%d" % i, [4, 512], BF16) for i in range(2)]
        btok = P.alloc("btok", [4, 256], BF16)
        Fb = [P.alloc("F%d" % i, [512], F32) for i in range(2)]
        ftb = [P.alloc("ft%d" % i, [4, 128], F32) for i in range(2)]
        dtrb = [P.alloc("dtr%d" % i, [512], F32) for i in range(2)]
        s16 = [P.alloc("s16_%d" % i, [512], F32) for i in range(6)]
        dt16 = P.alloc("dt16", [512], F32)
        et = P.alloc("et", [4], F32)
        DD = P.alloc("DD", [4, 16], F32)
        xw = [P.alloc("xw%d" % i, [512], BF16) for i in range(2)]
        for b in Fb:
            P.pool(lambda e, b=b: e.memset(b[:], 0.0), w=[b])
        cw = sm[:, S_CW:S_CW + 40].rearrange("p (c j) -> p c j", j=5)
        cb = sm[:, S_CB:S_CB + 8]
        mf, mb, nmb = cf[0:16, K_MF:K_MF + 1], cf[0:16, K_MF + 1:K_MF + 2], cf[0:16, K_MF + 2:K_MF + 3]
        def loads1(t):
            t0 = t * 512
            xb = xbch[t % 2]
            lo, hi = max(t0 - 2, 0), min(t0 + 514, L)
            if t == 0:
                P.pool(lambda e, xb=xb: e.memset(xb[:, :, 0:2], 0.0), w=[xb])
            if t == NT - 1:
                P.pool(lambda e, xb=xb: e.memset(xb[:, :, 514:516], 0.0), w=[xb])
            P.dma(xb[:, :, lo - (t0 - 2):hi - (t0 - 2)], self.XBC[:, lo:hi].rearrange("(c p) t -> p c t", p=128), w=[xb],
                  join=(t == 0 or t == NT - 1))
            P.dma(dtrb[t % 2][0:16, :], self.DT[:, t0:t0 + 512], w=[dtrb[t % 2]])

        loads1(0)
        for t in range(NT):
            t0 = t * 512
            xb = xbch[t % 2]
            dtr = dtrb[t % 2]
            if t + 1 < NT:
                loads1(t + 1)
            for c in range(8):
                a = acc[c % 3]
                P.pool(lambda e, c=c, a=a, xb=xb: e.tensor_scalar(out=a[:], in0=xb[:, c, 0:512], scalar1=cw[:, c, 0:1], scalar2=cb[:, c:c + 1],
                                                                  op0=ALU.mult, op1=ALU.add), r=[xb, sm], w=[a])
                for j in range(1, 5):
                    P.dve(lambda e, c=c, j=j, a=a, xb=xb: e.scalar_tensor_tensor(out=a[:], in0=xb[:, c, j:j + 512], scalar=cw[:, c, j:j + 1], in1=a[:],
                                                                                 op0=ALU.mult, op1=ALU.add), r=[xb, a, sm], w=[a])
                P.act(lambda e, c=c, a=a: e.activation(out=xc[:, c, :], in_=a[:], func=AF.Silu), r=[a], w=[xc], join=(c > 0))
            P.dma(self.BCF[:, t0:t0 + 512].rearrange("(c p) t -> p c t", p=128), xc[:, 4:8, :], r=[xc])
            F = Fb[t % 2]
            ft = ftb[t % 2]
            e1, la, cs, tme, tq, ew = s16
            P.act(lambda e, dtr=dtr: e.activation(out=e1[0:16, :], in_=dtr[0:16, :], func=AF.Exp, bias=sm[0:16, S_DTB:S_DTB + 1]), r=[dtr, sm], w=[e1])
            P.act(lambda e: e.activation(out=dt16[0:16, :], in_=e1[0:16, :], func=AF.Ln, bias=1.0), r=[e1], w=[dt16])
            P.act(lambda e, F=F: e.activation(out=F[32:48, :], in_=dt16[0:16, :], func=AF.Copy), r=[dt16], w=[F])
            P.dve(lambda e: e.tensor_scalar(out=la[0:16, :], in0=dt16[0:16, :], scalar1=A16[0:16, 0:1], scalar2=None, op0=ALU.mult), r=[dt16, A16], w=[la])
            P.dve(lambda e: e.tensor_tensor_scan(out=cs[0:16, :], data0=cf[0:16, K_RM:K_RM + 512], data1=la[0:16, :], initial=0.0,
                                                 op0=ALU.mult, op1=ALU.add), r=[la], w=[cs])
            P.dve(lambda e, F=F: e.scalar_tensor_tensor(out=F[0:16, :], in0=la[0:16, :], scalar=nmb, in1=cs[0:16, :], op0=ALU.mult, op1=ALU.add),
                  r=[la, cs], w=[F], join=True)
            cs3 = cs[0:16, :].rearrange("p (c j) -> p c j", j=128)
            P.dve(lambda e, F=F: e.tensor_tensor(out=tme[0:16, :].rearrange("p (c j) -> p c j", j=128), in0=bc(cs3[:, :, 127:128], [16, 4, 128]),
                                                 in1=F[0:16, :].rearrange("p (c j) -> p c j", j=128), op=ALU.subtract), r=[cs, F], w=[tme])
            P.dve(lambda e, F=F: e.tensor_scalar(out=tq[0:16, :], in0=F[0:16, :], scalar1=mb, scalar2=None, op0=ALU.mult), r=[F], w=[tq])
            P.dve(lambda e: e.scalar_tensor_tensor(out=tq[0:16, :], in0=tme[0:16, :], scalar=mf, in1=tq[0:16, :], op0=ALU.mult, op1=ALU.add),
                  r=[tme, tq], w=[tq])
            P.act(lambda e: e.activation(out=ew[0:16, :], in_=tq[0:16, :], func=AF.Exp), r=[tq], w=[ew])
            P.dve(lambda e, F=F: e.tensor_tensor(out=F[64:80, :], in0=ew[0:16, :], in1=dt16[0:16, :], op=ALU.mult), r=[ew, dt16], w=[F], join=True)
            P.dve(lambda e: e.tensor_scalar(out=e1[0:16, :], in0=tme[0:16, :], scalar1=mb, scalar2=None, op0=ALU.mult), r=[tme], w=[e1])
            P.dve(lambda e, F=F: e.scalar_tensor_tensor(out=e1[0:16, :], in0=F[0:16, :], scalar=mf, in1=e1[0:16, :], op0=ALU.mult, op1=ALU.add),
                  r=[F, e1], w=[e1])
            P.act(lambda e, F=F: e.activation(out=F[96:112, :], in_=e1[0:16, :], func=AF.Exp), r=[e1], w=[F], join=True)
            P.act(lambda e: e.activation(out=et[0:16, :], in_=cs3[:, :, 127], func=AF.Exp), r=[cs], w=[et])
            P.dve(lambda e: e.tensor_tensor(out=DD[0:16, :, :], in0=bc(et[0:16, :].unsqueeze(2), [16, 4, 16]),
                                            in1=bc(cf[0:16, K_ID:K_ID + 16].unsqueeze(1), [16, 4, 16]), op=ALU.mult), r=[et], w=[DD])
            ps = P.psum()
            P.pe(lambda e, ps=ps: e.matmul(ps[:, 0:64], lhsT=self.onesf[0:16, :], rhs=DD[0:16, :, :].rearrange("p a b -> p (a b)"), start=True, stop=True),
                 r=[DD], w=[ps])
            P.act(lambda e, ps=ps, t=t: e.activation(out=DEC[:, 4 * t:4 * t + 4, :].rearrange("p a b -> p (a b)"), in_=ps[:, 0:64], func=AF.Copy),
                  r=[ps], w=[DEC], join=True)
            P.dma(self.UF[:, t0:t0 + 512], F[0:16, :], r=[F])
            ps = P.psum()
            for tc in range(4):
                P.pe(lambda e, tc=tc, ps=ps, F=F: e.transpose(ps[:, tc * 128:(tc + 1) * 128], F[:, tc * 128:(tc + 1) * 128], cf[:, K_ID:K_ID + 128]),
                     r=[F], w=[ps], join=(tc > 0))
            P.evac(ft[:].rearrange("p a b -> p (a b)"), ps[:], r=[ps], w=[ft])
            P.dma(self.FT[t0:t0 + 512, :].rearrange("(tc p) c -> p tc c", p=128), ft[:], r=[ft])
            xk = xtok[t % 2]
            for tc in range(4):
                self.transpose_to([xc[:, c4, tc * 128:(tc + 1) * 128] for c4 in range(4)], xk[:, tc, :], [xc], xk, join=(tc > 0))
                self.transpose_to([xc[:, 4 + g, tc * 128:(tc + 1) * 128] for g in range(2)], btok[:, tc, :], [xc], btok, join=(tc > 0))
            P.dma(self.XTOK[t0:t0 + 512, :].rearrange("(tc p) c -> p tc c", p=128), xk[:], r=[xk])
            for tc in range(4):
                c = 4 * t + tc
                for d, S in ((0, SF), (1, SB)):
                    x_w = xw[d]
                    P.dve(lambda e, tc=tc, d=d, x_w=x_w, xk=xk, ft=ft: e.tensor_tensor(
                        out=x_w[:].rearrange("p (h q) -> p h q", q=64), in0=xk[:, tc, :].rearrange("p (h q) -> p h q", q=64),
                        in1=bc(ft[:, tc, 64 + 8 * d:72 + 8 * d].unsqueeze(2), [128, 8, 64]), op=ALU.mult), r=[xk, ft], w=[x_w])
                    ps = P.psum()
                    for g in range(2):
                        P.pe(lambda e, g=g, tc=tc, ps=ps, x_w=x_w: e.matmul(ps[:, g * 256:(g + 1) * 256], lhsT=btok[:, tc, g * 128:(g + 1) * 128],
                                                                          rhs=x_w[:, g * 256:(g + 1) * 256], start=True, stop=True),
                             r=[btok, x_w], w=[ps], join=(g > 0))
                    P.evac(S[c][:], ps[:], r=[ps], w=[S[c]])
        P.barrier()
        P.off = mark
        kt_ = [P.alloc("kt%d" % i, [2, 512], BF16) for i in range(2)]
        kr = P.alloc("kr", [2, 512], BF16)
        cs_ = [P.alloc("cs%d" % i, [512], F32) for i in range(2)]
        sn_ = [P.alloc("sn%d" % i, [512], F32) for i in range(2)]
        t1 = P.alloc("t1", [512], F32)
        t2 = P.alloc("t2", [512], F32)
        vtok = [P.alloc("vtok%d" % i, [4, 512], BF16) for i in range(2)]
        ktok = P.alloc("ktok", [4, 256], BF16)
        kw = P.alloc("kw", [4, 128], BF16)
        def loads1r(t):
            t0 = t * 512
            ktb, csb, snb, vt = kt_[t % 2], cs_[t % 2], sn_[t % 2], vtok[t % 2]
            P.dma(ktb[:], self.RK[:, t0:t0 + 512].rearrange("(c p) t -> p c t", p=128), w=[ktb])
            P.dma(csb[:], self.rot[0, :, t0:t0 + 512], w=[csb])
            P.dma(snb[:], self.rot[1, :, t0:t0 + 512], w=[snb])
            P.dma(vt[:], self.RVT[t0:t0 + 512, :].rearrange("(tc p) c -> p tc c", p=128), w=[vt])

        loads1r(0)
        for t in range(NT):
            t0 = t * 512
            ktb, csb, snb, vt = kt_[t % 2], cs_[t % 2], sn_[t % 2], vtok[t % 2]
            if t + 1 < NT:
                loads1r(t + 1)
            self.rotary(ktb, kr, csb, snb, 0.125, t1, t2)
            P.dma(self.RKR[:, t0:t0 + 512].rearrange("(c p) t -> p c t", p=128), kr[:], r=[kr])
            for tc in range(4):
                self.transpose_to([kr[:, hp, tc * 128:(tc + 1) * 128] for hp in range(2)], ktok[:, tc, :], [kr], ktok, join=(tc > 0))
            for tc in range(4):
                c = 4 * t + tc
                P.dve(lambda e, tc=tc: e.tensor_tensor(out=kw[:].rearrange("p h (d n) -> p h d n", d=2),
                                                       in0=bc(ktok[:, tc, :].rearrange("p (h n) -> p h n", n=64).unsqueeze(2), [128, 4, 2, 64]),
                                                       in1=bc(WFB[:].unsqueeze(3), [128, 4, 2, 64]), op=ALU.mult), r=[ktok, WFB], w=[kw])
                ps = P.psum()
                for h in range(4):
                    P.pe(lambda e, h=h, tc=tc, ps=ps, vt=vt: e.matmul(ps[:, h * 128:(h + 1) * 128], lhsT=kw[:, h, :], rhs=vt[:, tc, h * 128:(h + 1) * 128],
                                                                     start=True, stop=True), r=[kw, vt], w=[ps], join=(h > 0))
                P.evac(RSm[c][:], ps[:], r=[ps], w=[RSf[c], RSb[c]])

        P.barrier()
        P.off = mark
        if getattr(self, "scan_stop", 9) < 1:
            return
        Rf = [P.alloc("Rf%d" % i, [512], F32) for i in range(2)]
        Rb = [P.alloc("Rb%d" % i, [512], F32) for i in range(2)]
        RR = [P.alloc("RR%d" % i, [512], F32) for i in range(2)]
        RRf = [Buf("RRf%d" % i, b.ap[0:64]) for i, b in enumerate(RR)]
        RRb = [Buf("RRb%d" % i, b.ap[64:128]) for i, b in enumerate(RR)]
        for b in (Rf[0], Rb[0], RR[0]):
            P.pool(lambda e, b=b: e.memset(b[:], 0.0), w=[b])
        RRf[0].w = dict(RR[0].w)
        RRb[0].w = dict(RR[0].w)
        for k in range(C):
            cur, nxt = k % 2, (k + 1) % 2
            for (R, S, c, d0) in ((Rf, SF, k, 0), (Rb, SB, C - 1 - k, 8)):
                P.dve(lambda e, R=R, c=c, d0=d0, cur=cur, nxt=nxt: e.tensor_tensor(
                    out=R[nxt][:].rearrange("p (h q) -> p h q", q=64), in0=R[cur][:].rearrange("p (h q) -> p h q", q=64),
                    in1=bc(DEC[:, c, d0:d0 + 8].unsqueeze(2), [128, 8, 64]), op=ALU.mult), r=[R[cur], DEC], w=[R[nxt]])
                P.pool(lambda e, R=R, S=S, c=c, nxt=nxt: e.tensor_tensor(out=R[nxt][:], in0=R[nxt][:], in1=S[c][:], op=ALU.add),
                       r=[R[nxt], S[c]], w=[R[nxt]])
                P.act(lambda e, R=R, S=S, c=c, cur=cur: e.activation(out=S[c][:], in_=R[cur][:], func=AF.Copy), r=[R[cur]], w=[S[c]])
            for (RRh, RSh, c, p0) in ((RRf, RSf, k, 0), (RRb, RSb, C - 1 - k, 64)):
                P.dve(lambda e, c=c, p0=p0, cur=cur, nxt=nxt: e.tensor_tensor(
                    out=RR[nxt][p0:p0 + 64, :].rearrange("p (h q) -> p h q", q=128), in0=RR[cur][p0:p0 + 64, :].rearrange("p (h q) -> p h q", q=128),
                    in1=bc(DECR[p0:p0 + 64, :].unsqueeze(2), [64, 4, 128]), op=ALU.mult), r=[RRh[cur], DECR], w=[RRh[nxt]])
                P.pool(lambda e, c=c, p0=p0, nxt=nxt: e.tensor_tensor(out=RR[nxt][p0:p0 + 64, :], in0=RR[nxt][p0:p0 + 64, :], in1=RSm[c][p0:p0 + 64, :],
                                                                      op=ALU.add), r=[RRh[nxt], RSh[c]], w=[RRh[nxt]])
                P.act(lambda e, c=c, p0=p0, cur=cur: e.activation(out=RSm[c][p0:p0 + 64, :], in_=RR[cur][p0:p0 + 64, :], func=AF.Copy),
                      r=[RRh[cur]], w=[RSh[c]])
        for c in range(C):
            self.dump("D_SF%d" % c, SF[c], SF[c][:], [512], BF16)
            self.dump("D_SB%d" % c, SB[c], SB[c][:], [512], BF16)
        self.dump("D_DEC", DEC, DEC[:].rearrange("p a b -> p (a b)"), [C * 16], F32)
        P.barrier()
        P.off = mark
        if getattr(self, "scan_stop", 9) < 2:
            return
        bcf = [P.alloc("bcf%d" % i, [4, 512], BF16) for i in range(2)]
        xtok = [P.alloc("xtok%d" % i, [4, 512], BF16) for i in range(2)]
        ftb = [P.alloc("ft%d" % i, [4, 128], F32) for i in range(2)]
        ufb = [P.alloc("uf%d" % i, [512], F32) for i in range(2)]
        ztok = [P.alloc("ztok%d" % i, [4, 512], BF16) for i in range(2)]
        vfb = [P.alloc("vf%d" % i, [512], BF16) for i in range(2)]
        seg = [P.alloc("seg%d" % i, [512], F32) for i in range(2)]
        dec = [P.alloc("dec%d" % i, [512], F32) for i in range(2)]
        Gb = [P.alloc("G%d" % i, [512], BF16) for i in range(4)]
        m_ = [P.alloc("m%d" % i, [512], F32) for i in range(4)]
        szb = P.alloc("sz", [512], F32)
        yzb = P.alloc("yz", [512], F32)
        junk = P.alloc("junk", [512], BF16)
        ssb = P.alloc("ss", [8], F32)
        otk = P.alloc("otk", [512], BF16)
        oT = [P.alloc("oT%d" % i, [4, 512], BF16) for i in range(2)]
        DSK = sm[:, S_DSK:S_DSK + 8]
        NW = sm[:, S_NW:S_NW + 512]
        gi = [0]
        def loads2(t):
            t0 = t * 512
            bf_, xk, ft, uf, zt = bcf[t % 2], xtok[t % 2], ftb[t % 2], ufb[t % 2], ztok[t % 2]
            P.dma(bf_[:], self.BCF[:, t0:t0 + 512].rearrange("(c p) t -> p c t", p=128), w=[bf_])
            P.dma(xk[:], self.XTOK[t0:t0 + 512, :].rearrange("(tc p) c -> p tc c", p=128), w=[xk])
            P.dma(ft[:], self.FT[t0:t0 + 512, :].rearrange("(tc p) c -> p tc c", p=128), w=[ft])
            P.dma(uf[0:16, :], self.UF[:, t0:t0 + 512], w=[uf])
            P.dma(zt[:], self.ZT[t0:t0 + 512, :].rearrange("(tc p) c -> p tc c", p=128), w=[zt])

        loads2(0)
        for t in range(NT):
            t0 = t * 512
            bf_, xk, ft, uf, zt = bcf[t % 2], xtok[t % 2], ftb[t % 2], ufb[t % 2], ztok[t % 2]
            if t + 1 < NT:
                loads2(t + 1)
            oTs = oT[t % 2]
            for tc in (range(4) if getattr(self, "s2_part", "both") in ("both", "ssd") else []):
                c = 4 * t + tc
                cols = slice(tc * 128, (tc + 1) * 128)
                for d in range(2):
                    P.pool(lambda e, d=d, tc=tc, xk=xk, ft=ft: e.tensor_tensor(
                        out=vfb[d][:].rearrange("p (h q) -> p h q", q=64), in0=xk[:, tc, :].rearrange("p (h q) -> p h q", q=64),
                        in1=bc(ft[:, tc, 32 + 8 * d:40 + 8 * d].unsqueeze(2), [128, 8, 64]), op=ALU.mult), r=[xk, ft], w=[vfb[d]])
                pss = P.psum()
                for g in range(2):
                    P.pe(lambda e, g=g, pss=pss, bf_=bf_, cols=cols: e.matmul(pss[:, g * 128:(g + 1) * 128], lhsT=bf_[:, g, cols], rhs=bf_[:, 2 + g, cols],
                                                                            start=True, stop=True), r=[bf_], w=[pss], join=(g > 0))
                psY = P.psum()
                Gs = {}
                for g in range(2):
                    for d in range(2):
                        psa = P.psum()
                        q0 = d * 8 + g * 4
                        mk = K_MNF if d == 0 else K_MNB
                        P.pe(lambda e, psa=psa, mk=mk: e.matmul(psa[:], lhsT=cf[:, K_ID:K_ID + 128], rhs=cf[:, mk:mk + 512], start=True, stop=False),
                             r=[], w=[psa])
                        for r_ in range(4):
                            P.pe(lambda e, r_=r_, q0=q0, psa=psa, uf=uf, cols=cols: e.matmul(
                                psa[:, r_ * 128:(r_ + 1) * 128], lhsT=cf[0:16, K_SEL + (q0 + r_) * 128:K_SEL + (q0 + r_ + 1) * 128], rhs=uf[0:16, cols],
                                start=False, stop=(r_ == 3)), r=[uf], w=[psa], join=True)
                        sg, dc = seg[gi[0] % 2], dec[gi[0] % 2]
                        G = Gb[gi[0] % 4]
                        gi[0] += 1
                        P.dve(lambda e, psa=psa, sg=sg, ft=ft, tc=tc, q0=q0, d=d: e.tensor_tensor(
                            out=sg[:].rearrange("p (h q) -> p h q", q=128), in0=psa[:].rearrange("p (h q) -> p h q", q=128),
                            in1=bc(ft[:, tc, q0:q0 + 4].unsqueeze(2), [128, 4, 128]), op=(ALU.subtract if d == 0 else ALU.add)), r=[psa, ft], w=[sg])
                        P.act(lambda e, sg=sg, dc=dc: e.activation(out=dc[:], in_=sg[:], func=AF.Exp), r=[sg], w=[dc])
                        P.dve(lambda e, dc=dc, G=G, pss=pss, g=g: e.tensor_tensor(
                            out=G[:].rearrange("p (h q) -> p h q", q=128), in0=dc[:].rearrange("p (h q) -> p h q", q=128),
                            in1=bc(pss[:, g * 128:(g + 1) * 128].unsqueeze(1), [128, 4, 128]), op=ALU.mult), r=[dc, pss], w=[G])
                        Gs[(g, d)] = G
                for g in range(2):
                    for r_ in range(4):
                        h = g * 4 + r_
                        for d in range(2):
                            G = Gs[(g, d)]
                            P.pe(lambda e, G=G, r_=r_, h=h, d=d, psY=psY: e.matmul(psY[:, h * 64:(h + 1) * 64], lhsT=G[:, r_ * 128:(r_ + 1) * 128],
                                                                                 rhs=vfb[d][:, h * 64:(h + 1) * 64], start=(d == 0), stop=(d == 1)),
                                 r=[G, vfb[d]], w=[psY], join=not (g == 0 and r_ == 0 and d == 0))
                psI = []
                for d, S in ((0, SF), (1, SB)):
                    pi = P.psum()
                    for g in range(2):
                        P.pe(lambda e, g=g, pi=pi, S=S, c=c, bf_=bf_, cols=cols: e.matmul(pi[:, g * 256:(g + 1) * 256], lhsT=bf_[:, 2 + g, cols],
                                                                                        rhs=S[c][:, g * 256:(g + 1) * 256], start=True, stop=True),
                             r=[bf_, S[c]], w=[pi], join=(g > 0))
                    psI.append(pi)
                m1, m2, m3, yb = m_
                for d, mm in ((0, m1), (1, m2)):
                    P.dve(lambda e, d=d, mm=mm, pi=psI[d], ft=ft, tc=tc: e.tensor_tensor(
                        out=mm[:].rearrange("p (h q) -> p h q", q=64), in0=pi[:].rearrange("p (h q) -> p h q", q=64),
                        in1=bc(ft[:, tc, 96 + 8 * d:104 + 8 * d].unsqueeze(2), [128, 8, 64]), op=ALU.mult), r=[psI[d], ft], w=[mm])
                P.pool(lambda e, xk=xk, tc=tc: e.tensor_tensor(out=m3[:].rearrange("p (h q) -> p h q", q=64), in0=xk[:, tc, :].rearrange("p (h q) -> p h q", q=64),
                                                              in1=bc(DSK.unsqueeze(2), [128, 8, 64]), op=ALU.mult), r=[xk, sm], w=[m3])
                P.pool(lambda e: e.tensor_tensor(out=m1[:], in0=m1[:], in1=m2[:], op=ALU.add), r=[m1, m2], w=[m1])
                P.pool(lambda e: e.tensor_tensor(out=m1[:], in0=m1[:], in1=m3[:], op=ALU.add), r=[m1, m3], w=[m1])
                P.dve(lambda e, psY=psY: e.tensor_tensor(out=yb[:], in0=psY[:], in1=m1[:], op=ALU.add), r=[psY, m1], w=[yb])
                if c == getattr(self, "dbgc", 0):
                    self.dump("D_SFX", SF[c], SF[c][:], [512], BF16)
                    self.dump("D_SBX", SB[c], SB[c][:], [512], BF16)
                    self.dump("D_FT", ft, ft[:, tc, :], [128], F32)
                    self.dump("D_Y", yb, yb[:], [512], F32)
                    self.dump("D_G", Gs[(0, 0)], Gs[(0, 0)][:], [512], BF16)
                    self.dump("D_M1", m1, m1[:], [512], F32)
                P.act(lambda e, zt=zt, tc=tc: e.activation(out=szb[:], in_=zt[:, tc, :], func=AF.Silu), r=[zt], w=[szb])
                P.pool(lambda e: e.tensor_tensor(out=yzb[:], in0=yb[:], in1=szb[:], op=ALU.mult), r=[yb, szb], w=[yzb])
                P.act(lambda e: e.activation(out=junk[:], in_=yzb[:], func=AF.Square, accum_out=ssb[:, 0:1]), r=[yzb], w=[junk, ssb])
                P.act(lambda e: e.activation(out=ssb[:, 1:2], in_=ssb[:, 0:1], func=AF.Sqrt, bias=self.epsb[:, 0:1], scale=1.0 / 512.0), r=[ssb], w=[ssb])
                P.dve(lambda e: e.reciprocal(out=ssb[:, 2:3], in_=ssb[:, 1:2]), r=[ssb], w=[ssb])
                P.dve(lambda e: e.scalar_tensor_tensor(out=otk[:], in0=yzb[:], scalar=ssb[:, 2:3], in1=NW, op0=ALU.mult, op1=ALU.mult),
                      r=[yzb, ssb, sm], w=[otk])
                self.transpose_to([otk[:, c4 * 128:(c4 + 1) * 128] for c4 in range(4)], oTs[:, :, cols], [otk], oTs, join=(tc > 0))
            P.dma(self.OS[:, t0:t0 + 512].rearrange("(c p) t -> p c t", p=128), oTs[:], r=[oTs])
        P.barrier()
        P.off = mark_rs
        qt_ = [P.alloc("qt%d" % i, [2, 512], BF16) for i in range(2)]
        qr = P.alloc("qrm", [4, 512], BF16)
        P.pool(lambda e: e.memset(qr[:], 0.0), w=[qr])
        krt = [P.alloc("krt%d" % i, [2, 512], BF16) for i in range(2)]
        cs_ = [P.alloc("cs%d" % i, [512], F32) for i in range(2)]
        sn_ = [P.alloc("sn%d" % i, [512], F32) for i in range(2)]
        t1 = P.alloc("t1", [512], F32)
        t2 = P.alloc("t2", [512], F32)
        vtok = [P.alloc("vtok%d" % i, [4, 512], BF16) for i in range(2)]
        rgt = [P.alloc("rgt%d" % i, [4, 512], BF16) for i in range(2)]
        qfb = P.alloc("qfb", [4, 512], BF16)
        Gr = [P.alloc("Gr%d" % i, [512], BF16) for i in range(2)]
        sqr = P.alloc("sqr", [512], F32)
        onr = P.alloc("onr", [512], F32)
        sgr = P.alloc("sgr", [512], F32)
        otr = P.alloc("otr", [512], BF16)
        oTr = [P.alloc("oTr%d" % i, [4, 512], BF16) for i in range(2)]
        ssr = P.alloc("ssr2", [8], F32)
        SCT = P.alloc("SCT", [4, 2, 128], F32)
        P.dve(lambda e: e.tensor_tensor(out=SCT[:], in0=bc(lg_hd.unsqueeze(3), [128, 4, 2, 128]),
                                        in1=bc(cf[:, K_EI:K_EI + 256].rearrange("p (d i) -> p d i", d=2).unsqueeze(1), [128, 4, 2, 128]), op=ALU.mult),
              r=[lgv], w=[SCT])
        P.act(lambda e: e.activation(out=SCT[:], in_=SCT[:], func=AF.Exp), r=[SCT], w=[SCT])
        DCOMB = P.alloc("DCOMB", [4, 128], F32)
        dtmp = P.alloc("dtmp", [4, 128], F32)
        P.dve(lambda e: e.tensor_tensor(out=DCOMB[:], in0=bc(cf[:, K_M1:K_M1 + 128].unsqueeze(1), [128, 4, 128]),
                                        in1=bc(lgv[:, 0:4].unsqueeze(2), [128, 4, 128]), op=ALU.mult), r=[lgv], w=[DCOMB])
        P.dve(lambda e: e.tensor_tensor(out=dtmp[:], in0=bc(cf[:, K_M2:K_M2 + 128].unsqueeze(1), [128, 4, 128]),
                                        in1=bc(lgv[:, 4:8].unsqueeze(2), [128, 4, 128]), op=ALU.mult), r=[lgv], w=[dtmp])
        P.pool(lambda e: e.tensor_tensor(out=DCOMB[:], in0=DCOMB[:], in1=dtmp[:], op=ALU.add), r=[DCOMB, dtmp], w=[DCOMB])
        P.act(lambda e: e.activation(out=DCOMB[:], in_=DCOMB[:], func=AF.Exp), r=[DCOMB], w=[DCOMB])
        def loads2r(t):
            t0 = t * 512
            qtb, krb, csb, snb, vt, rg = qt_[t % 2], krt[t % 2], cs_[t % 2], sn_[t % 2], vtok[t % 2], rgt[t % 2]
            P.dma(qtb[:], self.RQ[:, t0:t0 + 512].rearrange("(c p) t -> p c t", p=128), w=[qtb])
            P.dma(krb[:], self.RKR[:, t0:t0 + 512].rearrange("(c p) t -> p c t", p=128), w=[krb])
            P.dma(csb[:], self.rot[0, :, t0:t0 + 512], w=[csb])
            P.dma(snb[:], self.rot[1, :, t0:t0 + 512], w=[snb])
            P.dma(vt[:], self.RVT[t0:t0 + 512, :].rearrange("(tc p) c -> p tc c", p=128), w=[vt])
            P.dma(rg[:], self.RGT[t0:t0 + 512, :].rearrange("(tc p) c -> p tc c", p=128), w=[rg])

        loads2r(0)
        for t in range(NT):
            t0 = t * 512
            qtb, krb, csb, snb, vt, rg = qt_[t % 2], krt[t % 2], cs_[t % 2], sn_[t % 2], vtok[t % 2], rgt[t % 2]
            if t + 1 < NT:
                loads2r(t + 1)
            oTrr = oTr[t % 2]
            self.rotary(qtb, qr, csb, snb, 1.0, t1, t2, split=True)
            for h in (range(4) if not getattr(self, "dbg_noqfb", False) else []):
                hp, o = h // 2, (h % 2) * 64
                for d in range(2):
                    P.dve(lambda e, h=h, hp=hp, o=o, d=d: e.tensor_tensor(
                        out=qfb[d * 64:(d + 1) * 64, h, :].rearrange("p (c i) -> p c i", i=128), in0=qr[o:o + 64, h, :].rearrange("p (c i) -> p c i", i=128),
                        in1=bc(SCT[o:o + 64, h, d, :].unsqueeze(1), [64, 4, 128]), op=ALU.mult), r=[qr, SCT], w=[qfb], join=not (h == 0 and d == 0))
            RST = getattr(self, "ret_stop", 9)
            for tc in (range(4) if RST >= 1 else []):
                c = 4 * t + tc
                cols = slice(tc * 128, (tc + 1) * 128)
                pss = P.psum()
                for h in range(4):
                    hp, o = h // 2, (h % 2) * 64
                    P.pe(lambda e, h=h, hp=hp, o=o, pss=pss, krb=krb, cols=cols: e.matmul(pss[:, h * 128:(h + 1) * 128], lhsT=krb[:, hp, cols],
                                                                                        rhs=qr[:, h, cols], start=True, stop=True),
                         r=[krb, qr], w=[pss], join=(h > 0))
                if RST < 2:
                    continue
                G = Gr[tc % 2]
                P.dve(lambda e, G=G, pss=pss: e.tensor_tensor(out=G[:], in0=pss[:], in1=DCOMB[:].rearrange("p a b -> p (a b)"), op=ALU.mult),
                      r=[pss, DCOMB], w=[G])
                if RST < 3:
                    continue
                psY = P.psum()
                for h in range(4):
                    P.pe(lambda e, h=h, G=G, psY=psY, vt=vt, tc=tc: e.matmul(psY[:, h * 128:(h + 1) * 128], lhsT=G[:, h * 128:(h + 1) * 128],
                                                                           rhs=vt[:, tc, h * 128:(h + 1) * 128], start=True, stop=False),
                         r=[G, vt], w=[psY], join=(h > 0))
                    P.pe(lambda e, h=h, psY=psY, c=c, cols=cols: e.matmul(psY[:, h * 128:(h + 1) * 128], lhsT=qfb[:, h, cols],
                                                                        rhs=RSm[c][:, h * 128:(h + 1) * 128], start=False, stop=True),
                         r=[qfb, RSf[c], RSb[c]], w=[psY], join=True)
                if RST < 4:
                    continue
                for h in range(4):
                    P.act(lambda e, psY=psY, h=h: e.activation(out=sqr[:, h * 128:(h + 1) * 128], in_=psY[:, h * 128:(h + 1) * 128], func=AF.Square,
                                                              accum_out=ssr[:, 4 + h:5 + h]), r=[psY], w=[sqr, ssr], join=(h > 0))
                P.act(lambda e: e.activation(out=ssr[:, 4:8], in_=ssr[:, 4:8], func=AF.Sqrt, bias=self.epsb[:, 0:1], scale=1.0 / 128.0), r=[ssr], w=[ssr])
                P.dve(lambda e: e.reciprocal(out=ssr[:, 4:8], in_=ssr[:, 4:8]), r=[ssr], w=[ssr])
                P.dve(lambda e, psY=psY: e.tensor_tensor(out=onr[:].rearrange("p (h q) -> p h q", q=128), in0=psY[:].rearrange("p (h q) -> p h q", q=128),
                                                        in1=bc(ssr[:, 4:8].unsqueeze(2), [128, 4, 128]), op=ALU.mult), r=[psY, ssr], w=[onr])
                if RST < 5:
                    continue
                P.act(lambda e, rg=rg, tc=tc: e.activation(out=sgr[:], in_=rg[:, tc, :], func=AF.Silu), r=[rg], w=[sgr])
                P.pool(lambda e: e.tensor_tensor(out=otr[:], in0=onr[:], in1=sgr[:], op=ALU.mult), r=[onr, sgr], w=[otr])
                self.transpose_to([otr[:, c4 * 128:(c4 + 1) * 128] for c4 in range(4)], oTrr[:, :, cols], [otr], oTrr, join=(tc > 0))
            if RST >= 5:
                P.dma(self.OR[:, t0:t0 + 512].rearrange("(c p) t -> p c t", p=128), oTrr[:], r=[oTrr])


    def ph_na(self):
        P = self.P
        l, L, NT = self.l, self.L, self.NT
        rows = L // 64
        cf = self.cf
        rvc = P.alloc("rvc", [NT, 8, 8], F32)
        P.dma(rvc[:].rearrange("p a b c -> p (a b c)"), self.rvc[L][:, :], w=[rvc])
        ZR = P.alloc("ZR", [8, 22, 64], BF16)
        ZRI = P.alloc("ZRI", [8, 22, 64], BF16) if NT > 2 else None
        mark = P.off
        TT = P.alloc("TT", [8, 15, 64], F32)
        E = P.alloc("E", [8, 17, 64], BF16)
        P.dma(TT[0:64, :, :, :], bass.AP(self.rpbp.tensor, self.rpbp[l].offset, [[1, 64], [15 * 127, 8], [127, 15], [1, 64]]), w=[TT])
        P.pool(lambda e: e.memset(E[0:64, :, :, :], 0.0), w=[E])
        P.pool(lambda e: e.memset(ZR[:], 0.0), w=[ZR])
        P.act(lambda e: e.activation(out=E[0:64, :, 1:16, :], in_=TT[0:64, :, :, :], func=AF.Exp), r=[TT], w=[E])
        for h in range(8):
            for ub in range(2):
                ps = P.psum()
                for k in range(8):
                    u = 3 + ub * 8 + k
                    s = 10 - u + 8
                    P.pe(lambda e, h=h, k=k, s=s, ps=ps: e.matmul(ps[:, k * 64:(k + 1) * 64], lhsT=E[0:64, h, s:s + 2, :].rearrange("p a b -> p (a b)"),
                                                                 rhs=self.jb[0:64, :], start=True, stop=True), r=[E], w=[ps], join=(k > 0))
                P.dve(lambda e, h=h, ub=ub, ps=ps: e.tensor_tensor(out=ZR[:, h, 3 + ub * 8:11 + ub * 8, :], in0=ps[:].rearrange("p (u q) -> p u q", q=64),
                                                                   in1=bc(cf[:, K_CV:K_CV + 64].unsqueeze(1), [128, 8, 64]), op=ALU.mult),
                      r=[ps], w=[ZR], join=True)
        if ZRI is not None:
            for h in range(8):
                P.dve(lambda e, h=h: e.tensor_tensor(out=ZRI[:, h, :, :], in0=ZR[:, h, :, :], in1=bc(cf[:, K_RMK:K_RMK + 22].unsqueeze(2), [128, 22, 64]), op=ALU.mult),
                      r=[ZR], w=[ZRI], join=(h > 0))
        P.barrier()
        P.off = mark
        nq = [P.alloc("nq%d" % i, [4, 512], BF16) for i in range(2)]
        nk = [P.alloc("nk%d" % i, [4, 1024], BF16) for i in range(2)]
        nv = [P.alloc("nv%d" % i, [8, 512], BF16) for i in range(2)]
        va = [P.alloc("va%d" % i, [8, 8, 128], BF16) for i in range(2)]
        eb = [P.alloc("eb%d" % i, [512], BF16) for i in range(3)]
        p1 = [P.alloc("p1_%d" % i, [512], BF16) for i in range(3)]
        p2 = [P.alloc("p2_%d" % i, [512], BF16) for i in range(3)]
        rc = [P.alloc("rc%d" % i, [512], F32) for i in range(2)]
        ona = [P.alloc("ona%d" % i, [4, 512], BF16) for i in range(2)]
        for b in va:
            P.pool(lambda e, b=b: e.memset(b[:], 1.0), w=[b])
        def geom(t):
            R0 = 8 * t
            rs0 = min(max(R0 - 4, 0), rows - 8)
            rs7 = min(max(R0 + 7 - 4, 0), rows - 8)
            kt0, kt1 = rs0 // 2, (rs7 + 7) // 2
            return R0, kt0, kt1, kt1 - kt0 + 1

        def loads(t):
            t0 = t * 512
            R0, kt0, kt1, nkt = geom(t)
            q, k_, v, vaug = nq[t % 2], nk[t % 2], nv[t % 2], va[t % 2]
            P.dma(q[:], self.NQ[:, t0:t0 + 512].rearrange("(c p) t -> p c t", p=128), w=[q])
            P.dma(k_[:, :, 0:nkt * 128], self.NK[:, kt0 * 128:(kt1 + 1) * 128].rearrange("(c p) t -> p c t", p=128), w=[k_])
            P.dma(v[:, 0:nkt, :], self.NVT[kt0 * 128:(kt1 + 1) * 128, :].rearrange("(j p) c -> p j c", p=128), w=[v])
            P.pool(lambda e, v=v, vaug=vaug, nkt=nkt: e.tensor_copy(out=vaug[:, 0:nkt, :, 0:64], in_=v[:, 0:nkt, :].rearrange("p j (h d) -> p j h d", d=64)),
                   r=[v], w=[vaug])

        LA = 2
        loads(0)
        for t in range(NT):
            t0 = t * 512
            R0, kt0, kt1, nkt = geom(t)
            interior = (t > 0 and t < NT - 1)
            q, k_, v, vaug, on = nq[t % 2], nk[t % 2], nv[t % 2], va[t % 2], ona[t % 2]
            if t + 1 < NT:
                loads(t + 1)
            items = [(h, j) for h in range(8) for j in range(nkt)]
            pbuf = {}

            def front(i):
                h, j = items[i]
                hp, o = h // 2, (h % 2) * 64
                u0 = 10 - (2 * (kt0 + j) - R0)
                pS = P.psum(2, 8)
                P.pe(lambda e, pS=pS, k_=k_, q=q, o=o, hp=hp, j=j: e.matmul(pS[:], lhsT=k_[o:o + 64, hp, j * 128:(j + 1) * 128], rhs=q[o:o + 64, hp, :],
                                                                          start=True, stop=True), r=[k_, q], w=[pS])
                e_, p1_, p2_ = eb[i % 3], p1[i % 3], p2[i % 3]
                P.act(lambda e, pS=pS, e_=e_: e.activation(out=e_[:], in_=pS[:], func=AF.Exp, scale=0.125), r=[pS], w=[e_])
                Z = ZRI if interior else ZR
                P.dve(lambda e, e_=e_, p1_=p1_, h=h, u0=u0, Z=Z: e.tensor_tensor(out=p1_[:], in0=e_[:], in1=Z[:, h, u0:u0 + 8, :].rearrange("p a b -> p (a b)"),
                                                                              op=ALU.mult), r=[e_, Z], w=[p1_])
                if interior:
                    pbuf[i] = p1_
                else:
                    P.pool(lambda e, p1_=p1_, p2_=p2_, t=t, j=j: e.tensor_tensor(out=p2_[:].rearrange("p (b q) -> p b q", q=64), in0=p1_[:].rearrange("p (b q) -> p b q", q=64),
                                                                               in1=bc(rvc[:, t, j, :].unsqueeze(2), [128, 8, 64]), op=ALU.mult), r=[p1_, rvc], w=[p2_])
                    pbuf[i] = p2_

            for i in range(min(LA, len(items))):
                front(i)
            psO = None
            for i, (h, j) in enumerate(items):
                hp, o = h // 2, (h % 2) * 64
                if i + LA < len(items):
                    front(i + LA)
                if j == 0:
                    psO = P.psum(0, 2)
                pb = pbuf.pop(i)
                P.pe(lambda e, psO=psO, vaug=vaug, pb=pb, j=j, h=h, nkt=nkt: e.matmul(psO[:], lhsT=vaug[:, j, h, :], rhs=pb[:], start=(j == 0), stop=(j == nkt - 1)),
                     r=[vaug, pb], w=[psO], join=(j > 0))
                if j == nkt - 1:
                    r_ = rc[h % 2]
                    P.dve(lambda e, psO=psO, r_=r_: e.reciprocal(out=r_[0:64, :], in_=psO[64:128, :]), r=[psO], w=[r_])
                    P.dve(lambda e, psO=psO, r_=r_, on=on, o=o, hp=hp: e.tensor_tensor(out=on[o:o + 64, hp, :], in0=psO[0:64, :], in1=r_[0:64, :], op=ALU.mult),
                          r=[psO, r_], w=[on], join=(h > 0))
            P.dma(self.ON[:, t0:t0 + 512].rearrange("(c p) t -> p c t", p=128), on[:], r=[on])

    def ph_merge(self):
        P = self.P
        l, NT = self.l, self.NT
        sm = self.load_small(l)
        xt = [P.alloc("xt%d" % i, [KC, 512], F32) for i in range(2)]
        ob = [[P.alloc("o%d_%d" % (i, k), [4, 512], BF16) for k in range(2)] for i in range(3)]
        gt = [P.alloc("gt%d" % i, [8, 512], BF16) for i in range(2)]
        wb = [P.alloc("wb%d" % i, [4, 1024], BF16) for i in range(2)]
        wo = [P.alloc("wo%d" % i, [KC, 512], BF16) for i in range(2)]
        gs = [P.alloc("gs%d" % i, [512], F32) for i in range(2)]
        tmp = [P.alloc("tmp%d" % i, [512], F32) for i in range(2)]
        MG = P.alloc("MG", [8, 512], F32)
        MGb = P.alloc("MGb", [8, 512], BF16)
        xo = [P.alloc("xo%d" % i, [KC, 512], F32) for i in range(2)]
        srcs = [self.OS, self.OR, self.ON]

        def loadx(t):
            t0 = t * 512
            P.dma(xt[t % 2][:], self.X[:, t0:t0 + 512].rearrange("(c p) t -> p c t", p=128), w=[xt[t % 2]])
            for i in range(3):
                P.dma(ob[i][t % 2][:], srcs[i][:, t0:t0 + 512].rearrange("(c p) t -> p c t", p=128), w=[ob[i][t % 2]])

        seq = [(t, j) for t in range(NT) for j in range(5)]
        wq = {}
        cnt = [0, 0]

        def ensure(k):
            if k < len(seq) and k not in wq:
                t, j = seq[k]
                t0 = t * 512
                if j < 3:
                    g, w = gt[cnt[0] % 2], wb[cnt[0] % 2]
                    cnt[0] += 1
                    P.dma(g[:], self.GT[j * 1024:(j + 1) * 1024, t0:t0 + 512].rearrange("(c p) t -> p c t", p=128), w=[g])
                    P.dma(w[:], self.wb_br[l, j].rearrange("(c p) n -> p c n", p=128), w=[w])
                    wq[k] = (g, w)
                else:
                    w = wo[cnt[1] % 2]
                    cnt[1] += 1
                    hf = j - 3
                    P.dma(w[:], self.wb_out[l][:, hf * 512:(hf + 1) * 512].rearrange("(c p) n -> p c n", p=128), w=[w])
                    wq[k] = (None, w)

        loadx(0)
        ensure(0)
        for k, (t, j) in enumerate(seq):
            t0 = t * 512
            a, xn = xt[t % 2], xo[t % 2]
            if j == 0 and t + 1 < NT:
                loadx(t + 1)
            ensure(k + 1)
            g, w = wq.pop(k)
            if j < 3:
                i = j
                o = ob[i][t % 2]
                for m in range(8):
                    ps = P.psum()
                    for c in range(4):
                        P.pe(lambda e, c=c, m=m, ps=ps, w=w, o=o: e.matmul(ps[:], lhsT=w[:, c, m * 128:(m + 1) * 128], rhs=o[:, c, :], start=(c == 0), stop=(c == 3)),
                             r=[w, o], w=[ps], join=(c > 0))
                    s_ = gs[m % 2]
                    P.act(lambda e, g=g, m=m, i=i, s_=s_: e.activation(out=s_[:], in_=g[:, m, :], func=AF.Sigmoid, bias=sm[:, S_GB + i * 8 + m:S_GB + i * 8 + m + 1]),
                          r=[g, sm], w=[s_])
                    if i == 0:
                        P.dve(lambda e, ps=ps, s_=s_, m=m: e.tensor_tensor(out=MG[:, m, :], in0=ps[:], in1=s_[:], op=ALU.mult), r=[ps, s_], w=[MG], join=(m > 0))
                    else:
                        tm = tmp[m % 2]
                        P.dve(lambda e, ps=ps, s_=s_, tm=tm: e.tensor_tensor(out=tm[:], in0=ps[:], in1=s_[:], op=ALU.mult), r=[ps, s_], w=[tm])
                        if i == 1:
                            P.pool(lambda e, tm=tm, m=m: e.tensor_tensor(out=MG[:, m, :], in0=MG[:, m, :], in1=tm[:], op=ALU.add), r=[tm, MG], w=[MG], join=(m > 0))
                        else:
                            P.pool(lambda e, tm=tm, m=m: e.tensor_tensor(out=MGb[:, m, :], in0=MG[:, m, :], in1=tm[:], op=ALU.add), r=[tm, MG], w=[MGb], join=(m > 0))
            else:
                hf = j - 3
                for mm in range(4):
                    m = hf * 4 + mm
                    ps = P.psum()
                    for c in range(KC):
                        P.pe(lambda e, c=c, mm=mm, ps=ps, w=w: e.matmul(ps[:], lhsT=w[:, c, mm * 128:(mm + 1) * 128], rhs=MGb[:, c, :], start=(c == 0), stop=(c == KC - 1)),
                             r=[w, MGb], w=[ps], join=(c > 0))
                    P.dve(lambda e, ps=ps, m=m, a=a, xn=xn: e.tensor_tensor(out=xn[:, m, :], in0=ps[:], in1=a[:, m, :], op=ALU.add), r=[ps, a], w=[xn], join=(m > 0))
                if hf == 1:
                    P.dma(self.XM[:, t0:t0 + 512].rearrange("(c p) t -> p c t", p=128), xn[:], r=[xn])

    def ph_ffn(self):
        P = self.P
        l, L = self.l, self.L
        sm = self.load_small(l)
        ntile = (L + 509) // 510
        TF = (L + ntile - 1) // ntile
        xt = [P.alloc("xt%d" % i, [KC, 512], F32) for i in range(2)]
        sq = P.alloc("sq", [KC, 512], BF16)
        rs = P.alloc("rs", [512], F32)
        hb = P.alloc("h", [KC, 512], BF16)
        wu = [P.alloc("wu%d" % i, [KC, 2, 512], BF16) for i in range(2)]
        wd = [P.alloc("wd%d" % i, [22, 256], BF16) for i in range(2)]
        av = [P.alloc("av%d" % i, [512], F32) for i in range(2)]
        ag = [P.alloc("ag%d" % i, [512], F32) for i in range(2)]
        sg = [P.alloc("sg%d" % i, [512], F32) for i in range(2)]
        actb = P.alloc("actb", [22, 512], BF16)
        xo = [P.alloc("xo%d" % i, [KC, 512], F32) for i in range(2)]
        fw = sm[:, S_FW:S_FW + 132].rearrange("p (c j) -> p c j", j=3)
        fb = sm[:, S_FB:S_FB + 44]
        def geom(t):
            t0 = t * TF
            n = min(TF, L - t0)
            return t0, n, max(t0 - 1, 0), min(t0 + n + 1, L)

        def loadx(t):
            t0, n, lo, hi = geom(t)
            a = xt[t % 2]
            if lo > t0 - 1:
                P.pool(lambda e, a=a: e.memset(a[:, :, 0:1], 0.0), w=[a])
            if hi < t0 + n + 1:
                P.pool(lambda e, a=a, n=n: e.memset(a[:, :, n + 1:n + 2], 0.0), w=[a])
            P.dma(a[:, :, lo - (t0 - 1):hi - (t0 - 1)], self.XM[:, lo:hi].rearrange("(c p) t -> p c t", p=128), w=[a], join=True)

        steps = [("u", mb_) for mb_ in range(0, 22, 4)] + [("d", mp) for mp in range(4)]
        seq = [(t, j) for t in range(ntile) for j in range(len(steps))]
        wq = {}
        cnt = [0, 0]

        def ensure(k):
            if k < len(seq) and k not in wq:
                kind, idx = steps[seq[k][1]]
                if kind == "u":
                    nm = min(4, 22 - idx)
                    w = wu[cnt[0] % 2]
                    cnt[0] += 1
                    for part in range(2):
                        c0 = part * DFF + idx * 128
                        P.dma(w[:, :, part, 0:nm * 128], self.wb_up[l][:, c0:c0 + nm * 128].rearrange("(c p) n -> p c n", p=128), w=[w], join=(part > 0))
                else:
                    w = wd[cnt[1] % 2]
                    cnt[1] += 1
                    P.dma(w[:], self.wb_dn[l][:, idx * 256:(idx + 1) * 256].rearrange("(c p) n -> p c n", p=128), w=[w])
                wq[k] = w

        loadx(0)
        ensure(0)
        for k, (t, j) in enumerate(seq):
            kind, idx = steps[j]
            t0, n, lo, hi = geom(t)
            a, xn = xt[t % 2], xo[t % 2]
            if j == 0:
                if t + 1 < ntile:
                    loadx(t + 1)
                self.norm(a, n + 2, sm[:, S_GFFN:S_GFFN + 8], hb, sq, rs)
            ensure(k + 1)
            w = wq.pop(k)
            if kind == "u":
                mb_ = idx
                nm = min(4, 22 - mb_)
                for mi in range(nm):
                    m = mb_ + mi
                    res = []
                    for part, accb in ((0, av), (1, ag)):
                        ch = part * 22 + m
                        ps = P.psum()
                        for c in range(KC):
                            P.pe(lambda e, c=c, part=part, mi=mi, ps=ps, w=w, n=n: e.matmul(ps[:, 0:n + 2], lhsT=w[:, c, part, mi * 128:(mi + 1) * 128], rhs=hb[:, c, 0:n + 2],
                                                                                         start=(c == 0), stop=(c == KC - 1)), r=[w, hb], w=[ps], join=(c > 0))
                        ac = accb[m % 2]
                        P.act(lambda e, ps=ps, ac=ac, ch=ch, n=n: e.activation(out=ac[:, 0:n], in_=ps[:, 0:n], func=AF.Identity, bias=fb[:, ch:ch + 1], scale=fw[:, ch, 0:1]),
                              r=[ps, sm], w=[ac])
                        for jj in (1, 2):
                            P.dve(lambda e, ps=ps, ac=ac, ch=ch, jj=jj, n=n: e.scalar_tensor_tensor(out=ac[:, 0:n], in0=ps[:, jj:jj + n], scalar=fw[:, ch, jj:jj + 1], in1=ac[:, 0:n],
                                                                                                 op0=ALU.mult, op1=ALU.add), r=[ps, ac, sm], w=[ac])
                        res.append(ac)
                    s_ = sg[m % 2]
                    P.act(lambda e, s_=s_, g_=res[1], n=n: e.activation(out=s_[:, 0:n], in_=g_[:, 0:n], func=AF.Silu), r=[res[1]], w=[s_])
                    P.pool(lambda e, s_=s_, v_=res[0], m=m, n=n: e.tensor_tensor(out=actb[:, m, 0:n], in0=s_[:, 0:n], in1=v_[:, 0:n], op=ALU.mult),
                           r=[s_, res[0]], w=[actb], join=(m > 0))
            else:
                mp = idx
                for mi in range(2):
                    m = mp * 2 + mi
                    ps = P.psum()
                    for c in range(22):
                        P.pe(lambda e, c=c, mi=mi, ps=ps, w=w, n=n: e.matmul(ps[:, 0:n], lhsT=w[:, c, mi * 128:(mi + 1) * 128], rhs=actb[:, c, 0:n], start=(c == 0), stop=(c == 21)),
                             r=[w, actb], w=[ps], join=(c > 0))
                    P.dve(lambda e, ps=ps, m=m, a=a, xn=xn, n=n: e.tensor_tensor(out=xn[:, m, 0:n], in0=ps[:, 0:n], in1=a[:, m, 1:n + 1], op=ALU.add), r=[ps, a], w=[xn], join=(m > 0))
                if mp == 3:
                    P.dma(self.X[:, t0:t0 + n].rearrange("(c p) t -> p c t", p=128), xn[:, :, 0:n], r=[xn])


def make_consts(Lmax):
    c = np.zeros((128, NCON), np.float32)
    c[:, K_ID:K_ID + 128] = np.eye(128, dtype=np.float32)
    j = np.arange(128)[:, None]
    i = np.arange(128)[None, :]
    mnf = np.where(i >= j, 0.0, -30000.0).astype(np.float32)
    mnb = np.where(j > i, 0.0, -30000.0).astype(np.float32)
    c[:, K_MNF:K_MNF + 512] = np.tile(mnf, (1, 4))
    c[:, K_MNB:K_MNB + 512] = np.tile(mnb, (1, 4))
    rm = np.ones(512, np.float32)
    rm[::128] = 0.0
    c[:, K_RM:K_RM + 512] = rm[None, :]
    sel = np.zeros((128, 16, 128), np.float32)
    for q in range(16):
        sel[q, q, :] = 1.0 if q < 8 else -1.0
    c[:, K_SEL:K_SEL + 2048] = sel.reshape(128, 2048)
    c[0:8, K_MF] = 1.0
    c[8:16, K_MF + 1] = 1.0
    c[8:16, K_MF + 2] = -1.0
    prot = np.zeros((128, 128), np.float32)
    for m in range(128):
        if m % 64 < 32:
            prot[m + 32, m] = -1.0
        else:
            prot[m - 32, m] = 1.0
    c[:, K_PROT:K_PROT + 128] = prot
    c[:, K_EJ] = 127 - np.arange(128)
    c[:, K_EJ + 1] = np.arange(128)
    c[:, K_EI:K_EI + 128] = (np.arange(128) + 1)[None, :]
    c[:, K_EI + 128:K_EI + 256] = (128 - np.arange(128))[None, :]
    c[:, K_M1:K_M1 + 128] = np.maximum(i - j, 0)
    c[:, K_M2:K_M2 + 128] = np.maximum(j - i, 0)
    c[0:64, K_J:K_J + 64] = np.eye(64, dtype=np.float32)[::-1]
    qc = np.arange(64)[None, :]
    kc = np.arange(64)[:, None]
    cst = np.clip(qc - 8, 0, 48)
    cv = ((kc >= cst) & (kc < cst + 16)).astype(np.float32)
    c[:, K_CV:K_CV + 64] = np.tile(cv, (2, 1))
    for a_ in range(2):
        for u in range(22):
            dr = a_ + 10 - u
            c[a_ * 64:(a_ + 1) * 64, K_RMK + u] = 1.0 if -4 <= dr <= 3 else 0.0
    half = 32
    inv = (1.0 / (10000.0 ** (np.arange(half, dtype=np.float32) / half))).astype(np.float32)
    pos = np.arange(Lmax, dtype=np.float32)
    ang = pos[None, :] * inv[:, None]
    f = (np.arange(128) % 64) % 32
    rot = np.stack([np.cos(ang)[f], np.sin(ang)[f]]).astype(np.float32)
    return c, rot


def make_rvc(L):
    rows = L // 64
    NT = L // 512
    r = np.zeros((128, NT, 8, 8), np.float32)
    for t in range(NT):
        R0 = 8 * t
        rs0 = min(max(R0 - 4, 0), rows - 8)
        kt0 = rs0 // 2
        for j in range(8):
            for a in range(2):
                kr = 2 * (kt0 + j) + a
                for b in range(8):
                    qr = R0 + b
                    rs = min(max(qr - 4, 0), rows - 8)
                    if rs <= kr < rs + 8:
                        r[a * 64:(a + 1) * 64, t, j, b] = 1.0
    return r.reshape(128, NT * 64)


def make_small(inp, depth):
    sm = np.zeros((depth + 1, 128, NSM), np.float32)

    def pc(v):
        return np.asarray(v, np.float32).reshape(-1, 128).T

    for l in range(depth):
        sm[l, :, S_GMIX:S_GMIX + 8] = pc(inp["norm_mix"][l])
        sm[l, :, S_GFFN:S_GFFN + 8] = pc(inp["norm_ffn"][l])
        sm[l, :, S_GB:S_GB + 24] = pc(inp["gate_bias"][l])
        cw = np.asarray(inp["ssd_conv_w"][l], np.float32)
        sm[l, :, S_CW:S_CW + 40] = cw.reshape(5, 8, 128).transpose(2, 1, 0).reshape(128, 40)
        sm[l, :, S_CB:S_CB + 8] = pc(inp["ssd_conv_b"][l])
        fw = np.asarray(inp["ffn_conv_w"][l], np.float32)
        sm[l, :, S_FW:S_FW + 132] = fw.reshape(3, 44, 128).transpose(2, 1, 0).reshape(128, 132)
        sm[l, :, S_FB:S_FB + 44] = pc(inp["ffn_conv_b"][l])
        sm[l, :, S_DSK:S_DSK + 8] = np.asarray(inp["ssd_d"][l], np.float32)[None, :]
        sm[l, :, S_NW:S_NW + 512] = np.asarray(inp["ssd_norm"][l], np.float32)[None, :]
        sm[l, :, S_TH:S_TH + 8] = np.asarray(inp["ret_theta"][l], np.float32).reshape(8)[None, :]
        sm[l, 0:16, S_DTB] = np.asarray(inp["ssd_dt_bias"][l], np.float32).reshape(16)
        sm[l, 0:16, S_ALOG] = np.asarray(inp["ssd_a_log"][l], np.float32).reshape(16)
    sm[depth, :, S_GMIX:S_GMIX + 8] = pc(inp["norm_final"])
    return sm


def host_inputs(inp, depth, seq_lens):
    Lmax = max(seq_lens)
    c, rot = make_consts(Lmax)
    rp = np.zeros((depth, 8, 15, 127), np.float32)
    rp[:, :, :, 48:79] = np.asarray(inp["na_rpb"], np.float32)[:depth]
    common = {
        "w_in": np.ascontiguousarray(np.asarray(inp["w_in"], np.float32)[:depth]),
        "w_branch": np.ascontiguousarray(np.asarray(inp["w_branch"], np.float32)[:depth]),
        "w_out": np.ascontiguousarray(np.asarray(inp["w_out"], np.float32)[:depth]),
        "ffn_w_up": np.ascontiguousarray(np.asarray(inp["ffn_w_up"], np.float32)[:depth]),
        "ffn_w_down": np.ascontiguousarray(np.asarray(inp["ffn_w_down"], np.float32)[:depth]),
        "small": make_small(inp, depth),
        "consts": c,
        "rot": rot,
        "rpbp": rp,
    }
    for L in sorted(set(seq_lens)):
        common["rvc%d" % L] = make_rvc(L)
    return common


_CACHE = {}


def kernel(**inputs):
    xp = np.asarray(inputs["x_prompt"], np.float32)
    xs = np.asarray(inputs["x_sample"], np.float32)
    depth = np.asarray(inputs["w_in"]).shape[0]
    n = 8
    seq_lens = [xp.shape[1], xp.shape[1], xs.shape[1]]
    key = (tuple(seq_lens), depth)
    if key not in _CACHE:
        _CACHE[key] = Builder(seq_lens, depth).build()
    nc = _CACHE[key]
    common = host_inputs(inputs, depth, seq_lens)
    in_maps = []
    for c in range(n):
        m = dict(common)
        m["x0"] = np.ascontiguousarray(xp[2 * c])
        m["x1"] = np.ascontiguousarray(xp[2 * c + 1])
        m["x2"] = np.ascontiguousarray(xs[c])
        in_maps.append(m)
    res = run_bass_kernel_spmd(nc, in_maps, core_ids=list(range(n)))
    yp = np.empty_like(xp)
    ys = np.empty_like(xs)
    for c in range(n):
        r = res.results[c]
        yp[2 * c] = r["y0"]
        yp[2 * c + 1] = r["y1"]
        ys[c] = r["y2"]
    return (yp, ys)
```

```python
import math
from contextlib import ExitStack
import numpy as np
import concourse.bass as bass
import concourse.mybir as mybir
from concourse.bass_utils import run_bass_kernel_spmd

F32 = mybir.dt.float32
BF16 = mybir.dt.bfloat16
U8 = mybir.dt.uint8
AF = mybir.ActivationFunctionType
ALU = mybir.AluOpType
AX = mybir.AxisListType

DM = 1024
KC = 8
PROJ = 7696
DFF = 2816
EPS = 1e-6
ENG = ["pe", "act", "dve", "pool", "sp"]
NDS = 72
ARENA = 200 * 1024

C_Z, C_XBC, C_DT, C_RQ, C_RK, C_RV, C_RG, C_NQ, C_NK, C_NV, C_GATE = 0, 512, 1536, 1552, 1808, 2064, 2576, 3088, 3600, 4112, 4624

S_GMIX, S_GFFN, S_GB, S_CW, S_CB, S_FW, S_FB, S_DSK, S_NW, S_TH, S_DTB, S_ALOG = 0, 8, 16, 40, 80, 88, 220, 264, 272, 784, 792, 793
NSM = 800
K_ID = 0
K_MNF = 128
K_MNB = 640
K_RM = 1152
K_SEL = 1664
K_MF = 3712
K_PROT = 3716
K_EJ = 3844
K_EI = 3848
K_M1 = 4104
K_M2 = 4232
K_J = 4360
K_CV = 4424
K_RMK = 4488
NCON = 4512


class Op:
    __slots__ = ("eng", "fn", "deps", "needed", "value", "is_dma", "sem", "dval")


class Buf:
    def __init__(self, name, ap):
        self.name = name
        self.ap = ap
        self.w = {}
        self.r = {}
        self.dsem = None
        self.const = False

    def __getitem__(self, k):
        return self.ap[k]


class Prog:
    def __init__(self, nc, es):
        self.nc = nc
        self.ops = {e: [] for e in ENG}
        self.arena = es.enter_context(nc.sbuf_tensor("arena", [128, ARENA], U8))
        self.off = 0
        self.esem = {e: es.enter_context(nc.semaphore("s_" + e)) for e in ENG}
        self.dsems = [es.enter_context(nc.semaphore("d%d" % i)) for i in range(NDS)]
        self.dcount = [0] * NDS
        self.dnext = 0
        self.dlast = {}
        self.last_real = {}
        self.ps = []
        for i in range(8):
            t = es.enter_context(nc.psum_tensor("ps%d" % i, [128, 512], F32))
            self.ps.append(Buf("ps%d" % i, t[:]))
        self.psi = 0
        self.psk = {}
        self.evi = 0
        self.dummy = Buf("dummy", None)
        self.nops = 0

    def alloc(self, name, shape, dt):
        esz = 4 if dt == F32 else 2
        n = 1
        for s in shape:
            n *= s
        off = (self.off + 63) // 64 * 64
        assert off + n * esz <= ARENA, ("SBUF arena overflow", name, off, n * esz)
        ap = self.arena[:, off:off + n * esz].bitcast(dt)
        if len(shape) == 2:
            ap = ap.rearrange("p (a b) -> p a b", a=shape[0])
        elif len(shape) == 3:
            ap = ap.rearrange("p (a b c) -> p a b c", a=shape[0], b=shape[1])
        self.off = off + n * esz
        return Buf(name, ap)

    def psum(self, lo=0, hi=8):
        n = hi - lo
        k = self.psk.get((lo, hi), 0)
        self.psk[(lo, hi)] = k + 1
        return self.ps[lo + k % n]

    def op(self, eng, fn, r=(), w=(), join=False, is_dma=False, sem=None):
        o = Op()
        o.eng, o.fn, o.needed, o.value, o.is_dma, o.sem, o.dval = eng, fn, False, 0, is_dma, sem, 0
        deps = {}

        def add(d, raw):
            if d.is_dma or d.eng != eng or (raw and eng in ("act", "dve", "pool")):
                deps[id(d)] = d

        for b in r:
            for d in b.w.values():
                add(d, True)
        for b in w:
            if eng == "pe" and not join and b.w and not b.r and not is_dma:
                assert all(d.eng != "pe" for d in b.w.values()), ("PSUM bank overwritten before being read", b.name)
            for d in b.r.values():
                add(d, False)
            if not join or b.r:
                for d in b.w.values():
                    add(d, False)
        o.deps = list(deps.values())
        for d in o.deps:
            d.needed = True
        key = ("d", sem) if is_dma else eng
        for b in r:
            if not b.const:
                b.r[key] = o
        for b in w:
            if join and not b.r:
                b.w[key] = o
            else:
                b.w = {key: o}
            b.r = {}
        self.ops[eng].append(o)
        if not is_dma:
            self.last_real[eng] = o
        self.nops += 1
        return o

    def pe(self, fn, r=(), w=(), join=False):
        return self.op("pe", fn, r, w, join)

    def act(self, fn, r=(), w=(), join=False):
        return self.op("act", fn, r, w, join)

    def dve(self, fn, r=(), w=(), join=False):
        return self.op("dve", fn, r, w, join)

    def pool(self, fn, r=(), w=(), join=False):
        return self.op("pool", fn, r, w, join)

    def dma(self, out, in_, r=(), w=(), join=False, q="sp"):
        bl = list(w) + list(r)
        b = bl[0] if bl else self.dummy
        if b.dsem is None:
            b.dsem = self.dnext % NDS
            self.dnext += 1
        s = b.dsem
        o = self.op(q, lambda e: e.dma_start(out=out, in_=in_), r, w, join, is_dma=True, sem=s)
        self.dcount[s] += 16
        o.dval = self.dcount[s]
        self.dlast[s] = o
        return o

    def barrier(self):
        lasts = list(self.last_real.values()) + list(self.dlast.values())
        for e in ENG:
            o = Op()
            o.eng, o.fn, o.needed, o.value, o.is_dma, o.sem, o.dval = e, (lambda h: None), False, 0, False, None, 0
            o.deps = [d for d in lasts if d.is_dma or d.eng != e]
            for d in o.deps:
                d.needed = True
            self.ops[e].append(o)
        self.dlast = {}
        self.dummy = Buf("dummy", None)

    def evac(self, out, in_, r, w, join=False):
        self.evi += 1
        if self.evi % 2:
            return self.act(lambda e: e.activation(out=out, in_=in_, func=AF.Copy), r, w, join)
        return self.dve(lambda e: e.tensor_copy(out=out, in_=in_), r, w, join)

    def emit(self, block):
        for e in ENG:
            c = 0
            for o in self.ops[e]:
                if o.needed and not o.is_dma:
                    c += 1
                    o.value = c

        def run(e, h):
            known = {}
            for o in self.ops[e]:
                need = {}
                for d in o.deps:
                    if d.is_dma:
                        key, sem, val = ("d", d.sem), self.dsems[d.sem], d.dval
                    else:
                        key, sem, val = d.eng, self.esem[d.eng], d.value
                    if need.get(key, (None, 0))[1] < val:
                        need[key] = (sem, val)
                for key, (sem, val) in need.items():
                    if known.get(key, 0) < val:
                        h.wait_ge(sem, val)
                        known[key] = val
                inst = o.fn(h)
                if inst is None:
                    continue
                if o.is_dma:
                    inst.then_inc(self.dsems[o.sem], 16)
                elif o.needed:
                    inst.then_inc(self.esem[e], 1)

        @block.tensor
        def _(h):
            run("pe", h)

        @block.scalar
        def _(h):
            run("act", h)

        @block.vector
        def _(h):
            run("dve", h)

        @block.gpsimd
        def _(h):
            run("pool", h)

        @block.sync
        def _(h):
            run("sp", h)


def bc(ap, shape):
    return ap.broadcast_to(shape)


class Builder:
    def __init__(self, seq_lens, depth, debug=()):
        self.seq_lens = list(seq_lens)
        self.depth = depth
        self.debug = set(debug)
        self.run_layers = depth
        self._dumps = {}
        self.only = None
        self.Lmax = max(seq_lens)
        self.nc = bass.Bass("TRN2", target_bir_lowering=False)

    def dram(self, name, shape, dt, kind="Internal"):
        if name in self.debug:
            kind = "ExternalOutput"
        return self.nc.dram_tensor(name, shape, dt, kind=kind).ap()

    def build(self, phases=None):
        nc = self.nc
        D = self.depth
        Lm = self.Lmax
        I = "ExternalInput"
        self.x_in = [self.dram("x%d" % i, [L, DM], F32, I) for i, L in enumerate(self.seq_lens)]
        self.y_out = [self.dram("y%d" % i, [L, DM], F32, "ExternalOutput") for i, L in enumerate(self.seq_lens)]
        self.w_in = self.dram("w_in", [D, DM, PROJ], F32, I)
        self.w_br = self.dram("w_branch", [D, 3, 512, DM], F32, I)
        self.w_out = self.dram("w_out", [D, DM, DM], F32, I)
        self.w_up = self.dram("ffn_w_up", [D, DM, 2 * DFF], F32, I)
        self.w_dn = self.dram("ffn_w_down", [D, DFF, DM], F32, I)
        self.small = self.dram("small", [D + 1, 128, NSM], F32, I)
        self.consts = self.dram("consts", [128, NCON], F32, I)
        self.rot = self.dram("rot", [2, 128, Lm], F32, I)
        self.rpbp = self.dram("rpbp", [D, 8, 15, 127], F32, I)
        self.rvc = {L: self.dram("rvc%d" % L, [128, (L // 512) * 64], F32, I) for L in sorted(set(self.seq_lens))}
        self.wb_in = self.dram("wb_in", [D, DM, PROJ], BF16)
        self.wb_br = self.dram("wb_br", [D, 3, 512, DM], BF16)
        self.wb_out = self.dram("wb_out", [D, DM, DM], BF16)
        self.wb_up = self.dram("wb_up", [D, DM, 2 * DFF], BF16)
        self.wb_dn = self.dram("wb_dn", [D, DFF, DM], BF16)
        self.X = self.dram("X", [DM, Lm], F32)
        self.XM = self.dram("XM", [DM, Lm], F32)
        self.XBC = self.dram("XBC", [1024, Lm], BF16)
        self.DT = self.dram("DT", [16, Lm], F32)
        self.RQ = self.dram("RQ", [256, Lm], BF16)
        self.RK = self.dram("RK", [256, Lm], BF16)
        self.NQ = self.dram("NQ", [512, Lm], BF16)
        self.NK = self.dram("NK", [512, Lm], BF16)
        self.GT = self.dram("GT", [3072, Lm], BF16)
        self.ZT = self.dram("ZT", [Lm, 512], BF16)
        self.RVT = self.dram("RVT", [Lm, 512], BF16)
        self.RGT = self.dram("RGT", [Lm, 512], BF16)
        self.NVT = self.dram("NVT", [Lm, 512], BF16)
        self.BCF = self.dram("BCF", [512, Lm], BF16)
        self.XTOK = self.dram("XTOK", [Lm, 512], BF16)
        self.FT = self.dram("FT", [Lm, 128], F32)
        self.UF = self.dram("UF", [16, Lm], F32)
        self.RKR = self.dram("RKR", [256, Lm], BF16)
        self.OS = self.dram("OS", [512, Lm], BF16)
        self.OR = self.dram("OR", [512, Lm], BF16)
        self.ON = self.dram("ON", [512, Lm], BF16)

        with ExitStack() as es:
            P = Prog(nc, es)
            self.P = P
            self.setup_consts()
            self.convert_weights()
            P.barrier()
            self.base_off = P.off
            for s, L in enumerate(self.seq_lens):
                self.s, self.L = s, L
                self.NT = L // 512
                self.phase(self.ph_in)
                for l in range(self.run_layers):
                    self.l = l
                    for nm, f in (("a", self.ph_a), ("scan", self.ph_scan), ("na", self.ph_na), ("merge", self.ph_merge), ("ffn", self.ph_ffn)):
                        if self.only is None or nm in self.only:
                            self.phase(f)
                self.phase(self.ph_out)
            blk = es.enter_context(nc.Block())
            P.emit(blk)
        return nc

    def dump(self, name, buf, ap, shape, dt):
        if name not in self.debug:
            return
        if name not in self._dumps:
            self._dumps[name] = self.nc.dram_tensor(name, [128] + list(shape), dt, kind="ExternalOutput").ap()
        self.P.dma(self._dumps[name], ap, r=[buf])

    def phase(self, fn):
        self.P.off = self.base_off
        self.P.dnext = 1
        fn()
        assert self.P.dnext <= NDS, self.P.dnext
        self.P.barrier()

    def setup_consts(self):
        P = self.P
        self.cf = P.alloc("cf", [NCON], F32)
        P.dma(self.cf[:], self.consts[:, :], w=[self.cf])
        cf = self.cf
        self.identb = P.alloc("identb", [128], BF16)
        P.dve(lambda e: e.tensor_copy(out=self.identb[:], in_=cf[:, K_ID:K_ID + 128]), r=[cf], w=[self.identb])
        self.protb = P.alloc("protb", [128], BF16)
        P.dve(lambda e: e.tensor_copy(out=self.protb[:], in_=cf[:, K_PROT:K_PROT + 128]), r=[cf], w=[self.protb])
        self.jb = P.alloc("jb", [64], BF16)
        P.dve(lambda e: e.tensor_copy(out=self.jb[:], in_=cf[:, K_J:K_J + 64]), r=[cf], w=[self.jb])
        self.onesm = P.alloc("onesm", [128], BF16)
        P.dve(lambda e: e.memset(self.onesm[:], 1.0 / 1024.0), w=[self.onesm])
        self.onesf = P.alloc("onesf", [128], F32)
        P.dve(lambda e: e.memset(self.onesf[:], 1.0), w=[self.onesf])
        self.epsb = P.alloc("epsb", [1], F32)
        P.dve(lambda e: e.memset(self.epsb[:], EPS), w=[self.epsb])
        for b in (cf, self.identb, self.protb, self.jb, self.onesm, self.onesf, self.epsb):
            b.const = True

    def convert_weights(self):
        P = self.P
        D = self.depth
        for l in range(D):
            for (src, dst, rows) in ((self.w_in[l], self.wb_in[l], DM), (self.w_out[l], self.wb_out[l], DM),
                                     (self.w_up[l], self.wb_up[l], DM), (self.w_dn[l], self.wb_dn[l], DFF)):
                for r0 in range(0, rows, 128):
                    P.dma(dst[r0:r0 + 128, :], src[r0:r0 + 128, :], q="pool")
            for i in range(3):
                for r0 in range(0, 512, 128):
                    P.dma(self.wb_br[l, i, r0:r0 + 128, :], self.w_br[l, i, r0:r0 + 128, :], q="pool")

    def load_small(self, l):
        P = self.P
        sm = P.alloc("sm", [NSM], F32)
        P.dma(sm[:], self.small[l], w=[sm])
        return sm

    def norm(self, xt, n, g, h, sq, rs, out_f32=False):
        P = self.P
        P.act(lambda e: e.activation(out=sq[:, :, 0:n], in_=xt[:, :, 0:n], func=AF.Square), r=[xt], w=[sq])
        ps = P.psum()
        for c in range(KC):
            P.pe(lambda e, c=c: e.matmul(ps[:, 0:n], lhsT=self.onesm[:], rhs=sq[:, c, 0:n], start=(c == 0), stop=(c == KC - 1)),
                 r=[sq], w=[ps], join=(c > 0))
        P.act(lambda e: e.activation(out=rs[:, 0:n], in_=ps[:, 0:n], func=AF.Sqrt, bias=self.epsb[:, 0:1]), r=[ps], w=[rs])
        P.dve(lambda e: e.reciprocal(out=rs[:, 0:n], in_=rs[:, 0:n]), r=[rs], w=[rs])
        for c in range(KC):
            P.dve(lambda e, c=c: e.scalar_tensor_tensor(out=h[:, c, 0:n], in0=xt[:, c, 0:n], scalar=g[:, c:c + 1], in1=rs[:, 0:n],
                                                        op0=ALU.mult, op1=ALU.mult), r=[xt, rs], w=[h], join=(c > 0))

    def ph_in(self):
        P = self.P
        x = self.x_in[self.s]
        cf = self.cf
        xin = [P.alloc("xin%d" % i, [4, DM], F32) for i in range(2)]
        xt = [P.alloc("xt%d" % i, [KC, 512], F32) for i in range(2)]
        for t in range(self.NT):
            t0 = t * 512
            a, b = xin[t % 2], xt[t % 2]
            P.dma(a[:], x[t0:t0 + 512, :].rearrange("(tc p) f -> p tc f", p=128), w=[a])
            for fc in range(KC):
                ps = P.psum()
                for tc in range(4):
                    P.pe(lambda e, tc=tc, fc=fc, ps=ps, a=a: e.transpose(ps[:, tc * 128:(tc + 1) * 128], a[:, tc, fc * 128:(fc + 1) * 128],
                                                                         cf[:, K_ID:K_ID + 128]), r=[a], w=[ps], join=(tc > 0))
                P.evac(b[:, fc, :], ps[:], r=[ps], w=[b], join=(fc > 0))
            P.dma(self.X[:, t0:t0 + 512].rearrange("(c p) t -> p c t", p=128), b[:], r=[b])

    def ph_out(self):
        P = self.P
        y = self.y_out[self.s]
        cf = self.cf
        sm = self.load_small(self.depth)
        xt = [P.alloc("xt%d" % i, [KC, 512], F32) for i in range(2)]
        sq = P.alloc("sq", [KC, 512], BF16)
        rs = P.alloc("rs", [512], F32)
        yn = P.alloc("yn", [KC, 512], F32)
        yt = [P.alloc("yt%d" % i, [4, DM], F32) for i in range(2)]
        for t in range(self.NT):
            t0 = t * 512
            a, o = xt[t % 2], yt[t % 2]
            P.dma(a[:], self.X[:, t0:t0 + 512].rearrange("(c p) t -> p c t", p=128), w=[a])
            self.norm(a, 512, sm[:, S_GMIX:S_GMIX + 8], yn, sq, rs)
            for tc in range(4):
                for hf in range(2):
                    ps = P.psum()
                    for k in range(4):
                        P.pe(lambda e, k=k, hf=hf, tc=tc, ps=ps: e.transpose(ps[:, k * 128:(k + 1) * 128], yn[:, hf * 4 + k, tc * 128:(tc + 1) * 128],
                                                                            cf[:, K_ID:K_ID + 128]), r=[yn], w=[ps], join=(k > 0))
                    P.evac(o[:, tc, hf * 512:(hf + 1) * 512], ps[:], r=[ps], w=[o], join=(tc > 0 or hf > 0))
            P.dma(y[t0:t0 + 512, :].rearrange("(tc p) f -> p tc f", p=128), o[:], r=[o])

    def ph_a(self):
        P = self.P
        l = self.l
        sm = self.load_small(l)
        W = self.wb_in[l]
        xt = [P.alloc("xt%d" % i, [KC, 512], F32) for i in range(2)]
        sq = P.alloc("sq", [KC, 512], BF16)
        rs = P.alloc("rs", [512], F32)
        hb = [P.alloc("h%d" % i, [KC, 512], BF16) for i in range(2)]
        wt = [P.alloc("w%d" % i, [KC, 1024], BF16) for i in range(2)]
        of = [P.alloc("of%d" % i, [4, 512], BF16) for i in range(2)]
        ot = [P.alloc("ot%d" % i, [4, 512], BF16) for i in range(2)]
        dtf = P.alloc("dtf", [512], F32)
        wi = [0]
        oi = [0]

        def loadw(c0, n):
            w = wt[wi[0] % 2]
            wi[0] += 1
            P.dma(w[:, :, 0:n], W[:, c0:c0 + n].rearrange("(c p) n -> p c n", p=128), w=[w])
            return w

        def loadx(t):
            P.dma(xt[t % 2][:], self.X[:, t * 512:(t + 1) * 512].rearrange("(c p) t -> p c t", p=128), w=[xt[t % 2]])

        jobs = [("d", None, C_DT, 16)]
        jobs += [("f", d_, c_, n_) for (d_, c_, n_) in
                 [(self.XBC, C_XBC, 1024), (self.RQ, C_RQ, 256), (self.RK, C_RK, 256), (self.NQ, C_NQ, 512),
                  (self.NK, C_NK, 512), (self.GT, C_GATE, 1024), (self.GT, C_GATE + 1024, 1024), (self.GT, C_GATE + 2048, 1024)]]
        jobs += [("t", d_, c_, 512) for (d_, c_) in [(self.ZT, C_Z), (self.RVT, C_RV), (self.RGT, C_RG), (self.NVT, C_NV)]]
        seq = [(t, j) for t in range(self.NT) for j in range(len(jobs))]
        wq = {}

        def ensure(i):
            if i < len(seq) and i not in wq:
                wq[i] = loadw(jobs[seq[i][1]][2], jobs[seq[i][1]][3])

        loadx(0)
        ensure(0)
        for i, (t, j) in enumerate(seq):
            kind, dst, c0, n = jobs[j]
            t0 = t * 512
            a, h = xt[t % 2], hb[t % 2]
            if j == 0:
                if t + 1 < self.NT:
                    loadx(t + 1)
                self.norm(a, 512, sm[:, S_GMIX:S_GMIX + 8], h, sq, rs)
            ensure(i + 1)
            w = wq.pop(i)
            if kind == "d":
                ps = P.psum()
                for c in range(KC):
                    P.pe(lambda e, c=c, w=w, ps=ps, h=h: e.matmul(ps[0:16, :], lhsT=w[:, c, 0:16], rhs=h[:, c, :], start=(c == 0), stop=(c == KC - 1)),
                         r=[w, h], w=[ps], join=(c > 0))
                P.act(lambda e, ps=ps: e.activation(out=dtf[0:16, :], in_=ps[0:16, :], func=AF.Copy), r=[ps], w=[dtf])
                P.dma(self.DT[:, t0:t0 + 512], dtf[0:16, :], r=[dtf])
            elif kind == "f":
                r0 = c0 - (C_GATE if dst is self.GT else c0)
                for m in range(n // 128):
                    if m % 4 == 0:
                        o = of[oi[0] % 2]
                        oi[0] += 1
                    ps = P.psum()
                    for c in range(KC):
                        P.pe(lambda e, c=c, m=m, w=w, ps=ps, h=h: e.matmul(ps[:], lhsT=w[:, c, m * 128:(m + 1) * 128], rhs=h[:, c, :],
                                                                          start=(c == 0), stop=(c == KC - 1)), r=[w, h], w=[ps], join=(c > 0))
                    P.evac(o[:, m % 4, :], ps[:], r=[ps], w=[o], join=(m % 4 > 0))
                    if m % 4 == 3 or m == n // 128 - 1:
                        k = m % 4 + 1
                        rr = r0 + (m - (k - 1)) * 128
                        P.dma(dst[rr:rr + k * 128, t0:t0 + 512].rearrange("(c p) t -> p c t", p=128), o[:, 0:k, :], r=[o])
            else:
                o = ot[oi[0] % 2]
                oi[0] += 1
                for tc in range(4):
                    ps = P.psum()
                    for c in range(KC):
                        P.pe(lambda e, c=c, tc=tc, w=w, ps=ps, h=h: e.matmul(ps[:], lhsT=h[:, c, tc * 128:(tc + 1) * 128], rhs=w[:, c, 0:512],
                                                                            start=(c == 0), stop=(c == KC - 1)), r=[w, h], w=[ps], join=(c > 0))
                    P.evac(o[:, tc, :], ps[:], r=[ps], w=[o], join=(tc > 0))
                P.dma(dst[t0:t0 + 512, :].rearrange("(tc p) c -> p tc c", p=128), o[:], r=[o])

    def rotary(self, src, dst, cs, sn, scale, tmp1, tmp2, split=False):
        P = self.P
        for hp in range(2):
            ps = P.psum()
            P.pe(lambda e, hp=hp, ps=ps: e.matmul(ps[:], lhsT=self.protb[:], rhs=src[:, hp, :], start=True, stop=True), r=[src], w=[ps])
            P.dve(lambda e, hp=hp: e.scalar_tensor_tensor(out=tmp1[:], in0=src[:, hp, :], scalar=scale, in1=cs[:], op0=ALU.mult, op1=ALU.mult),
                  r=[src, cs], w=[tmp1])
            P.dve(lambda e, ps=ps: e.scalar_tensor_tensor(out=tmp2[:], in0=ps[:], scalar=scale, in1=sn[:], op0=ALU.mult, op1=ALU.mult),
                  r=[ps, sn], w=[tmp2])
            if split:
                for hf in range(2):
                    o = hf * 64
                    P.pool(lambda e, hp=hp, hf=hf, o=o: e.tensor_tensor(out=dst[o:o + 64, 2 * hp + hf, :], in0=tmp1[o:o + 64, :], in1=tmp2[o:o + 64, :], op=ALU.add),
                           r=[tmp1, tmp2], w=[dst], join=True)
            else:
                P.pool(lambda e, hp=hp: e.tensor_tensor(out=dst[:, hp, :], in0=tmp1[:], in1=tmp2[:], op=ALU.add), r=[tmp1, tmp2], w=[dst], join=(hp > 0))

    def transpose_to(self, src_aps, dst_ap, rbufs, wbuf, join, rng=(0, 8)):
        P = self.P
        ps = P.psum(*rng)
        psb = ps.ap.bitcast(BF16)
        for k, s in enumerate(src_aps):
            P.pe(lambda e, k=k, s=s, psb=psb: e.transpose(psb[:, k * 128:(k + 1) * 128], s, self.identb[:]), r=rbufs, w=[ps], join=(k > 0))
        n = len(src_aps)
        src = psb[:, 0:n * 128]
        if len(dst_ap.shape) == 3:
            src = src.rearrange("p (a b) -> p a b", a=n)
        P.evac(dst_ap, src, r=[ps], w=[wbuf], join=join)

    def ph_scan(self):
        P = self.P
        l, L, NT = self.l, self.L, self.NT
        C = L // 128
        cf = self.cf
        sm = self.load_small(l)
        RSm = [P.alloc("RS%d" % c, [512], BF16) for c in range(C)]
        RSf = [Buf("RSf%d" % c, b.ap[0:64]) for c, b in enumerate(RSm)]
        RSb = [Buf("RSb%d" % c, b.ap[64:128]) for c, b in enumerate(RSm)]
        lgv = P.alloc("lgv", [8], F32)
        P.act(lambda e: e.activation(out=lgv[:], in_=sm[:, S_TH:S_TH + 8], func=AF.Exp), r=[sm], w=[lgv])
        P.dve(lambda e: e.tensor_scalar(out=lgv[:], in0=lgv[:], scalar1=-1.0, scalar2=None, op0=ALU.mult), r=[lgv], w=[lgv])
        lg_hd = lgv[:].rearrange("p (d h) -> p h d", d=2)
        A16 = P.alloc("A16", [1], F32)
        P.act(lambda e: e.activation(out=A16[0:16, :], in_=sm[0:16, S_ALOG:S_ALOG + 1], func=AF.Exp), r=[sm], w=[A16])
        P.dve(lambda e: e.tensor_scalar(out=A16[0:16, :], in0=A16[0:16, :], scalar1=-1.0, scalar2=None, op0=ALU.mult), r=[A16], w=[A16])
        WFB = P.alloc("WFB", [4, 2], F32)
        P.dve(lambda e: e.tensor_tensor(out=WFB[:], in0=lg_hd, in1=bc(cf[:, K_EJ:K_EJ + 2].unsqueeze(1), [128, 4, 2]), op=ALU.mult), r=[lgv], w=[WFB])
        P.act(lambda e: e.activation(out=WFB[:], in_=WFB[:], func=AF.Exp), r=[WFB], w=[WFB])
        DECR = P.alloc("DECR", [4], F32)
        P.act(lambda e: e.activation(out=DECR[0:64, :], in_=lgv[0:64, 0:4], func=AF.Exp, scale=128.0), r=[lgv], w=[DECR])
        P.act(lambda e: e.activation(out=DECR[64:128, :], in_=lgv[64:128, 4:8], func=AF.Exp, scale=128.0), r=[lgv], w=[DECR], join=True)
        mark_rs = P.off
        SF = [P.alloc("SF%d" % c, [512], BF16) for c in range(C)]
        SB = [P.alloc("SB%d" % c, [512], BF16) for c in range(C)]
        DEC = P.alloc("DEC", [C, 16], F32)
        mark = P.off

        xbch = [P.alloc("xbch%d" % i, [8, 516], BF16) for i in range(2)]
        acc = [P.alloc("acc%d" % i, [512], F32) for i in range(3)]
        xc = P.alloc("xc", [8, 512], BF16)
        xtok = [P.alloc("xtok%d" % i, [4, 512], BF16) for i in range(2)]
        btok = P.alloc("btok", [4, 256], BF16)
        Fb = [P.alloc("F%d" % i, [512], F32) for i in range(2)]
        ftb = [P.alloc("ft%d" % i, [4, 128], F32) for i in range(2)]
        dtrb = [P.alloc("dtr%d" % i, [512], F32) for i in range(2)]
        s16 = [P.alloc("s16_%d" % i, [512], F32) for i in range(6)]
        dt16 = P.alloc("dt16", [512], F32)
        et = P.alloc("et", [4], F32)
        DD = P.alloc("DD", [4, 16], F32)
        xw = [P.alloc("xw%d" % i, [512], BF16) for i in range(2)]
        for b in Fb:
            P.pool(lambda e, b=b: e.memset(b[:], 0.0), w=[b])
        cw = sm[:, S_CW:S_CW + 40].rearrange("p (c j) -> p c j", j=5)
        cb = sm[:, S_CB:S_CB + 8]
        mf, mb, nmb = cf[0:16, K_MF:K_MF + 1], cf[0:16, K_MF + 1:K_MF + 2], cf[0:16, K_MF + 2:K_MF + 3]
        def loads1(t):
            t0 = t * 512
            xb = xbch[t % 2]
            lo, hi = max(t0 - 2, 0), min(t0 + 514, L)
            if t == 0:
                P.pool(lambda e, xb=xb: e.memset(xb[:, :, 0:2], 0.0), w=[xb])
            if t == NT - 1:
                P.pool(lambda e, xb=xb: e.memset(xb[:, :, 514:516], 0.0), w=[xb])
            P.dma(xb[:, :, lo - (t0 - 2):hi - (t0 - 2)], self.XBC[:, lo:hi].rearrange("(c p) t -> p c t", p=128), w=[xb],
                  join=(t == 0 or t == NT - 1))
            P.dma(dtrb[t % 2][0:16, :], self.DT[:, t0:t0 + 512], w=[dtrb[t % 2]])

        loads1(0)
        for t in range(NT):
            t0 = t * 512
            xb = xbch[t % 2]
            dtr = dtrb[t % 2]
            if t + 1 < NT:
                loads1(t + 1)
            for c in range(8):
                a = acc[c % 3]
                P.pool(lambda e, c=c, a=a, xb=xb: e.tensor_scalar(out=a[:], in0=xb[:, c, 0:512], scalar1=cw[:, c, 0:1], scalar2=cb[:, c:c + 1],
                                                                  op0=ALU.mult, op1=ALU.add), r=[xb, sm], w=[a])
                for j in range(1, 5):
                    P.dve(lambda e, c=c, j=j, a=a, xb=xb: e.scalar_tensor_tensor(out=a[:], in0=xb[:, c, j:j + 512], scalar=cw[:, c, j:j + 1], in1=a[:],
                                                                                 op0=ALU.mult, op1=ALU.add), r=[xb, a, sm], w=[a])
                P.act(lambda e, c=c, a=a: e.activation(out=xc[:, c, :], in_=a[:], func=AF.Silu), r=[a], w=[xc], join=(c > 0))
            P.dma(self.BCF[:, t0:t0 + 512].rearrange("(c p) t -> p c t", p=128), xc[:, 4:8, :], r=[xc])
            F = Fb[t % 2]
            ft = ftb[t % 2]
            e1, la, cs, tme, tq, ew = s16
            P.act(lambda e, dtr=dtr: e.activation(out=e1[0:16, :], in_=dtr[0:16, :], func=AF.Exp, bias=sm[0:16, S_DTB:S_DTB + 1]), r=[dtr, sm], w=[e1])
            P.act(lambda e: e.activation(out=dt16[0:16, :], in_=e1[0:16, :], func=AF.Ln, bias=1.0), r=[e1], w=[dt16])
            P.act(lambda e, F=F: e.activation(out=F[32:48, :], in_=dt16[0:16, :], func=AF.Copy), r=[dt16], w=[F])
            P.dve(lambda e: e.tensor_scalar(out=la[0:16, :], in0=dt16[0:16, :], scalar1=A16[0:16, 0:1], scalar2=None, op0=ALU.mult), r=[dt16, A16], w=[la])
            P.dve(lambda e: e.tensor_tensor_scan(out=cs[0:16, :], data0=cf[0:16, K_RM:K_RM + 512], data1=la[0:16, :], initial=0.0,
                                                 op0=ALU.mult, op1=ALU.add), r=[la], w=[cs])
            P.dve(lambda e, F=F: e.scalar_tensor_tensor(out=F[0:16, :], in0=la[0:16, :], scalar=nmb, in1=cs[0:16, :], op0=ALU.mult, op1=ALU.add),
                  r=[la, cs], w=[F], join=True)
            cs3 = cs[0:16, :].rearrange("p (c j) -> p c j", j=128)
            P.dve(lambda e, F=F: e.tensor_tensor(out=tme[0:16, :].rearrange("p (c j) -> p c j", j=128), in0=bc(cs3[:, :, 127:128], [16, 4, 128]),
                                                 in1=F[0:16, :].rearrange("p (c j) -> p c j", j=128), op=ALU.subtract), r=[cs, F], w=[tme])
            P.dve(lambda e, F=F: e.tensor_scalar(out=tq[0:16, :], in0=F[0:16, :], scalar1=mb, scalar2=None, op0=ALU.mult), r=[F], w=[tq])
            P.dve(lambda e: e.scalar_tensor_tensor(out=tq[0:16, :], in0=tme[0:16, :], scalar=mf, in1=tq[0:16, :], op0=ALU.mult, op1=ALU.add),
                  r=[tme, tq], w=[tq])
            P.act(lambda e: e.activation(out=ew[0:16, :], in_=tq[0:16, :], func=AF.Exp), r=[tq], w=[ew])
            P.dve(lambda e, F=F: e.tensor_tensor(out=F[64:80, :], in0=ew[0:16, :], in1=dt16[0:16, :], op=ALU.mult), r=[ew, dt16], w=[F], join=True)
            P.dve(lambda e: e.tensor_scalar(out=e1[0:16, :], in0=tme[0:16, :], scalar1=mb, scalar2=None, op0=ALU.mult), r=[tme], w=[e1])
            P.dve(lambda e, F=F: e.scalar_tensor_tensor(out=e1[0:16, :], in0=F[0:16, :], scalar=mf, in1=e1[0:16, :], op0=ALU.mult, op1=ALU.add),
                  r=[F, e1], w=[e1])
            P.act(lambda e, F=F: e.activation(out=F[96:112, :], in_=e1[0:16, :], func=AF.Exp), r=[e1], w=[F], join=True)
            P.act(lambda e: e.activation(out=et[0:16, :], in_=cs3[:, :, 127], func=AF.Exp), r=[cs], w=[et])
            P.dve(lambda e: e.tensor_tensor(out=DD[0:16, :, :], in0=bc(et[0:16, :].unsqueeze(2), [16, 4, 16]),
                                            in1=bc(cf[0:16, K_ID:K_ID + 16].unsqueeze(1), [16, 4, 16]), op=ALU.mult), r=[et], w=[DD])
            ps = P.psum()
            P.pe(lambda e, ps=ps: e.matmul(ps[:, 0:64], lhsT=self.onesf[0:16, :], rhs=DD[0:16, :, :].rearrange("p a b -> p (a b)"), start=True, stop=True),
                 r=[DD], w=[ps])
            P.act(lambda e, ps=ps, t=t: e.activation(out=DEC[:, 4 * t:4 * t + 4, :].rearrange("p a b -> p (a b)"), in_=ps[:, 0:64], func=AF.Copy),
                  r=[ps], w=[DEC], join=True)
            P.dma(self.UF[:, t0:t0 + 512], F[0:16, :], r=[F])
            ps = P.psum()
            for tc in range(4):
                P.pe(lambda e, tc=tc, ps=ps, F=F: e.transpose(ps[:, tc * 128:(tc + 1) * 128], F[:, tc * 128:(tc + 1) * 128], cf[:, K_ID:K_ID + 128]),
                     r=[F], w=[ps], join=(tc > 0))
            P.evac(ft[:].rearrange("p a b -> p (a b)"), ps[:], r=[ps], w=[ft])
            P.dma(self.FT[t0:t0 + 512, :].rearrange("(tc p) c -> p tc c", p=128), ft[:], r=[ft])
            xk = xtok[t % 2]
            for tc in range(4):
                self.transpose_to([xc[:, c4, tc * 128:(tc + 1) * 128] for c4 in range(4)], xk[:, tc, :], [xc], xk, join=(tc > 0))
                self.transpose_to([xc[:, 4 + g, tc * 128:(tc + 1) * 128] for g in range(2)], btok[:, tc, :], [xc], btok, join=(tc > 0))
            P.dma(self.XTOK[t0:t0 + 512, :].rearrange("(tc p) c -> p tc c", p=128), xk[:], r=[xk])
            for tc in range(4):
                c = 4 * t + tc
                for d, S in ((0, SF), (1, SB)):
                    x_w = xw[d]
                    P.dve(lambda e, tc=tc, d=d, x_w=x_w, xk=xk, ft=ft: e.tensor_tensor(
                        out=x_w[:].rearrange("p (h q) -> p h q", q=64), in0=xk[:, tc, :].rearrange("p (h q) -> p h q", q=64),
                        in1=bc(ft[:, tc, 64 + 8 * d:72 + 8 * d].unsqueeze(2), [128, 8, 64]), op=ALU.mult), r=[xk, ft], w=[x_w])
                    ps = P.psum()
                    for g in range(2):
                        P.pe(lambda e, g=g, tc=tc, ps=ps, x_w=x_w: e.matmul(ps[:, g * 256:(g + 1) * 256], lhsT=btok[:, tc, g * 128:(g + 1) * 128],
                                                                          rhs=x_w[:, g * 256:(g + 1) * 256], start=True, stop=True),
                             r=[btok, x_w], w=[ps], join=(g > 0))
                    P.evac(S[c][:], ps[:], r=[ps], w=[S[c]])
        P.barrier()
        P.off = mark
        kt_ = [P.alloc("kt%d" % i, [2, 512], BF16) for i in range(2)]
        kr = P.alloc("kr", [2, 512], BF16)
        cs_ = [P.alloc("cs%d" % i, [512], F32) for i in range(2)]
        sn_ = [P.alloc("sn%d" % i, [512], F32) for i in range(2)]
        t1 = P.alloc("t1", [512], F32)
        t2 = P.alloc("t2", [512], F32)
        vtok = [P.alloc("vtok%d" % i, [4, 512], BF16) for i in range(2)]
        ktok = P.alloc("ktok", [4, 256], BF16)
        kw = P.alloc("kw", [4, 128], BF16)
        def loads1r(t):
            t0 = t * 512
            ktb, csb, snb, vt = kt_[t % 2], cs_[t % 2], sn_[t % 2], vtok[t % 2]
            P.dma(ktb[:], self.RK[:, t0:t0 + 512].rearrange("(c p) t -> p c t", p=128), w=[ktb])
            P.dma(csb[:], self.rot[0, :, t0:t0 + 512], w=[csb])
            P.dma(snb[:], self.rot[1, :, t0:t0 + 512], w=[snb])
            P.dma(vt[:], self.RVT[t0:t0 + 512, :].rearrange("(tc p) c -> p tc c", p=128), w=[vt])

        loads1r(0)
        for t in range(NT):
            t0 = t * 512
            ktb, csb, snb, vt = kt_[t % 2], cs_[t % 2], sn_[t % 2], vtok[t % 2]
            if t + 1 < NT:
                loads1r(t + 1)
            self.rotary(ktb, kr, csb, snb, 0.125, t1, t2)
            P.dma(self.RKR[:, t0:t0 + 512].rearrange("(c p) t -> p c t", p=128), kr[:], r=[kr])
            for tc in range(4):
                self.transpose_to([kr[:, hp, tc * 128:(tc + 1) * 128] for hp in range(2)], ktok[:, tc, :], [kr], ktok, join=(tc > 0))
            for tc in range(4):
                c = 4 * t + tc
                P.dve(lambda e, tc=tc: e.tensor_tensor(out=kw[:].rearrange("p h (d n) -> p h d n", d=2),
                                                       in0=bc(ktok[:, tc, :].rearrange("p (h n) -> p h n", n=64).unsqueeze(2), [128, 4, 2, 64]),
                                                       in1=bc(WFB[:].unsqueeze(3), [128, 4, 2, 64]), op=ALU.mult), r=[ktok, WFB], w=[kw])
                ps = P.psum()
                for h in range(4):
                    P.pe(lambda e, h=h, tc=tc, ps=ps, vt=vt: e.matmul(ps[:, h * 128:(h + 1) * 128], lhsT=kw[:, h, :], rhs=vt[:, tc, h * 128:(h + 1) * 128],
                                                                     start=True, stop=True), r=[kw, vt], w=[ps], join=(h > 0))
                P.evac(RSm[c][:], ps[:], r=[ps], w=[RSf[c], RSb[c]])

        P.barrier()
        P.off = mark
        if getattr(self, "scan_stop", 9) < 1:
            return
        Rf = [P.alloc("Rf%d" % i, [512], F32) for i in range(2)]
        Rb = [P.alloc("Rb%d" % i, [512], F32) for i in range(2)]
        RR = [P.alloc("RR%d" % i, [512], F32) for i in range(2)]
        RRf = [Buf("RRf%d" % i, b.ap[0:64]) for i, b in enumerate(RR)]
        RRb = [Buf("RRb%d" % i, b.ap[64:128]) for i, b in enumerate(RR)]
        for b in (Rf[0], Rb[0], RR[0]):
            P.pool(lambda e, b=b: e.memset(b[:], 0.0), w=[b])
        RRf[0].w = dict(RR[0].w)
        RRb[0].w = dict(RR[0].w)
        for k in range(C):
            cur, nxt = k % 2, (k + 1) % 2
            for (R, S, c, d0) in ((Rf, SF, k, 0), (Rb, SB, C - 1 - k, 8)):
                P.dve(lambda e, R=R, c=c, d0=d0, cur=cur, nxt=nxt: e.tensor_tensor(
                    out=R[nxt][:].rearrange("p (h q) -> p h q", q=64), in0=R[cur][:].rearrange("p (h q) -> p h q", q=64),
                    in1=bc(DEC[:, c, d0:d0 + 8].unsqueeze(2), [128, 8, 64]), op=ALU.mult), r=[R[cur], DEC], w=[R[nxt]])
                P.pool(lambda e, R=R, S=S, c=c, nxt=nxt: e.tensor_tensor(out=R[nxt][:], in0=R[nxt][:], in1=S[c][:], op=ALU.add),
                       r=[R[nxt], S[c]], w=[R[nxt]])
                P.act(lambda e, R=R, S=S, c=c, cur=cur: e.activation(out=S[c][:], in_=R[cur][:], func=AF.Copy), r=[R[cur]], w=[S[c]])
            for (RRh, RSh, c, p0) in ((RRf, RSf, k, 0), (RRb, RSb, C - 1 - k, 64)):
                P.dve(lambda e, c=c, p0=p0, cur=cur, nxt=nxt: e.tensor_tensor(
                    out=RR[nxt][p0:p0 + 64, :].rearrange("p (h q) -> p h q", q=128), in0=RR[cur][p0:p0 + 64, :].rearrange("p (h q) -> p h q", q=128),
                    in1=bc(DECR[p0:p0 + 64, :].unsqueeze(2), [64, 4, 128]), op=ALU.mult), r=[RRh[cur], DECR], w=[RRh[nxt]])
                P.pool(lambda e, c=c, p0=p0, nxt=nxt: e.tensor_tensor(out=RR[nxt][p0:p0 + 64, :], in0=RR[nxt][p0:p0 + 64, :], in1=RSm[c][p0:p0 + 64, :],
                                                                      op=ALU.add), r=[RRh[nxt], RSh[c]], w=[RRh[nxt]])
                P.act(lambda e, c=c, p0=p0, cur=cur: e.activation(out=RSm[c][p0:p0 + 64, :], in_=RR[cur][p0:p0 + 64, :], func=AF.Copy),
                      r=[RRh[cur]], w=[RSh[c]])
        for c in range(C):
            self.dump("D_SF%d" % c, SF[c], SF[c][:], [512], BF16)
            self.dump("D_SB%d" % c, SB[c], SB[c][:], [512], BF16)
        self.dump("D_DEC", DEC, DEC[:].rearrange("p a b -> p (a b)"), [C * 16], F32)
        P.barrier()
        P.off = mark
        if getattr(self, "scan_stop", 9) < 2:
            return
        bcf = [P.alloc("bcf%d" % i, [4, 512], BF16) for i in range(2)]
        xtok = [P.alloc("xtok%d" % i, [4, 512], BF16) for i in range(2)]
        ftb = [P.alloc("ft%d" % i, [4, 128], F32) for i in range(2)]
        ufb = [P.alloc("uf%d" % i, [512], F32) for i in range(2)]
        ztok = [P.alloc("ztok%d" % i, [4, 512], BF16) for i in range(2)]
        vfb = [P.alloc("vf%d" % i, [512], BF16) for i in range(2)]
        seg = [P.alloc("seg%d" % i, [512], F32) for i in range(2)]
        dec = [P.alloc("dec%d" % i, [512], F32) for i in range(2)]
        Gb = [P.alloc("G%d" % i, [512], BF16) for i in range(4)]
        m_ = [P.alloc("m%d" % i, [512], F32) for i in range(4)]
        szb = P.alloc("sz", [512], F32)
        yzb = P.alloc("yz", [512], F32)
        junk = P.alloc("junk", [512], BF16)
        ssb = P.alloc("ss", [8], F32)
        otk = P.alloc("otk", [512], BF16)
        oT = [P.alloc("oT%d" % i, [4, 512], BF16) for i in range(2)]
        DSK = sm[:, S_DSK:S_DSK + 8]
        NW = sm[:, S_NW:S_NW + 512]
        gi = [0]
        def loads2(t):
            t0 = t * 512
            bf_, xk, ft, uf, zt = bcf[t % 2], xtok[t % 2], ftb[t % 2], ufb[t % 2], ztok[t % 2]
            P.dma(bf_[:], self.BCF[:, t0:t0 + 512].rearrange("(c p) t -> p c t", p=128), w=[bf_])
            P.dma(xk[:], self.XTOK[t0:t0 + 512, :].rearrange("(tc p) c -> p tc c", p=128), w=[xk])
            P.dma(ft[:], self.FT[t0:t0 + 512, :].rearrange("(tc p) c -> p tc c", p=128), w=[ft])
            P.dma(uf[0:16, :], self.UF[:, t0:t0 + 512], w=[uf])
            P.dma(zt[:], self.ZT[t0:t0 + 512, :].rearrange("(tc p) c -> p tc c", p=128), w=[zt])

        loads2(0)
        if NT > 1:
            loads2(1)
        st = {}

        def front(c):
            t, tc = divmod(c, 4)
            bf_, xk, ft, uf, zt = bcf[t % 2], xtok[t % 2], ftb[t % 2], ufb[t % 2], ztok[t % 2]
            cols = slice(tc * 128, (tc + 1) * 128)
            for d in range(2):
                P.pool(lambda e, d=d, tc=tc, xk=xk, ft=ft: e.tensor_tensor(
                    out=vfb[d][:].rearrange("p (h q) -> p h q", q=64), in0=xk[:, tc, :].rearrange("p (h q) -> p h q", q=64),
                    in1=bc(ft[:, tc, 32 + 8 * d:40 + 8 * d].unsqueeze(2), [128, 8, 64]), op=ALU.mult), r=[xk, ft], w=[vfb[d]])
            pss = P.psum(2, 4)
            for g in range(2):
                P.pe(lambda e, g=g, pss=pss, bf_=bf_, cols=cols: e.matmul(pss[:, g * 128:(g + 1) * 128], lhsT=bf_[:, g, cols], rhs=bf_[:, 2 + g, cols],
                                                                        start=True, stop=True), r=[bf_], w=[pss], join=(g > 0))
            psY = P.psum(0, 2)
            Gs = {}
            for g in range(2):
                for d in range(2):
                    psa = P.psum(4, 6)
                    q0 = d * 8 + g * 4
                    mk = K_MNF if d == 0 else K_MNB
                    P.pe(lambda e, psa=psa, mk=mk: e.matmul(psa[:], lhsT=cf[:, K_ID:K_ID + 128], rhs=cf[:, mk:mk + 512], start=True, stop=False),
                         r=[], w=[psa])
                    for r_ in range(4):
                        P.pe(lambda e, r_=r_, q0=q0, psa=psa, uf=uf, cols=cols: e.matmul(
                            psa[:, r_ * 128:(r_ + 1) * 128], lhsT=cf[0:16, K_SEL + (q0 + r_) * 128:K_SEL + (q0 + r_ + 1) * 128], rhs=uf[0:16, cols],
                            start=False, stop=(r_ == 3)), r=[uf], w=[psa], join=True)
                    sg, dc = seg[gi[0] % 2], dec[gi[0] % 2]
                    G = Gb[gi[0] % 4]
                    gi[0] += 1
                    P.dve(lambda e, psa=psa, sg=sg, ft=ft, tc=tc, q0=q0, d=d: e.tensor_tensor(
                        out=sg[:].rearrange("p (h q) -> p h q", q=128), in0=psa[:].rearrange("p (h q) -> p h q", q=128),
                        in1=bc(ft[:, tc, q0:q0 + 4].unsqueeze(2), [128, 4, 128]), op=(ALU.subtract if d == 0 else ALU.add)), r=[psa, ft], w=[sg])
                    P.act(lambda e, sg=sg, dc=dc: e.activation(out=dc[:], in_=sg[:], func=AF.Exp), r=[sg], w=[dc])
                    P.dve(lambda e, dc=dc, G=G, pss=pss, g=g: e.tensor_tensor(
                        out=G[:].rearrange("p (h q) -> p h q", q=128), in0=dc[:].rearrange("p (h q) -> p h q", q=128),
                        in1=bc(pss[:, g * 128:(g + 1) * 128].unsqueeze(1), [128, 4, 128]), op=ALU.mult), r=[dc, pss], w=[G])
                    Gs[(g, d)] = G
            for g in range(2):
                for r_ in range(4):
                    h = g * 4 + r_
                    for d in range(2):
                        G = Gs[(g, d)]
                        P.pe(lambda e, G=G, r_=r_, h=h, d=d, psY=psY: e.matmul(psY[:, h * 64:(h + 1) * 64], lhsT=G[:, r_ * 128:(r_ + 1) * 128],
                                                                             rhs=vfb[d][:, h * 64:(h + 1) * 64], start=(d == 0), stop=(d == 1)),
                             r=[G, vfb[d]], w=[psY], join=not (g == 0 and r_ == 0 and d == 0))
            st[c] = psY

        def back(c):
            t, tc = divmod(c, 4)
            t0 = t * 512
            bf_, xk, ft, uf, zt = bcf[t % 2], xtok[t % 2], ftb[t % 2], ufb[t % 2], ztok[t % 2]
            oTs = oT[t % 2]
            cols = slice(tc * 128, (tc + 1) * 128)
            psY = st.pop(c)
            psI = []
            for d, S in ((0, SF), (1, SB)):
                pi = P.psum(6, 8)
                for g in range(2):
                    P.pe(lambda e, g=g, pi=pi, S=S, c=c, bf_=bf_, cols=cols: e.matmul(pi[:, g * 256:(g + 1) * 256], lhsT=bf_[:, 2 + g, cols],
                                                                                    rhs=S[c][:, g * 256:(g + 1) * 256], start=True, stop=True),
                         r=[bf_, S[c]], w=[pi], join=(g > 0))
                psI.append(pi)
            m1, m2, m3, yb = m_
            for d, mm in ((0, m1), (1, m2)):
                P.dve(lambda e, d=d, mm=mm, pi=psI[d], ft=ft, tc=tc: e.tensor_tensor(
                    out=mm[:].rearrange("p (h q) -> p h q", q=64), in0=pi[:].rearrange("p (h q) -> p h q", q=64),
                    in1=bc(ft[:, tc, 96 + 8 * d:104 + 8 * d].unsqueeze(2), [128, 8, 64]), op=ALU.mult), r=[psI[d], ft], w=[mm])
            P.pool(lambda e, xk=xk, tc=tc: e.tensor_tensor(out=m3[:].rearrange("p (h q) -> p h q", q=64), in0=xk[:, tc, :].rearrange("p (h q) -> p h q", q=64),
                                                          in1=bc(DSK.unsqueeze(2), [128, 8, 64]), op=ALU.mult), r=[xk, sm], w=[m3])
            P.pool(lambda e: e.tensor_tensor(out=m1[:], in0=m1[:], in1=m2[:], op=ALU.add), r=[m1, m2], w=[m1])
            P.pool(lambda e: e.tensor_tensor(out=m1[:], in0=m1[:], in1=m3[:], op=ALU.add), r=[m1, m3], w=[m1])
            P.dve(lambda e, psY=psY: e.tensor_tensor(out=yb[:], in0=psY[:], in1=m1[:], op=ALU.add), r=[psY, m1], w=[yb])
            P.act(lambda e, zt=zt, tc=tc: e.activation(out=szb[:], in_=zt[:, tc, :], func=AF.Silu), r=[zt], w=[szb])
            P.pool(lambda e: e.tensor_tensor(out=yzb[:], in0=yb[:], in1=szb[:], op=ALU.mult), r=[yb, szb], w=[yzb])
            P.act(lambda e: e.activation(out=junk[:], in_=yzb[:], func=AF.Square, accum_out=ssb[:, 0:1]), r=[yzb], w=[junk, ssb])
            P.act(lambda e: e.activation(out=ssb[:, 1:2], in_=ssb[:, 0:1], func=AF.Sqrt, bias=self.epsb[:, 0:1], scale=1.0 / 512.0), r=[ssb], w=[ssb])
            P.dve(lambda e: e.reciprocal(out=ssb[:, 2:3], in_=ssb[:, 1:2]), r=[ssb], w=[ssb])
            P.dve(lambda e: e.scalar_tensor_tensor(out=otk[:], in0=yzb[:], scalar=ssb[:, 2:3], in1=NW, op0=ALU.mult, op1=ALU.mult),
                  r=[yzb, ssb, sm], w=[otk])
            self.transpose_to([otk[:, c4 * 128:(c4 + 1) * 128] for c4 in range(4)], oTs[:, :, cols], [otk], oTs, join=(tc > 0), rng=(4, 6))
            if tc == 3:
                P.dma(self.OS[:, t0:t0 + 512].rearrange("(c p) t -> p c t", p=128), oTs[:], r=[oTs])
                if t + 2 < NT:
                    loads2(t + 2)

        front(0)
        for c in range(C):
            if c + 1 < C:
                front(c + 1)
            back(c)
        P.barrier()
        P.off = mark_rs
        if getattr(self, "scan_stop", 9) < 3:
            return
        qt_ = [P.alloc("qt%d" % i, [2, 512], BF16) for i in range(2)]
        qr = P.alloc("qrm", [4, 512], BF16)
        P.pool(lambda e: e.memset(qr[:], 0.0), w=[qr])
        krt = [P.alloc("krt%d" % i, [2, 512], BF16) for i in range(2)]
        cs_ = [P.alloc("cs%d" % i, [512], F32) for i in range(2)]
        sn_ = [P.alloc("sn%d" % i, [512], F32) for i in range(2)]
        t1 = P.alloc("t1", [512], F32)
        t2 = P.alloc("t2", [512], F32)
        vtok = [P.alloc("vtok%d" % i, [4, 512], BF16) for i in range(2)]
        rgt = [P.alloc("rgt%d" % i, [4, 512], BF16) for i in range(2)]
        qfb = P.alloc("qfb", [4, 512], BF16)
        Gr = [P.alloc("Gr%d" % i, [512], BF16) for i in range(2)]
        sqr = P.alloc("sqr", [512], F32)
        onr = P.alloc("onr", [512], F32)
        sgr = P.alloc("sgr", [512], F32)
        otr = P.alloc("otr", [512], BF16)
        oTr = [P.alloc("oTr%d" % i, [4, 512], BF16) for i in range(2)]
        ssr = P.alloc("ssr2", [8], F32)
        SCT = P.alloc("SCT", [4, 2, 128], F32)
        P.dve(lambda e: e.tensor_tensor(out=SCT[:], in0=bc(lg_hd.unsqueeze(3), [128, 4, 2, 128]),
                                        in1=bc(cf[:, K_EI:K_EI + 256].rearrange("p (d i) -> p d i", d=2).unsqueeze(1), [128, 4, 2, 128]), op=ALU.mult),
              r=[lgv], w=[SCT])
        P.act(lambda e: e.activation(out=SCT[:], in_=SCT[:], func=AF.Exp), r=[SCT], w=[SCT])
        DCOMB = P.alloc("DCOMB", [4, 128], F32)
        dtmp = P.alloc("dtmp", [4, 128], F32)
        P.dve(lambda e: e.tensor_tensor(out=DCOMB[:], in0=bc(cf[:, K_M1:K_M1 + 128].unsqueeze(1), [128, 4, 128]),
                                        in1=bc(lgv[:, 0:4].unsqueeze(2), [128, 4, 128]), op=ALU.mult), r=[lgv], w=[DCOMB])
        P.dve(lambda e: e.tensor_tensor(out=dtmp[:], in0=bc(cf[:, K_M2:K_M2 + 128].unsqueeze(1), [128, 4, 128]),
                                        in1=bc(lgv[:, 4:8].unsqueeze(2), [128, 4, 128]), op=ALU.mult), r=[lgv], w=[dtmp])
        P.pool(lambda e: e.tensor_tensor(out=DCOMB[:], in0=DCOMB[:], in1=dtmp[:], op=ALU.add), r=[DCOMB, dtmp], w=[DCOMB])
        P.act(lambda e: e.activation(out=DCOMB[:], in_=DCOMB[:], func=AF.Exp), r=[DCOMB], w=[DCOMB])
        def loads2r(t):
            t0 = t * 512
            qtb, krb, csb, snb, vt, rg = qt_[t % 2], krt[t % 2], cs_[t % 2], sn_[t % 2], vtok[t % 2], rgt[t % 2]
            P.dma(qtb[:], self.RQ[:, t0:t0 + 512].rearrange("(c p) t -> p c t", p=128), w=[qtb])
            P.dma(krb[:], self.RKR[:, t0:t0 + 512].rearrange("(c p) t -> p c t", p=128), w=[krb])
            P.dma(csb[:], self.rot[0, :, t0:t0 + 512], w=[csb])
            P.dma(snb[:], self.rot[1, :, t0:t0 + 512], w=[snb])
            P.dma(vt[:], self.RVT[t0:t0 + 512, :].rearrange("(tc p) c -> p tc c", p=128), w=[vt])
            P.dma(rg[:], self.RGT[t0:t0 + 512, :].rearrange("(tc p) c -> p tc c", p=128), w=[rg])

        loads2r(0)
        for t in range(NT):
            t0 = t * 512
            qtb, krb, csb, snb, vt, rg = qt_[t % 2], krt[t % 2], cs_[t % 2], sn_[t % 2], vtok[t % 2], rgt[t % 2]
            if t + 1 < NT:
                loads2r(t + 1)
            oTrr = oTr[t % 2]
            self.rotary(qtb, qr, csb, snb, 1.0, t1, t2, split=True)
            for h in (range(4) if not getattr(self, "dbg_noqfb", False) else []):
                hp, o = h // 2, (h % 2) * 64
                for d in range(2):
                    P.dve(lambda e, h=h, hp=hp, o=o, d=d: e.tensor_tensor(
                        out=qfb[d * 64:(d + 1) * 64, h, :].rearrange("p (c i) -> p c i", i=128), in0=qr[o:o + 64, h, :].rearrange("p (c i) -> p c i", i=128),
                        in1=bc(SCT[o:o + 64, h, d, :].unsqueeze(1), [64, 4, 128]), op=ALU.mult), r=[qr, SCT], w=[qfb], join=not (h == 0 and d == 0))
            RST = getattr(self, "ret_stop", 9)
            for tc in (range(4) if RST >= 1 else []):
                c = 4 * t + tc
                cols = slice(tc * 128, (tc + 1) * 128)
                pss = P.psum()
                for h in range(4):
                    hp, o = h // 2, (h % 2) * 64
                    P.pe(lambda e, h=h, hp=hp, o=o, pss=pss, krb=krb, cols=cols: e.matmul(pss[:, h * 128:(h + 1) * 128], lhsT=krb[:, hp, cols],
                                                                                        rhs=qr[:, h, cols], start=True, stop=True),
                         r=[krb, qr], w=[pss], join=(h > 0))
                if RST < 2:
                    continue
                G = Gr[tc % 2]
                P.dve(lambda e, G=G, pss=pss: e.tensor_tensor(out=G[:], in0=pss[:], in1=DCOMB[:].rearrange("p a b -> p (a b)"), op=ALU.mult),
                      r=[pss, DCOMB], w=[G])
                if RST < 3:
                    continue
                psY = P.psum()
                for h in range(4):
                    P.pe(lambda e, h=h, G=G, psY=psY, vt=vt, tc=tc: e.matmul(psY[:, h * 128:(h + 1) * 128], lhsT=G[:, h * 128:(h + 1) * 128],
                                                                           rhs=vt[:, tc, h * 128:(h + 1) * 128], start=True, stop=False),
                         r=[G, vt], w=[psY], join=(h > 0))
                    P.pe(lambda e, h=h, psY=psY, c=c, cols=cols: e.matmul(psY[:, h * 128:(h + 1) * 128], lhsT=qfb[:, h, cols],
                                                                        rhs=RSm[c][:, h * 128:(h + 1) * 128], start=False, stop=True),
                         r=[qfb, RSf[c], RSb[c]], w=[psY], join=True)
                if RST < 4:
                    continue
                for h in range(4):
                    P.act(lambda e, psY=psY, h=h: e.activation(out=sqr[:, h * 128:(h + 1) * 128], in_=psY[:, h * 128:(h + 1) * 128], func=AF.Square,
                                                              accum_out=ssr[:, 4 + h:5 + h]), r=[psY], w=[sqr, ssr], join=(h > 0))
                P.act(lambda e: e.activation(out=ssr[:, 4:8], in_=ssr[:, 4:8], func=AF.Sqrt, bias=self.epsb[:, 0:1], scale=1.0 / 128.0), r=[ssr], w=[ssr])
                P.dve(lambda e: e.reciprocal(out=ssr[:, 4:8], in_=ssr[:, 4:8]), r=[ssr], w=[ssr])
                P.dve(lambda e, psY=psY: e.tensor_tensor(out=onr[:].rearrange("p (h q) -> p h q", q=128), in0=psY[:].rearrange("p (h q) -> p h q", q=128),
                                                        in1=bc(ssr[:, 4:8].unsqueeze(2), [128, 4, 128]), op=ALU.mult), r=[psY, ssr], w=[onr])
                if RST < 5:
                    continue
                P.act(lambda e, rg=rg, tc=tc: e.activation(out=sgr[:], in_=rg[:, tc, :], func=AF.Silu), r=[rg], w=[sgr])
                P.pool(lambda e: e.tensor_tensor(out=otr[:], in0=onr[:], in1=sgr[:], op=ALU.mult), r=[onr, sgr], w=[otr])
                self.transpose_to([otr[:, c4 * 128:(c4 + 1) * 128] for c4 in range(4)], oTrr[:, :, cols], [otr], oTrr, join=(tc > 0))
            if RST >= 5:
                P.dma(self.OR[:, t0:t0 + 512].rearrange("(c p) t -> p c t", p=128), oTrr[:], r=[oTrr])


    def ph_na(self):
        P = self.P
        l, L, NT = self.l, self.L, self.NT
        rows = L // 64
        cf = self.cf
        rvc = P.alloc("rvc", [NT, 8, 8], F32)
        P.dma(rvc[:].rearrange("p a b c -> p (a b c)"), self.rvc[L][:, :], w=[rvc])
        ZR = P.alloc("ZR", [8, 22, 64], BF16)
        ZRI = P.alloc("ZRI", [8, 22, 64], BF16) if NT > 2 else None
        mark = P.off
        TT = P.alloc("TT", [8, 15, 64], F32)
        E = P.alloc("E", [8, 17, 64], BF16)
        P.dma(TT[0:64, :, :, :], bass.AP(self.rpbp.tensor, self.rpbp[l].offset, [[1, 64], [15 * 127, 8], [127, 15], [1, 64]]), w=[TT])
        P.pool(lambda e: e.memset(E[0:64, :, :, :], 0.0), w=[E])
        P.pool(lambda e: e.memset(ZR[:], 0.0), w=[ZR])
        P.act(lambda e: e.activation(out=E[0:64, :, 1:16, :], in_=TT[0:64, :, :, :], func=AF.Exp), r=[TT], w=[E])
        for h in range(8):
            for ub in range(2):
                ps = P.psum()
                for k in range(8):
                    u = 3 + ub * 8 + k
                    s = 10 - u + 8
                    P.pe(lambda e, h=h, k=k, s=s, ps=ps: e.matmul(ps[:, k * 64:(k + 1) * 64], lhsT=E[0:64, h, s:s + 2, :].rearrange("p a b -> p (a b)"),
                                                                 rhs=self.jb[0:64, :], start=True, stop=True), r=[E], w=[ps], join=(k > 0))
                P.dve(lambda e, h=h, ub=ub, ps=ps: e.tensor_tensor(out=ZR[:, h, 3 + ub * 8:11 + ub * 8, :], in0=ps[:].rearrange("p (u q) -> p u q", q=64),
                                                                   in1=bc(cf[:, K_CV:K_CV + 64].unsqueeze(1), [128, 8, 64]), op=ALU.mult),
                      r=[ps], w=[ZR], join=True)
        if ZRI is not None:
            for h in range(8):
                P.dve(lambda e, h=h: e.tensor_tensor(out=ZRI[:, h, :, :], in0=ZR[:, h, :, :], in1=bc(cf[:, K_RMK:K_RMK + 22].unsqueeze(2), [128, 22, 64]), op=ALU.mult),
                      r=[ZR], w=[ZRI], join=(h > 0))
        P.barrier()
        P.off = mark
        nq = [P.alloc("nq%d" % i, [4, 512], BF16) for i in range(2)]
        nk = [P.alloc("nk%d" % i, [4, 1024], BF16) for i in range(2)]
        nv = [P.alloc("nv%d" % i, [8, 512], BF16) for i in range(2)]
        va = [P.alloc("va%d" % i, [8, 8, 128], BF16) for i in range(2)]
        NB = 6
        eb = [P.alloc("eb%d" % i, [512], BF16) for i in range(NB)]
        p1 = [P.alloc("p1_%d" % i, [512], BF16) for i in range(NB)]
        p2 = [P.alloc("p2_%d" % i, [512], BF16) for i in range(NB)]
        rc = [P.alloc("rc%d" % i, [512], F32) for i in range(2)]
        ona = [P.alloc("ona%d" % i, [4, 512], BF16) for i in range(2)]
        for b in va:
            P.pool(lambda e, b=b: e.memset(b[:], 1.0), w=[b])
        def geom(t):
            R0 = 8 * t
            rs0 = min(max(R0 - 4, 0), rows - 8)
            rs7 = min(max(R0 + 7 - 4, 0), rows - 8)
            kt0, kt1 = rs0 // 2, (rs7 + 7) // 2
            return R0, kt0, kt1, kt1 - kt0 + 1

        def loads(t):
            t0 = t * 512
            R0, kt0, kt1, nkt = geom(t)
            q, k_, v, vaug = nq[t % 2], nk[t % 2], nv[t % 2], va[t % 2]
            P.dma(q[:], self.NQ[:, t0:t0 + 512].rearrange("(c p) t -> p c t", p=128), w=[q])
            P.dma(k_[:, :, 0:nkt * 128], self.NK[:, kt0 * 128:(kt1 + 1) * 128].rearrange("(c p) t -> p c t", p=128), w=[k_])
            P.dma(v[:, 0:nkt, :], self.NVT[kt0 * 128:(kt1 + 1) * 128, :].rearrange("(j p) c -> p j c", p=128), w=[v])
            P.pool(lambda e, v=v, vaug=vaug, nkt=nkt: e.tensor_copy(out=vaug[:, 0:nkt, :, 0:64], in_=v[:, 0:nkt, :].rearrange("p j (h d) -> p j h d", d=64)),
                   r=[v], w=[vaug])

        LA = 4
        loads(0)
        for t in range(NT):
            t0 = t * 512
            R0, kt0, kt1, nkt = geom(t)
            interior = (t > 0 and t < NT - 1)
            q, k_, v, vaug, on = nq[t % 2], nk[t % 2], nv[t % 2], va[t % 2], ona[t % 2]
            if t + 1 < NT:
                loads(t + 1)
            items = [(h, j) for h in range(8) for j in range(nkt)]
            pbuf = {}

            def front(i):
                h, j = items[i]
                hp, o = h // 2, (h % 2) * 64
                u0 = 10 - (2 * (kt0 + j) - R0)
                pS = P.psum(2, 8)
                P.pe(lambda e, pS=pS, k_=k_, q=q, o=o, hp=hp, j=j: e.matmul(pS[:], lhsT=k_[o:o + 64, hp, j * 128:(j + 1) * 128], rhs=q[o:o + 64, hp, :],
                                                                          start=True, stop=True), r=[k_, q], w=[pS])
                e_, p1_, p2_ = eb[i % NB], p1[i % NB], p2[i % NB]
                P.act(lambda e, pS=pS, e_=e_: e.activation(out=e_[:], in_=pS[:], func=AF.Exp, scale=0.125), r=[pS], w=[e_])
                Z = ZRI if interior else ZR
                P.dve(lambda e, e_=e_, p1_=p1_, h=h, u0=u0, Z=Z: e.tensor_tensor(out=p1_[:], in0=e_[:], in1=Z[:, h, u0:u0 + 8, :].rearrange("p a b -> p (a b)"),
                                                                              op=ALU.mult), r=[e_, Z], w=[p1_])
                if interior:
                    pbuf[i] = p1_
                else:
                    P.pool(lambda e, p1_=p1_, p2_=p2_, t=t, j=j: e.tensor_tensor(out=p2_[:].rearrange("p (b q) -> p b q", q=64), in0=p1_[:].rearrange("p (b q) -> p b q", q=64),
                                                                               in1=bc(rvc[:, t, j, :].unsqueeze(2), [128, 8, 64]), op=ALU.mult), r=[p1_, rvc], w=[p2_])
                    pbuf[i] = p2_

            for i in range(min(LA, len(items))):
                front(i)
            psO = None
            for i, (h, j) in enumerate(items):
                hp, o = h // 2, (h % 2) * 64
                if i + LA < len(items):
                    front(i + LA)
                if j == 0:
                    psO = P.psum(0, 2)
                pb = pbuf.pop(i)
                P.pe(lambda e, psO=psO, vaug=vaug, pb=pb, j=j, h=h, nkt=nkt: e.matmul(psO[:], lhsT=vaug[:, j, h, :], rhs=pb[:], start=(j == 0), stop=(j == nkt - 1)),
                     r=[vaug, pb], w=[psO], join=(j > 0))
                if j == nkt - 1:
                    r_ = rc[h % 2]
                    P.dve(lambda e, psO=psO, r_=r_: e.reciprocal(out=r_[0:64, :], in_=psO[64:128, :]), r=[psO], w=[r_])
                    P.dve(lambda e, psO=psO, r_=r_, on=on, o=o, hp=hp: e.tensor_tensor(out=on[o:o + 64, hp, :], in0=psO[0:64, :], in1=r_[0:64, :], op=ALU.mult),
                          r=[psO, r_], w=[on], join=(h > 0))
            P.dma(self.ON[:, t0:t0 + 512].rearrange("(c p) t -> p c t", p=128), on[:], r=[on])

    def ph_merge(self):
        P = self.P
        l, NT = self.l, self.NT
        sm = self.load_small(l)
        xt = [P.alloc("xt%d" % i, [KC, 512], F32) for i in range(2)]
        ob = [[P.alloc("o%d_%d" % (i, k), [4, 512], BF16) for k in range(2)] for i in range(3)]
        gt = [P.alloc("gt%d" % i, [8, 512], BF16) for i in range(2)]
        wb = [P.alloc("wb%d" % i, [4, 1024], BF16) for i in range(3)]
        wo = [P.alloc("wo%d" % i, [KC, 512], BF16) for i in range(2)]
        for i in range(3):
            P.dma(wb[i][:], self.wb_br[l, i].rearrange("(c p) n -> p c n", p=128), w=[wb[i]])
        for hf in range(2):
            P.dma(wo[hf][:], self.wb_out[l][:, hf * 512:(hf + 1) * 512].rearrange("(c p) n -> p c n", p=128), w=[wo[hf]])
        gs = [P.alloc("gs%d" % i, [512], F32) for i in range(2)]
        tmp = [P.alloc("tmp%d" % i, [512], F32) for i in range(2)]
        MG = P.alloc("MG", [8, 512], F32)
        MGb = P.alloc("MGb", [8, 512], BF16)
        xo = [P.alloc("xo%d" % i, [KC, 512], F32) for i in range(2)]
        srcs = [self.OS, self.OR, self.ON]

        def loadx(t):
            t0 = t * 512
            P.dma(xt[t % 2][:], self.X[:, t0:t0 + 512].rearrange("(c p) t -> p c t", p=128), w=[xt[t % 2]])
            for i in range(3):
                P.dma(ob[i][t % 2][:], srcs[i][:, t0:t0 + 512].rearrange("(c p) t -> p c t", p=128), w=[ob[i][t % 2]])

        seq = [(t, j) for t in range(NT) for j in range(5)]
        wq = {}
        cnt = [0, 0]

        def ensure(k):
            if k < len(seq) and k not in wq:
                t, j = seq[k]
                t0 = t * 512
                if j < 3:
                    g, w = gt[cnt[0] % 2], wb[j]
                    cnt[0] += 1
                    P.dma(g[:], self.GT[j * 1024:(j + 1) * 1024, t0:t0 + 512].rearrange("(c p) t -> p c t", p=128), w=[g])
                    wq[k] = (g, w)
                else:
                    wq[k] = (None, wo[j - 3])

        loadx(0)
        ensure(0)
        for k, (t, j) in enumerate(seq):
            t0 = t * 512
            a, xn = xt[t % 2], xo[t % 2]
            if j == 0 and t + 1 < NT:
                loadx(t + 1)
            ensure(k + 1)
            g, w = wq.pop(k)
            if j < 3:
                i = j
                o = ob[i][t % 2]
                for m in range(8):
                    ps = P.psum()
                    for c in range(4):
                        P.pe(lambda e, c=c, m=m, ps=ps, w=w, o=o: e.matmul(ps[:], lhsT=w[:, c, m * 128:(m + 1) * 128], rhs=o[:, c, :], start=(c == 0), stop=(c == 3)),
                             r=[w, o], w=[ps], join=(c > 0))
                    s_ = gs[m % 2]
                    P.act(lambda e, g=g, m=m, i=i, s_=s_: e.activation(out=s_[:], in_=g[:, m, :], func=AF.Sigmoid, bias=sm[:, S_GB + i * 8 + m:S_GB + i * 8 + m + 1]),
                          r=[g, sm], w=[s_])
                    if i == 0:
                        P.dve(lambda e, ps=ps, s_=s_, m=m: e.tensor_tensor(out=MG[:, m, :], in0=ps[:], in1=s_[:], op=ALU.mult), r=[ps, s_], w=[MG], join=(m > 0))
                    else:
                        tm = tmp[m % 2]
                        P.dve(lambda e, ps=ps, s_=s_, tm=tm: e.tensor_tensor(out=tm[:], in0=ps[:], in1=s_[:], op=ALU.mult), r=[ps, s_], w=[tm])
                        if i == 1:
                            P.pool(lambda e, tm=tm, m=m: e.tensor_tensor(out=MG[:, m, :], in0=MG[:, m, :], in1=tm[:], op=ALU.add), r=[tm, MG], w=[MG], join=(m > 0))
                        else:
                            P.pool(lambda e, tm=tm, m=m: e.tensor_tensor(out=MGb[:, m, :], in0=MG[:, m, :], in1=tm[:], op=ALU.add), r=[tm, MG], w=[MGb], join=(m > 0))
            else:
                hf = j - 3
                for mm in range(4):
                    m = hf * 4 + mm
                    ps = P.psum()
                    for c in range(KC):
                        P.pe(lambda e, c=c, mm=mm, ps=ps, w=w: e.matmul(ps[:], lhsT=w[:, c, mm * 128:(mm + 1) * 128], rhs=MGb[:, c, :], start=(c == 0), stop=(c == KC - 1)),
                             r=[w, MGb], w=[ps], join=(c > 0))
                    P.dve(lambda e, ps=ps, m=m, a=a, xn=xn: e.tensor_tensor(out=xn[:, m, :], in0=ps[:], in1=a[:, m, :], op=ALU.add), r=[ps, a], w=[xn], join=(m > 0))
                if hf == 1:
                    P.dma(self.XM[:, t0:t0 + 512].rearrange("(c p) t -> p c t", p=128), xn[:], r=[xn])

    def ph_ffn(self):
        P = self.P
        l, L = self.l, self.L
        sm = self.load_small(l)
        ntile = (L + 509) // 510
        TF = (L + ntile - 1) // ntile
        xt = [P.alloc("xt%d" % i, [KC, 512], F32) for i in range(2)]
        sq = P.alloc("sq", [KC, 512], BF16)
        rs = P.alloc("rs", [512], F32)
        hb = P.alloc("h", [KC, 512], BF16)
        wu = [P.alloc("wu%d" % i, [KC, 2, 512], BF16) for i in range(2)]
        wd = [P.alloc("wd%d" % i, [22, 256], BF16) for i in range(2)]
        av = [P.alloc("av%d" % i, [512], F32) for i in range(2)]
        ag = [P.alloc("ag%d" % i, [512], F32) for i in range(2)]
        sg = [P.alloc("sg%d" % i, [512], F32) for i in range(2)]
        actb = P.alloc("actb", [22, 512], BF16)
        xo = [P.alloc("xo%d" % i, [KC, 512], F32) for i in range(2)]
        fw = sm[:, S_FW:S_FW + 132].rearrange("p (c j) -> p c j", j=3)
        fb = sm[:, S_FB:S_FB + 44]
        def geom(t):
            t0 = t * TF
            n = min(TF, L - t0)
            return t0, n, max(t0 - 1, 0), min(t0 + n + 1, L)

        def loadx(t):
            t0, n, lo, hi = geom(t)
            a = xt[t % 2]
            if lo > t0 - 1:
                P.pool(lambda e, a=a: e.memset(a[:, :, 0:1], 0.0), w=[a])
            if hi < t0 + n + 1:
                P.pool(lambda e, a=a, n=n: e.memset(a[:, :, n + 1:n + 2], 0.0), w=[a])
            P.dma(a[:, :, lo - (t0 - 1):hi - (t0 - 1)], self.XM[:, lo:hi].rearrange("(c p) t -> p c t", p=128), w=[a], join=True)

        steps = [("u", mb_) for mb_ in range(0, 22, 4)] + [("d", mp) for mp in range(4)]
        seq = [(t, j) for t in range(ntile) for j in range(len(steps))]
        wq = {}
        cnt = [0, 0]

        def ensure(k):
            if k < len(seq) and k not in wq:
                kind, idx = steps[seq[k][1]]
                if kind == "u":
                    nm = min(4, 22 - idx)
                    w = wu[cnt[0] % 2]
                    cnt[0] += 1
                    for part in range(2):
                        c0 = part * DFF + idx * 128
                        P.dma(w[:, :, part, 0:nm * 128], self.wb_up[l][:, c0:c0 + nm * 128].rearrange("(c p) n -> p c n", p=128), w=[w], join=(part > 0))
                else:
                    w = wd[cnt[1] % 2]
                    cnt[1] += 1
                    P.dma(w[:], self.wb_dn[l][:, idx * 256:(idx + 1) * 256].rearrange("(c p) n -> p c n", p=128), w=[w])
                wq[k] = w

        loadx(0)
        ensure(0)
        for k, (t, j) in enumerate(seq):
            kind, idx = steps[j]
            t0, n, lo, hi = geom(t)
            a, xn = xt[t % 2], xo[t % 2]
            if j == 0:
                if t + 1 < ntile:
                    loadx(t + 1)
                self.norm(a, n + 2, sm[:, S_GFFN:S_GFFN + 8], hb, sq, rs)
            ensure(k + 1)
            w = wq.pop(k)
            if kind == "u":
                mb_ = idx
                nm = min(4, 22 - mb_)
                for mi in range(nm):
                    m = mb_ + mi
                    res = []
                    for part, accb in ((0, av), (1, ag)):
                        ch = part * 22 + m
                        ps = P.psum()
                        for c in range(KC):
                            P.pe(lambda e, c=c, part=part, mi=mi, ps=ps, w=w, n=n: e.matmul(ps[:, 0:n + 2], lhsT=w[:, c, part, mi * 128:(mi + 1) * 128], rhs=hb[:, c, 0:n + 2],
                                                                                         start=(c == 0), stop=(c == KC - 1)), r=[w, hb], w=[ps], join=(c > 0))
                        ac = accb[m % 2]
                        P.act(lambda e, ps=ps, ac=ac, ch=ch, n=n: e.activation(out=ac[:, 0:n], in_=ps[:, 0:n], func=AF.Identity, bias=fb[:, ch:ch + 1], scale=fw[:, ch, 0:1]),
                              r=[ps, sm], w=[ac])
                        for jj in (1, 2):
                            P.dve(lambda e, ps=ps, ac=ac, ch=ch, jj=jj, n=n: e.scalar_tensor_tensor(out=ac[:, 0:n], in0=ps[:, jj:jj + n], scalar=fw[:, ch, jj:jj + 1], in1=ac[:, 0:n],
                                                                                                 op0=ALU.mult, op1=ALU.add), r=[ps, ac, sm], w=[ac])
                        res.append(ac)
                    s_ = sg[m % 2]
                    P.act(lambda e, s_=s_, g_=res[1], n=n: e.activation(out=s_[:, 0:n], in_=g_[:, 0:n], func=AF.Silu), r=[res[1]], w=[s_])
                    P.pool(lambda e, s_=s_, v_=res[0], m=m, n=n: e.tensor_tensor(out=actb[:, m, 0:n], in0=s_[:, 0:n], in1=v_[:, 0:n], op=ALU.mult),
                           r=[s_, res[0]], w=[actb], join=(m > 0))
            else:
                mp = idx
                for mi in range(2):
                    m = mp * 2 + mi
                    ps = P.psum()
                    for c in range(22):
                        P.pe(lambda e, c=c, mi=mi, ps=ps, w=w, n=n: e.matmul(ps[:, 0:n], lhsT=w[:, c, mi * 128:(mi + 1) * 128], rhs=actb[:, c, 0:n], start=(c == 0), stop=(c == 21)),
                             r=[w, actb], w=[ps], join=(c > 0))
                    P.dve(lambda e, ps=ps, m=m, a=a, xn=xn, n=n: e.tensor_tensor(out=xn[:, m, 0:n], in0=ps[:, 0:n], in1=a[:, m, 1:n + 1], op=ALU.add), r=[ps, a], w=[xn], join=(m > 0))
                if mp == 3:
                    P.dma(self.X[:, t0:t0 + n].rearrange("(c p) t -> p c t", p=128), xn[:, :, 0:n], r=[xn])


def make_consts(Lmax):
    c = np.zeros((128, NCON), np.float32)
    c[:, K_ID:K_ID + 128] = np.eye(128, dtype=np.float32)
    j = np.arange(128)[:, None]
    i = np.arange(128)[None, :]
    mnf = np.where(i >= j, 0.0, -30000.0).astype(np.float32)
    mnb = np.where(j > i, 0.0, -30000.0).astype(np.float32)
    c[:, K_MNF:K_MNF + 512] = np.tile(mnf, (1, 4))
    c[:, K_MNB:K_MNB + 512] = np.tile(mnb, (1, 4))
    rm = np.ones(512, np.float32)
    rm[::128] = 0.0
    c[:, K_RM:K_RM + 512] = rm[None, :]
    sel = np.zeros((128, 16, 128), np.float32)
    for q in range(16):
        sel[q, q, :] = 1.0 if q < 8 else -1.0
    c[:, K_SEL:K_SEL + 2048] = sel.reshape(128, 2048)
    c[0:8, K_MF] = 1.0
    c[8:16, K_MF + 1] = 1.0
    c[8:16, K_MF + 2] = -1.0
    prot = np.zeros((128, 128), np.float32)
    for m in range(128):
        if m % 64 < 32:
            prot[m + 32, m] = -1.0
        else:
            prot[m - 32, m] = 1.0
    c[:, K_PROT:K_PROT + 128] = prot
    c[:, K_EJ] = 127 - np.arange(128)
    c[:, K_EJ + 1] = np.arange(128)
    c[:, K_EI:K_EI + 128] = (np.arange(128) + 1)[None, :]
    c[:, K_EI + 128:K_EI + 256] = (128 - np.arange(128))[None, :]
    c[:, K_M1:K_M1 + 128] = np.maximum(i - j, 0)
    c[:, K_M2:K_M2 + 128] = np.maximum(j - i, 0)
    c[0:64, K_J:K_J + 64] = np.eye(64, dtype=np.float32)[::-1]
    qc = np.arange(64)[None, :]
    kc = np.arange(64)[:, None]
    cst = np.clip(qc - 8, 0, 48)
    cv = ((kc >= cst) & (kc < cst + 16)).astype(np.float32)
    c[:, K_CV:K_CV + 64] = np.tile(cv, (2, 1))
    for a_ in range(2):
        for u in range(22):
            dr = a_ + 10 - u
            c[a_ * 64:(a_ + 1) * 64, K_RMK + u] = 1.0 if -4 <= dr <= 3 else 0.0
    half = 32
    inv = (1.0 / (10000.0 ** (np.arange(half, dtype=np.float32) / half))).astype(np.float32)
    pos = np.arange(Lmax, dtype=np.float32)
    ang = pos[None, :] * inv[:, None]
    f = (np.arange(128) % 64) % 32
    rot = np.stack([np.cos(ang)[f], np.sin(ang)[f]]).astype(np.float32)
    return c, rot


def make_rvc(L):
    rows = L // 64
    NT = L // 512
    r = np.zeros((128, NT, 8, 8), np.float32)
    for t in range(NT):
        R0 = 8 * t
        rs0 = min(max(R0 - 4, 0), rows - 8)
        kt0 = rs0 // 2
        for j in range(8):
            for a in range(2):
                kr = 2 * (kt0 + j) + a
                for b in range(8):
                    qr = R0 + b
                    rs = min(max(qr - 4, 0), rows - 8)
                    if rs <= kr < rs + 8:
                        r[a * 64:(a + 1) * 64, t, j, b] = 1.0
    return r.reshape(128, NT * 64)


def make_small(inp, depth):
    sm = np.zeros((depth + 1, 128, NSM), np.float32)

    def pc(v):
        return np.asarray(v, np.float32).reshape(-1, 128).T

    for l in range(depth):
        sm[l, :, S_GMIX:S_GMIX + 8] = pc(inp["norm_mix"][l])
        sm[l, :, S_GFFN:S_GFFN + 8] = pc(inp["norm_ffn"][l])
        sm[l, :, S_GB:S_GB + 24] = pc(inp["gate_bias"][l])
        cw = np.asarray(inp["ssd_conv_w"][l], np.float32)
        sm[l, :, S_CW:S_CW + 40] = cw.reshape(5, 8, 128).transpose(2, 1, 0).reshape(128, 40)
        sm[l, :, S_CB:S_CB + 8] = pc(inp["ssd_conv_b"][l])
        fw = np.asarray(inp["ffn_conv_w"][l], np.float32)
        sm[l, :, S_FW:S_FW + 132] = fw.reshape(3, 44, 128).transpose(2, 1, 0).reshape(128, 132)
        sm[l, :, S_FB:S_FB + 44] = pc(inp["ffn_conv_b"][l])
        sm[l, :, S_DSK:S_DSK + 8] = np.asarray(inp["ssd_d"][l], np.float32)[None, :]
        sm[l, :, S_NW:S_NW + 512] = np.asarray(inp["ssd_norm"][l], np.float32)[None, :]
        sm[l, :, S_TH:S_TH + 8] = np.asarray(inp["ret_theta"][l], np.float32).reshape(8)[None, :]
        sm[l, 0:16, S_DTB] = np.asarray(inp["ssd_dt_bias"][l], np.float32).reshape(16)
        sm[l, 0:16, S_ALOG] = np.asarray(inp["ssd_a_log"][l], np.float32).reshape(16)
    sm[depth, :, S_GMIX:S_GMIX + 8] = pc(inp["norm_final"])
    return sm


def host_inputs(inp, depth, seq_lens):
    Lmax = max(seq_lens)
    c, rot = make_consts(Lmax)
    rp = np.zeros((depth, 8, 15, 127), np.float32)
    rp[:, :, :, 48:79] = np.asarray(inp["na_rpb"], np.float32)[:depth]
    common = {
        "w_in": np.ascontiguousarray(np.asarray(inp["w_in"], np.float32)[:depth]),
        "w_branch": np.ascontiguousarray(np.asarray(inp["w_branch"], np.float32)[:depth]),
        "w_out": np.ascontiguousarray(np.asarray(inp["w_out"], np.float32)[:depth]),
        "ffn_w_up": np.ascontiguousarray(np.asarray(inp["ffn_w_up"], np.float32)[:depth]),
        "ffn_w_down": np.ascontiguousarray(np.asarray(inp["ffn_w_down"], np.float32)[:depth]),
        "small": make_small(inp, depth),
        "consts": c,
        "rot": rot,
        "rpbp": rp,
    }
    for L in sorted(set(seq_lens)):
        common["rvc%d" % L] = make_rvc(L)
    return common


_CACHE = {}


def kernel(**inputs):
    xp = np.asarray(inputs["x_prompt"], np.float32)
    xs = np.asarray(inputs["x_sample"], np.float32)
    depth = np.asarray(inputs["w_in"]).shape[0]
    n = 8
    seq_lens = [xp.shape[1], xp.shape[1], xs.shape[1]]
    key = (tuple(seq_lens), depth)
    if key not in _CACHE:
        _CACHE[key] = Builder(seq_lens, depth).build()
    nc = _CACHE[key]
    common = host_inputs(inputs, depth, seq_lens)
    in_maps = []
    for c in range(n):
        m = dict(common)
        m["x0"] = np.ascontiguousarray(xp[2 * c])
        m["x1"] = np.ascontiguousarray(xp[2 * c + 1])
        m["x2"] = np.ascontiguousarray(xs[c])
        in_maps.append(m)
    res = run_bass_kernel_spmd(nc, in_maps, core_ids=list(range(n)))
    yp = np.empty_like(xp)
    ys = np.empty_like(xs)
    for c in range(n):
        r = res.results[c]
        yp[2 * c] = r["y0"]
        yp[2 * c + 1] = r["y1"]
        ys[c] = r["y2"]
    return (yp, ys)
```
